# Optimizing a Trainium2 kernel written in Bass

```python
import math
import jax, jax.numpy as jnp
from jax import lax
import numpy as np

D_MODEL = 1024
BATCH = 4
SEQ = 4096
DEPTH = 2
DEC_BATCH = 128
DEC_SEQ = 4
PAST_LEN = 8192
PAGE_SIZE = 128

N_MIXERS = 2
N_ATTN_LAYERS = (DEPTH + 1) // 2
N_SSM_LAYERS = DEPTH // 2
RMS_EPS = 1e-6
N_HEADS = 16
N_KV_HEADS = 4
HEAD_DIM = 64
KV_REP = N_HEADS // N_KV_HEADS
WINDOW = 128
ATTN_BLOCK = WINDOW
NUM_BUCKETS = 32
MAX_DISTANCE = 128
SSM_EXPAND = 2
D_INNER = SSM_EXPAND * D_MODEL
SSM_HEAD_DIM = 64
SSM_HEADS = D_INNER // SSM_HEAD_DIM
SSM_GROUPS = 8
SSM_REP = SSM_HEADS // SSM_GROUPS
D_STATE = 128
SSM_CONV = 4
CONV_DIM = D_INNER + 2 * SSM_GROUPS * D_STATE
SSM_IN_DIM = D_INNER + CONV_DIM + SSM_HEADS
SSD_CHUNK = 128
D_FF = 2816
FFN_CONV = 3

kernel_name = 'hybrid_swa_sink_ssd_convffn_step'


def rmsnorm(x, g):
    xf = x.astype(jnp.float32)
    y = xf * lax.rsqrt(jnp.mean(xf * xf, axis=-1, keepdims=True) + RMS_EPS)
    return (y * g.astype(jnp.float32)).astype(x.dtype)


def causal_dwconv(x, buf, w, b):
    K = w.shape[0]
    L = x.shape[1]
    xp = jnp.concatenate([buf.astype(x.dtype), x], axis=1)
    y = b.astype(x.dtype)
    for k in range(K):
        y = y + xp[:, k:k + L] * w[k]
    return y, xp[:, L:]


def t5_bucket(dist):
    n = jnp.maximum(dist, 0)
    max_exact = NUM_BUCKETS // 2
    nf = jnp.maximum(n, 1).astype(jnp.float32)
    large = max_exact + (jnp.log(nf / max_exact) / math.log(MAX_DISTANCE / max_exact)
                         * (NUM_BUCKETS - max_exact)).astype(jnp.int32)
    large = jnp.minimum(large, NUM_BUCKETS - 1)
    return jnp.where(n < max_exact, n, large)


def rel_bias(dist, table):
    return jnp.transpose(table[t5_bucket(dist)], (2, 0, 1)).astype(jnp.float32)


def sink_attention(q, k, v, bias, mask, sinks):
    s = jnp.einsum('bnqgrd,bnkgd->bngrqk', q, k).astype(jnp.float32) * (HEAD_DIM ** -0.5)
    s = s + bias.reshape(N_KV_HEADS, KV_REP, *bias.shape[1:])
    s = jnp.where(mask[None, :, None, None], s, -jnp.inf)
    sink = sinks.astype(jnp.float32).reshape(N_KV_HEADS, KV_REP)[:, :, None, None]
    m = jnp.maximum(jnp.max(s, axis=-1, keepdims=True), sink)
    p = jnp.exp(s - m)
    p = p / (jnp.sum(p, axis=-1, keepdims=True) + jnp.exp(sink - m))
    return jnp.einsum('bngrqk,bnkgd->bnqgrd', p.astype(v.dtype), v)


def qkv_proj(h, wqkv):
    b, L, _ = h.shape
    qkv = h @ wqkv
    nq = N_HEADS * HEAD_DIM
    nk = N_KV_HEADS * HEAD_DIM
    q = qkv[..., :nq].reshape(b, L, N_KV_HEADS, KV_REP, HEAD_DIM)
    k = qkv[..., nq:nq + nk].reshape(b, L, N_KV_HEADS, HEAD_DIM)
    v = qkv[..., nq + nk:].reshape(b, L, N_KV_HEADS, HEAD_DIM)
    return q, k, v


def attn_prompt(h, wqkv, wo, sinks, table):
    b, L, _ = h.shape
    q, k, v = qkv_proj(h, wqkv)
    nb = L // ATTN_BLOCK
    pad = jnp.zeros((b, ATTN_BLOCK, N_KV_HEADS, HEAD_DIM), k.dtype)
    kp = jnp.concatenate([pad, k], axis=1)
    vp = jnp.concatenate([pad, v], axis=1)

    def band(t):
        tb = t.reshape(b, nb + 1, ATTN_BLOCK, N_KV_HEADS, HEAD_DIM)
        return jnp.concatenate([tb[:, :-1], tb[:, 1:]], axis=2)

    kj = jnp.arange(2 * ATTN_BLOCK)[None, :]
    dist = (jnp.arange(ATTN_BLOCK)[:, None] + ATTN_BLOCK) - kj
    kpos = jnp.arange(nb)[:, None, None] * ATTN_BLOCK - ATTN_BLOCK + kj[None]
    valid = (dist >= 0) & (dist < WINDOW) & (kpos >= 0)
    qb = q.reshape(b, nb, ATTN_BLOCK, N_KV_HEADS, KV_REP, HEAD_DIM)
    o = sink_attention(qb, band(kp), band(vp), rel_bias(dist, table), valid, sinks)
    y = o.reshape(b, L, N_HEADS * HEAD_DIM) @ wo
    return y, kp[:, -WINDOW:], vp[:, -WINDOW:]


def attn_sample(h, k_buf, v_buf, wqkv, wo, sinks, table):
    b, S, _ = h.shape
    wb = k_buf.shape[1]
    q, k, v = qkv_proj(h, wqkv)
    k_all = jnp.concatenate([k_buf.astype(k.dtype), k], axis=1)
    v_all = jnp.concatenate([v_buf.astype(v.dtype), v], axis=1)
    dist = (wb + jnp.arange(S))[:, None] - jnp.arange(wb + S)[None, :]
    valid = (dist >= 0) & (dist < WINDOW)
    o = sink_attention(q[:, None], k_all[:, None], v_all[:, None], rel_bias(dist, table), valid[None], sinks)
    y = o.reshape(b, S, N_HEADS * HEAD_DIM) @ wo
    return y, k_all[:, -wb:], v_all[:, -wb:]


def ssd_scan(x, dt, A, Bm, Cm, h0):
    b, L = x.shape[:2]
    T = SSD_CHUNK if L % SSD_CHUNK == 0 else L
    nc = L // T
    f32 = jnp.float32
    xc = x.astype(f32).reshape(b, nc, T, SSM_GROUPS, SSM_REP, SSM_HEAD_DIM)
    dtc = dt.reshape(b, nc, T, SSM_GROUPS, SSM_REP)
    Bc = Bm.astype(f32).reshape(b, nc, T, SSM_GROUPS, D_STATE)
    Cc = Cm.astype(f32).reshape(b, nc, T, SSM_GROUPS, D_STATE)
    cs = jnp.cumsum(dtc * A, axis=2)
    causal = (jnp.arange(T)[:, None] >= jnp.arange(T)[None, :])[:, :, None, None]
    seg = cs[:, :, :, None] - cs[:, :, None]
    lmat = jnp.exp(jnp.where(causal, seg, -jnp.inf))
    cb = jnp.einsum('bclgn,bcsgn->bclsg', Cc, Bc)
    w = cb[..., None] * lmat * dtc[:, :, None]
    y_diag = jnp.einsum('bclsgr,bcsgrp->bclgrp', w, xc)
    decay = jnp.exp(cs[:, :, -1:] - cs)
    states = jnp.einsum('bcsgn,bcsgr,bcsgrp->bcgrpn', Bc, decay * dtc, xc)
    chunk_decay = jnp.exp(cs[:, :, -1])

    def step(hc, inp):
        st, dec = inp
        return hc * dec[..., None, None] + st, hc

    h_final, h_prev = lax.scan(step, h0.astype(f32),
                               (jnp.moveaxis(states, 1, 0), jnp.moveaxis(chunk_decay, 1, 0)))
    h_prev = jnp.moveaxis(h_prev, 0, 1)
    y_off = jnp.einsum('bclgn,bcgrpn,bclgr->bclgrp', Cc, h_prev, jnp.exp(cs))
    y = (y_diag + y_off).reshape(b, L, SSM_GROUPS, SSM_REP, SSM_HEAD_DIM)
    return y, h_final


def ssd_mixer(h, conv_buf, ssm_state, w_in, conv_w, conv_b, dt_bias, A_log, D_skip, norm_w, w_out):
    b, L, _ = h.shape
    proj = h @ w_in
    z = proj[..., :D_INNER]
    xbc = proj[..., D_INNER:D_INNER + CONV_DIM]
    dt_raw = proj[..., D_INNER + CONV_DIM:]
    xbc_c, new_conv = causal_dwconv(xbc, conv_buf, conv_w, conv_b)
    xbc_c = jax.nn.silu(xbc_c)
    gn = SSM_GROUPS * D_STATE
    xs = xbc_c[..., :D_INNER].reshape(b, L, SSM_GROUPS, SSM_REP, SSM_HEAD_DIM)
    Bm = xbc_c[..., D_INNER:D_INNER + gn].reshape(b, L, SSM_GROUPS, D_STATE)
    Cm = xbc_c[..., D_INNER + gn:].reshape(b, L, SSM_GROUPS, D_STATE)
    dt = jax.nn.softplus(dt_raw.astype(jnp.float32) + dt_bias.astype(jnp.float32))
    dt = dt.reshape(b, L, SSM_GROUPS, SSM_REP)
    A = -jnp.exp(A_log.astype(jnp.float32)).reshape(SSM_GROUPS, SSM_REP)
    h0 = ssm_state.reshape(b, SSM_GROUPS, SSM_REP, SSM_HEAD_DIM, D_STATE)
    y, h_new = ssd_scan(xs, dt, A, Bm, Cm, h0)
    y = y + D_skip.astype(jnp.float32).reshape(SSM_GROUPS, SSM_REP)[:, :, None] * xs.astype(jnp.float32)
    y = y.reshape(b, L, D_INNER) * jax.nn.silu(z.astype(jnp.float32))
    yg = y.reshape(b, L, SSM_GROUPS, D_INNER // SSM_GROUPS)
    yg = yg * lax.rsqrt(jnp.mean(yg * yg, axis=-1, keepdims=True) + RMS_EPS)
    y = (yg.reshape(b, L, D_INNER) * norm_w.astype(jnp.float32)).astype(h.dtype)
    out = y @ w_out
    return out, new_conv, h_new.reshape(b, SSM_HEADS, SSM_HEAD_DIM, D_STATE).astype(ssm_state.dtype)


def conv_ffn(h, buf, w_up, conv_w, conv_b, w_down):
    u = h @ w_up
    u, new_buf = causal_dwconv(u, buf, conv_w, conv_b)
    return (jax.nn.silu(u[..., :D_FF]) * u[..., D_FF:]) @ w_down, new_buf


def run_trunk(x, cache_k, cache_v, st_conv, st_ssm, st_ffn,
              rel_bias_table, norm_mix, norm_ffn, norm_final,
              attn_wqkv, attn_wo, attn_sinks,
              ssm_w_in, ssm_conv_w, ssm_conv_b, ssm_dt_bias, ssm_A_log, ssm_D, ssm_norm, ssm_w_out,
              ffn_w_up, ffn_conv_w, ffn_conv_b, ffn_w_down):
    prompt = cache_k is None
    b = x.shape[0]
    new_k, new_v, new_conv, new_ssm, new_ffn = [], [], [], [], []
    for i in range(DEPTH):
        h = rmsnorm(x, norm_mix[i])
        if i % N_MIXERS == 0:
            a = i // N_MIXERS
            if prompt:
                o, kb, vb = attn_prompt(h, attn_wqkv[a], attn_wo[a], attn_sinks[a], rel_bias_table)
            else:
                o, kb, vb = attn_sample(h, cache_k[a], cache_v[a], attn_wqkv[a], attn_wo[a],
                                        attn_sinks[a], rel_bias_table)
            new_k.append(kb)
            new_v.append(vb)
        else:
            s = i // N_MIXERS
            cbuf = jnp.zeros((b, SSM_CONV - 1, CONV_DIM), x.dtype) if prompt else st_conv[s]
            hst = jnp.zeros((b, SSM_HEADS, SSM_HEAD_DIM, D_STATE), jnp.float32) if prompt else st_ssm[s]
            o, cb, hs = ssd_mixer(h, cbuf, hst, ssm_w_in[s], ssm_conv_w[s], ssm_conv_b[s], ssm_dt_bias[s],
                                  ssm_A_log[s], ssm_D[s], ssm_norm[s], ssm_w_out[s])
            new_conv.append(cb)
            new_ssm.append(hs)
        x = x + o
        h = rmsnorm(x, norm_ffn[i])
        fbuf = jnp.zeros((b, FFN_CONV - 1, 2 * D_FF), x.dtype) if prompt else st_ffn[i]
        o, fb = conv_ffn(h, fbuf, ffn_w_up[i], ffn_conv_w[i], ffn_conv_b[i], ffn_w_down[i])
        new_ffn.append(fb)
        x = x + o
    y = rmsnorm(x, norm_final)
    return (y, jnp.stack(new_k), jnp.stack(new_v), jnp.stack(new_conv), jnp.stack(new_ssm), jnp.stack(new_ffn))


def setup_inputs(seed: int = 0) -> dict:
    key = jax.random.key(seed)
    ks = jax.random.split(key, 32)
    f32 = jnp.float32

    def nrm(k, shape, scale):
        return scale * jax.random.normal(k, shape, f32)

    win_buf = min(WINDOW, PAST_LEN)
    qkv_dim = (N_HEADS + 2 * N_KV_HEADS) * HEAD_DIM
    dt0 = jnp.exp(jax.random.uniform(ks[17], (N_SSM_LAYERS, SSM_HEADS), f32, math.log(1e-3), math.log(1e-1)))
    return {
        'x_prompt': nrm(ks[0], (BATCH, SEQ, D_MODEL), 1.0),
        'x_sample': nrm(ks[1], (DEC_BATCH, DEC_SEQ, D_MODEL), 1.0),
        'cache_k_win': nrm(ks[2], (N_ATTN_LAYERS, DEC_BATCH, win_buf, N_KV_HEADS, HEAD_DIM), 1.0),
        'cache_v_win': nrm(ks[3], (N_ATTN_LAYERS, DEC_BATCH, win_buf, N_KV_HEADS, HEAD_DIM), 1.0),
        'state_ssm_conv': nrm(ks[4], (N_SSM_LAYERS, DEC_BATCH, SSM_CONV - 1, CONV_DIM), 1.0),
        'state_ssm': nrm(ks[5], (N_SSM_LAYERS, DEC_BATCH, SSM_HEADS, SSM_HEAD_DIM, D_STATE), 0.5),
        'state_ffn_conv': nrm(ks[6], (DEPTH, DEC_BATCH, FFN_CONV - 1, 2 * D_FF), 1.0),
        'rel_bias_table': nrm(ks[7], (NUM_BUCKETS, N_HEADS), 0.5),
        'norm_mix': 1.0 + nrm(ks[8], (DEPTH, D_MODEL), 0.02),
        'norm_ffn': 1.0 + nrm(ks[9], (DEPTH, D_MODEL), 0.02),
        'norm_final': 1.0 + nrm(ks[10], (D_MODEL,), 0.02),
        'attn_wqkv': nrm(ks[11], (N_ATTN_LAYERS, D_MODEL, qkv_dim), D_MODEL ** -0.5),
        'attn_wo': nrm(ks[12], (N_ATTN_LAYERS, N_HEADS * HEAD_DIM, D_MODEL), (N_HEADS * HEAD_DIM) ** -0.5),
        'attn_sinks': nrm(ks[13], (N_ATTN_LAYERS, N_HEADS), 0.5),
        'ssm_w_in': nrm(ks[14], (N_SSM_LAYERS, D_MODEL, SSM_IN_DIM), D_MODEL ** -0.5),
        'ssm_conv_w': nrm(ks[15], (N_SSM_LAYERS, SSM_CONV, CONV_DIM), SSM_CONV ** -0.5),
        'ssm_conv_b': nrm(ks[16], (N_SSM_LAYERS, CONV_DIM), 0.01),
        'ssm_dt_bias': dt0 + jnp.log(-jnp.expm1(-dt0)),
        'ssm_A_log': jnp.log(jax.random.uniform(ks[18], (N_SSM_LAYERS, SSM_HEADS), f32, 1.0, 16.0)),
        'ssm_D': 1.0 + nrm(ks[19], (N_SSM_LAYERS, SSM_HEADS), 0.1),
        'ssm_norm': 1.0 + nrm(ks[20], (N_SSM_LAYERS, D_INNER), 0.02),
        'ssm_w_out': nrm(ks[21], (N_SSM_LAYERS, D_INNER, D_MODEL), D_INNER ** -0.5),
        'ffn_w_up': nrm(ks[22], (DEPTH, D_MODEL, 2 * D_FF), D_MODEL ** -0.5),
        'ffn_conv_w': nrm(ks[23], (DEPTH, FFN_CONV, 2 * D_FF), FFN_CONV ** -0.5),
        'ffn_conv_b': nrm(ks[24], (DEPTH, 2 * D_FF), 0.01),
        'ffn_w_down': nrm(ks[25], (DEPTH, D_FF, D_MODEL), D_FF ** -0.5),
    }


def reference(x_prompt, x_sample, cache_k_win, cache_v_win, state_ssm_conv, state_ssm, state_ffn_conv,
              rel_bias_table, norm_mix, norm_ffn, norm_final,
              attn_wqkv, attn_wo, attn_sinks,
              ssm_w_in, ssm_conv_w, ssm_conv_b, ssm_dt_bias, ssm_A_log, ssm_D, ssm_norm, ssm_w_out,
              ffn_w_up, ffn_conv_w, ffn_conv_b, ffn_w_down):
    y_prompt, p_k, p_v, p_conv, p_ssm, p_ffn = run_trunk(
        x_prompt, None, None, None, None, None,
        rel_bias_table, norm_mix, norm_ffn, norm_final, attn_wqkv, attn_wo, attn_sinks,
        ssm_w_in, ssm_conv_w, ssm_conv_b, ssm_dt_bias, ssm_A_log, ssm_D, ssm_norm, ssm_w_out,
        ffn_w_up, ffn_conv_w, ffn_conv_b, ffn_w_down)
    y_sample, s_k, s_v, s_conv, s_ssm, s_ffn = run_trunk(
        x_sample, cache_k_win, cache_v_win, state_ssm_conv, state_ssm, state_ffn_conv,
        rel_bias_table, norm_mix, norm_ffn, norm_final, attn_wqkv, attn_wo, attn_sinks,
        ssm_w_in, ssm_conv_w, ssm_conv_b, ssm_dt_bias, ssm_A_log, ssm_D, ssm_norm, ssm_w_out,
        ffn_w_up, ffn_conv_w, ffn_conv_b, ffn_w_down)
    return (y_prompt, y_sample, p_k, p_v, p_conv, p_ssm, p_ffn, s_k, s_v, s_conv, s_ssm, s_ffn)
```

```python
import contextlib
import math
import numpy as np
import concourse.bass as bass
import concourse.mybir as mybir
from concourse.bass_utils import run_bass_kernel_spmd

F32 = mybir.dt.float32
BF16 = mybir.dt.bfloat16
AF = mybir.ActivationFunctionType
ALU = mybir.AluOpType

D = 1024
KT = 8
SEQ = 4096
DFF = 2816
NUP = 5632
DIN = 2048
CONVD = 4096
SIN = 6176
NH = 16
NKV = 4
HD = 64
SH = 32
SGR = 8
NST = 128
EPS = 1e-6
NEG = -30000.0
NCORES = 8
SB_PER_CORE = 16
TS = 64

ENGS = ("pe", "act", "dve", "pool", "sp")
EPOCH = 30000
GRAN = 64


class Prog:
    def __init__(self, nc, n_dma_sems=60, same_engine_sync=True):
        self.nc = nc
        self.ops = {e: [] for e in ENGS}
        self.count = {e: 0 for e in ENGS}
        self.seen = {e: {} for e in ENGS}
        self.lw = {}
        self.rd = {}
        self.n_dma_sems = n_dma_sems
        self.dma_rr = [0, 0]
        self.dma_val = [0] * n_dma_sems
        self.same_engine_sync = same_engine_sync
        self.max_epoch = {e: 0 for e in ENGS}
        self.pend_r = {e: [] for e in ENGS}
        self.pend_w = {e: [] for e in ENGS}

    @staticmethod
    def _cells(keys):
        out = []
        for k in keys:
            if isinstance(k, str):
                out.append(k)
            else:
                lo, hi = k
                out.extend(range(lo // GRAN, (hi - 1) // GRAN + 1))
        return out

    def _need(self, eng, dep):
        semkey, val, peng = dep
        if peng == eng and semkey[0] == "e" and (eng == "pe" or not self.same_engine_sync):
            return
        cur = self.seen[eng].get(semkey, 0)
        if cur >= val:
            return
        self.seen[eng][semkey] = val
        self.ops[eng].append(("wait", (semkey, val)))

    def _deps(self, eng, rc, wc):
        lw, rd = self.lw, self.rd
        for c in rc:
            d = lw.get(c)
            if d is not None:
                self._need(eng, d)
        for c in wc:
            d = lw.get(c)
            if d is not None:
                self._need(eng, d)
            r = rd.get(c)
            if r:
                for semkey, (val, peng) in r.items():
                    self._need(eng, (semkey, val, peng))

    def _commit(self, tok, rc, wc):
        semkey, val, eng = tok
        for c in rc:
            self.rd.setdefault(c, {})[semkey] = (val, eng)
        for c in wc:
            self.lw[c] = tok
            self.rd[c] = {}

    def op(self, eng, fn, reads=(), writes=()):
        rc, wc = self._cells(reads), self._cells(writes)
        self._deps(eng, rc, wc)
        idx = self.count[eng]
        self.count[eng] += 1
        ep = idx // EPOCH
        self.max_epoch[eng] = max(self.max_epoch[eng], ep)
        semkey = ("e", eng, ep)
        self.ops[eng].append(("op", (fn, semkey)))
        self._commit((semkey, idx % EPOCH + 1, eng), rc, wc)

    def raw(self, eng, fn, reads=(), writes=()):
        rc, wc = self._cells(reads), self._cells(writes)
        self._deps(eng, rc, wc)
        idx = self.count[eng]
        ep = idx // EPOCH
        self.max_epoch[eng] = max(self.max_epoch[eng], ep)
        self._commit((("e", eng, ep), idx % EPOCH + 1, eng), rc, wc)
        self.ops[eng].append(("raw", fn))

    def dma(self, eng, fns, reads=(), writes=(), inc=16):
        rc, wc = self._cells(reads), self._cells(writes)
        self._deps(eng, rc, wc)
        nhw = 24
        if eng == "pool":
            s = nhw + self.dma_rr[1] % (self.n_dma_sems - nhw)
            self.dma_rr[1] += 1
        else:
            s = self.dma_rr[0] % nhw
            self.dma_rr[0] += 1
        semkey = ("d", s)
        prev = self.dma_val[s]
        if prev > 0:
            self._need(eng, (semkey, prev, "dma"))
        newv = prev + inc * len(fns)
        self.dma_val[s] = newv
        self.ops[eng].append(("dma", (fns, semkey, inc)))
        self._commit((semkey, newv, "dma"), rc, wc)

    def finish(self, eng="sp"):
        for s in range(self.n_dma_sems):
            if self.dma_val[s] > 0:
                self._need(eng, (("d", s), self.dma_val[s], "dma"))
        for e in ENGS:
            if self.count[e] > 0:
                idx = self.count[e] - 1
                self._need(eng, (("e", e, idx // EPOCH), idx % EPOCH + 1, e))

    def emit(self, ctx):
        nc = self.nc
        sems = {}
        for e in ENGS:
            if self.count[e] > 0:
                for ep in range(self.max_epoch[e] + 1):
                    sems[("e", e, ep)] = ctx.enter_context(nc.semaphore(f"s_{e}_{ep}"))
        for s in range(self.n_dma_sems):
            if self.dma_val[s] > 0:
                sems[("d", s)] = ctx.enter_context(nc.semaphore(f"s_dma_{s}"))
        block = ctx.enter_context(nc.Block())
        ops = self.ops

        def run(e, lst):
            for kind, pl in lst:
                if kind == "wait":
                    e.wait_ge(sems[pl[0]], pl[1])
                elif kind == "op":
                    pl[0](e).then_inc(sems[pl[1]], 1)
                elif kind == "raw":
                    pl(e)
                else:
                    fns, semkey, inc = pl
                    for fn in fns:
                        fn(e).then_inc(sems[semkey], inc)

        @block.tensor
        def _(e):
            run(e, ops["pe"])

        @block.scalar
        def _(e):
            run(e, ops["act"])

        @block.vector
        def _(e):
            run(e, ops["dve"])

        @block.gpsimd
        def _(e):
            run(e, ops["pool"])

        @block.sync
        def _(e):
            run(e, ops["sp"])


class SB:
    def __init__(self, nc, name, shape, dtype, off):
        self.t = nc.alloc_sbuf_tensor_at(name, list(shape), dtype, offset=off)
        self.off = off
        self.shape = list(shape)
        self.esz = 4 if dtype == F32 else 2
        self.nbytes = int(np.prod(shape[1:])) * self.esz

    def __getitem__(self, k):
        return self.t[k]

    def k(self, *idx):
        lo, span = 0, int(np.prod(self.shape[1:]))
        dims = self.shape[1:]
        for d, i in zip(dims, idx):
            span //= d
            if isinstance(i, tuple):
                lo += i[0] * span
                n = i[1] - i[0]
                return (self.off + lo * self.esz, self.off + (lo + n * span) * self.esz)
            lo += i * span
        return (self.off + lo * self.esz, self.off + (lo + span) * self.esz)


class Arena:
    def __init__(self, nc, base, limit):
        self.nc, self.base, self.limit, self.cur, self.n = nc, base, limit, base, 0
        self.hi = base

    def alloc(self, name, shape, dtype):
        esz = 4 if dtype == F32 else 2
        nbytes = int(np.prod(shape[1:])) * esz
        off = (self.cur + 127) // 128 * 128
        assert off + nbytes <= self.limit, f"SBUF overflow allocating {name}: {off}+{nbytes} > {self.limit}"
        self.cur = off + nbytes
        self.hi = max(self.hi, self.cur)
        self.n += 1
        return SB(self.nc, f"{name}_{self.n}", shape, dtype, off)

    def mark(self):
        return self.cur

    def reset(self, m):
        self.cur = m


def _t5_bucket_np(n):
    n = np.maximum(n, 0)
    nf = np.maximum(n, 1).astype(np.float32)
    v = (np.log(nf / np.float32(16)) / np.float32(math.log(128 / 16)) * np.float32(16)).astype(np.float32)
    large = np.minimum(16 + v.astype(np.int32), 31)
    return np.where(n < 16, n, large)


def host_consts():
    c = {}
    i = np.arange(128)
    c["ident"] = np.eye(128, dtype=np.float32)
    c["tri"] = (i[:, None] <= i[None, :]).astype(np.float32)
    c["ugt"] = (i[:, None] > i[None, :]).astype(np.float32)
    c["ones"] = np.ones((128, 128), np.float32)
    c["maskp"] = np.where(i[:, None] > i[None, :], 0.0, NEG).astype(np.float32)
    c["masko"] = np.where(i[:, None] <= i[None, :], 0.0, NEG).astype(np.float32)
    d = np.arange(256)
    bk = _t5_bucket_np(d)
    G = np.zeros((32, 384), np.float32)
    G[bk, 255 - d] = 1.0
    c["grev"] = G
    j = np.arange(64)
    tj, bj = j // 16, j % 16
    same = bj[:, None] == bj[None, :]
    c["s_tri"] = (same & (tj[:, None] <= tj[None, :])).astype(np.float32)
    c["s_ugt"] = (same & (tj[:, None] > tj[None, :])).astype(np.float32)
    c["s_same"] = same.astype(np.float32)
    c["s_seqsel"] = (bj[:, None] == np.arange(16)[None, :]).astype(np.float32)
    c["s_samem"] = np.where(bj[:, None] == np.arange(16)[None, :], 0.0, NEG).astype(np.float32)
    mrow = np.where(np.arange(16)[:, None] == bj[None, :], 0.0, NEG).astype(np.float32)
    c["s_mrow"] = np.tile(mrow[:, None, :], (1, 4, 1)).reshape(1, 16 * 256)
    c["s_seqrow"] = np.tile((np.arange(16)[:, None] == bj[None, :]).astype(np.float32).reshape(1, 1024), (128, 1))
    sel = np.zeros((4, 64), np.float32)
    sel[tj, j] = 1.0
    c["s_sel"] = sel
    R = np.zeros((32, 16, 128), np.float32)
    for it in range(16):
        for m in range(128):
            R[2 * it + m // 64, it, m] = 1.0
    c["s_rep"] = R.reshape(32, 2048)
    return c


CONST_SHAPES = {k: v.shape for k, v in host_consts().items()}


class _Stop(Exception):
    pass


def pipeline(tasks, stages):
    n, k = len(tasks), len(stages)
    for step in range(n + k - 1):
        for s in range(k - 1, -1, -1):
            i = step - s
            if 0 <= i < n:
                stages[s](tasks[i])


class Rot:
    def __init__(self, items):
        self.items, self.i = items, 0

    def next(self):
        x = self.items[self.i % len(self.items)]
        self.i += 1
        return x


def build_program(order=None, NG=8, do_sample=True, debug=(), NB=4, NSLOT=4, same_engine_sync=True, stop=None, wreuse=True):
    T = NB * 128
    nc = bass.Bass("TRN2", target_bir_lowering=False)

    def din(name, shape):
        return nc.dram_tensor(name, list(shape), F32, kind="ExternalInput").ap()

    def dout(name, shape):
        return nc.dram_tensor(name, list(shape), F32, kind="ExternalOutput").ap()

    xp = din("xp", [SEQ, D])
    xs_in = din("xs", [TS, D])
    ck_in = din("ck", [SB_PER_CORE, 128, 256])
    cv_in = din("cv", [SB_PER_CORE, 128, 256])
    sconv_in = din("sconv", [3 * SB_PER_CORE, CONVD])
    sssm_in = din("sssm", [SB_PER_CORE, 2048, 128])
    sffn_in = din("sffn", [2, 2 * SB_PER_CORE, NUP])
    table = din("table", [32, 16])
    wqkv = din("wqkv", [D, 1536])
    wo = din("wo", [D, D])
    w_in = din("w_in", [D, SIN])
    w_out = din("w_out", [DIN, D])
    w_up = [din("w_up0", [D, NUP]), din("w_up1", [D, NUP])]
    w_dn = [din("w_dn0", [DFF, D]), din("w_dn1", [DFF, D])]
    gT_in = din("gT", [128, 4 * KT])
    gfin_in = din("gfin", [1, D])
    sinks_in = din("sinks", [1, 16])
    fcw_in = din("fcw", [128, 2 * 44 * 4])
    scw_in = din("scw", [128, 32 * 5])
    ssmv_in = din("ssmv", [1, 96])
    nwT_in = din("nwT", [128, 16])
    cin = {k: din("c_" + k, list(s)) for k, s in CONST_SHAPES.items()}

    o_y = dout("o_y", [SEQ, D])
    o_pk = dout("o_pk", [128, 256])
    o_pv = dout("o_pv", [128, 256])
    o_pconv = dout("o_pconv", [3, CONVD])
    o_pssm = dout("o_pssm", [2048, 128])
    o_pffn = dout("o_pffn", [2, 2, NUP])
    o_ys = dout("o_ys", [TS, D])
    o_sk = dout("o_sk", [SB_PER_CORE, 128, 256])
    o_sv = dout("o_sv", [SB_PER_CORE, 128, 256])
    o_sconv = dout("o_sconv", [3 * SB_PER_CORE, CONVD])
    o_sssm = dout("o_sssm", [SB_PER_CORE, 2048, 128])
    o_sffn = dout("o_sffn", [2, 2 * SB_PER_CORE, NUP])
    dbg_out = {name: dout("dbg_" + name, shape) for name, shape in debug}
    bt_dram = nc.dram_tensor("bt_scr", [2, 128, 2048], F32).ap()

    P = Prog(nc, same_engine_sync=same_engine_sync)
    ar = Arena(nc, 16640, 229000)
    A = ar.alloc

    PS = [nc.alloc_psum_tensor(f"psb{i}", [128, 512], F32) for i in range(8)]
    PK = [f"ps{i}" for i in range(8)]

    xg = A("xg", [128, NB, D], F32)
    hT = A("hT", [128, 16, T], BF16)
    ring = [A(f"ring{i}", [128, 4096], BF16) for i in range(NSLOT)]
    hst = A("hst", [128, SH, HD], F32)
    hst_bf = A("hst_bf", [128, SH, HD], BF16)
    ident = A("ident", [128, 128], F32)
    tri = A("tri", [128, 128], F32)
    ugt = A("ugt", [128, 128], F32)
    ones = A("ones", [128, 128], F32)
    ident_bf = A("ident_bf", [128, 128], BF16)
    ones_bf = A("ones_bf", [128, 128], BF16)
    gbc = A("gbc", [128, D], F32)
    KTb = A("KTb", [64, NKV, 128 + T], BF16)
    Vau = A("Vau", [128, NB + 1, NKV, 66], BF16)
    fhist = A("fhist", [128, 2, 44, 2], F32)
    chist = A("chist", [128, 32, 3], F32)
    gT = A("gT", [128, 4, KT], F32)
    fcw = A("fcw", [128, 2, 44, 4], F32)
    scw = A("scw", [128, 32, 5], F32)
    nwT = A("nwT", [128, 16], F32)
    esink = A("esink", [128, 16], F32)
    dtb = A("dtb", [128, 32], F32)
    Abc = A("Abc", [128, 32], F32)
    Dbc = A("Dbc", [128, 32], F32)
    small = A("small", [128, 16, 16], F32)
    sqs = Rot([A("sq", [128, 256], BF16) for _ in range(2)])
    smi = [0]

    def sm():
        i = smi[0] % 16
        smi[0] += 1
        return small[:, i, :], small.k(i)

    def mm(out, lhsT, rhs, start, stop, reads, writes, track):
        fn = lambda e: e.matmul(out, lhsT, rhs, start=start, stop=stop)
        (P.op if track else P.raw)("pe", fn, reads=reads, writes=writes)

    def tr(out, in_, idn, reads, writes, track):
        fn = lambda e: e.transpose(out, in_, idn)
        (P.op if track else P.raw)("pe", fn, reads=list(reads) + [ident.k()], writes=writes)

    PSB = [p.bitcast(BF16) for p in PS]

    def trb(out, in_, idn, reads, writes, track):
        fn = lambda e: e.transpose(out, in_, idn)
        (P.op if track else P.raw)("pe", fn, reads=list(reads) + [ident_bf.k()], writes=writes)

    def act(out, in_, func, reads, writes, **kw):
        P.op("act", lambda e: e.activation(out, in_, func, **kw), reads=reads, writes=writes)

    def acopy(out, in_, reads, writes):
        P.op("act", lambda e: e.copy(out, in_), reads=reads, writes=writes)

    def cp(eng, out, in_, reads, writes):
        if eng == "act":
            return acopy(out, in_, reads, writes)
        P.op(eng, lambda e: e.tensor_copy(out, in_), reads=reads, writes=writes)

    def tt(eng, out, in0, in1, op, reads, writes):
        P.op(eng, lambda e: e.tensor_tensor(out, in0, in1, op), reads=reads, writes=writes)

    def ts2(eng, out, in0, s1, s2, op0, op1, reads, writes):
        P.op(eng, lambda e: e.tensor_scalar(out, in0, s1, s2, op0=op0, op1=op1), reads=reads, writes=writes)

    def ts1(eng, out, in0, s1, op0, reads, writes):
        P.op(eng, lambda e: e.tensor_single_scalar(out, in0, s1, op0), reads=reads, writes=writes)

    def stt(eng, out, in0, scalar, in1, op0, op1, reads, writes):
        P.op(eng, lambda e: e.scalar_tensor_tensor(out, in0, scalar, in1, op0=op0, op1=op1), reads=reads, writes=writes)

    def recip(out, in_, reads, writes):
        P.op("dve", lambda e: e.reciprocal(out, in_), reads=reads, writes=writes)

    def memset(eng, ap, val, writes):
        P.op(eng, lambda e: e.memset(ap, val), writes=writes)

    def dma_in(dst_ap, src_ap, wkeys, eng="sp", rkeys=()):
        P.dma(eng, [lambda e: e.dma_start(out=dst_ap, in_=src_ap)], reads=rkeys, writes=wkeys)

    def dma_out(dst_ap, src_ap, rkeys, eng="sp", wkeys=()):
        P.dma(eng, [lambda e: e.dma_start(out=dst_ap, in_=src_ap)], reads=rkeys, writes=wkeys)

    ps_rr = {}

    def ps_next(pool):
        i = ps_rr.get(pool, 0)
        ps_rr[pool] = i + 1
        return pool[i % len(pool)]

    class WS:
        def __init__(self):
            self.req = []
            self.issued = 0
            self.i = 0
            self.scr = {}
            self.reuse = wreuse

        def _issue(self, r):
            wname, W, kt0, nkt, c0, ncol = order[r]
            slot = ring[r % NSLOT]
            ck = (wname, kt0, nkt, c0, ncol)
            flat = slot[:, 0:nkt * ncol]
            if ck not in self.scr:
                dst = flat.rearrange("p (k c) -> p k c", k=nkt)
                src = W[kt0 * 128:(kt0 + nkt) * 128, c0:c0 + ncol].rearrange("(k p) c -> p k c", p=128)
                P.dma("pool", [lambda e: e.dma_start(out=dst, in_=src)], writes=[slot.k()])
                if self.reuse:
                    scr = nc.dram_tensor(f"wscr{len(self.scr)}", [128, nkt * ncol], BF16).ap()
                    self.scr[ck] = (scr, f"wscr{len(self.scr)}")
                    P.dma("sp", [lambda e: e.dma_start(out=scr, in_=flat)], reads=[slot.k()], writes=[self.scr[ck][1]])
            else:
                scr, skey = self.scr[ck]
                P.dma("sp", [lambda e: e.dma_start(out=flat, in_=scr)], reads=[skey], writes=[slot.k()])

        def get(self, wname, W, kt0, nkt, c0, ncol):
            self.req.append((wname, kt0, nkt, c0, ncol))
            r = self.i
            self.i += 1
            if order is None:
                slot = ring[0]
            else:
                assert order[r][2:] == (kt0, nkt, c0, ncol), (order[r][2:], (kt0, nkt, c0, ncol))
                while self.issued < min(r + NSLOT, len(order)):
                    self._issue(self.issued)
                    self.issued += 1
                slot = ring[r % NSLOT]
            return slot[:, 0:nkt * ncol].rearrange("p (k c) -> p k c", k=nkt), slot.k()

    wmap = {"wqkv": wqkv, "wo": wo, "w_in": w_in, "w_out": w_out, "w_up0": w_up[0], "w_up1": w_up[1],
            "w_dn0": w_dn[0], "w_dn1": w_dn[1]}
    if order is not None:
        order = [(o[0], wmap[o[0]]) + tuple(o[1:]) for o in order]
    ws = WS()

    dma_in(ident[:], cin["ident"], [ident.k()])
    dma_in(tri[:], cin["tri"], [tri.k()])
    dma_in(ugt[:], cin["ugt"], [ugt.k()])
    dma_in(ones[:], cin["ones"], [ones.k()])
    dma_in(gT[:], gT_in.rearrange("p (i k) -> p i k", i=4), [gT.k()])
    dma_in(fcw[:], fcw_in.rearrange("p (l j w) -> p l j w", l=2, j=44), [fcw.k()])
    dma_in(scw[:], scw_in.rearrange("p (j w) -> p j w", j=32), [scw.k()])
    dma_in(nwT[:], nwT_in, [nwT.k()])
    dma_in(gbc[:], gfin_in[0:1, :].to_broadcast([128, D]), [gbc.k()])
    dma_in(esink[:], sinks_in[0:1, :].to_broadcast([128, 16]), [esink.k()])
    dma_in(dtb[:], ssmv_in[0:1, 0:32].to_broadcast([128, 32]), [dtb.k()])
    dma_in(Abc[:], ssmv_in[0:1, 32:64].to_broadcast([128, 32]), [Abc.k()])
    dma_in(Dbc[:], ssmv_in[0:1, 64:96].to_broadcast([128, 32]), [Dbc.k()])
    cp("dve", ident_bf[:], ident[:], [ident.k()], [ident_bf.k()])
    cp("dve", ones_bf[:], ones[:], [ones.k()], [ones_bf.k()])
    act(esink[:], esink[:], AF.Exp, [esink.k()], [esink.k()])
    act(Abc[:], Abc[:], AF.Exp, [Abc.k()], [Abc.k()])
    ts1("dve", Abc[:], Abc[:], -1.0, ALU.mult, [Abc.k()], [Abc.k()])
    memset("dve", fhist[:], 0.0, [fhist.k()])
    memset("dve", chist[:], 0.0, [chist.k()])
    memset("dve", hst[:], 0.0, [hst.k()])
    memset("dve", hst_bf[:], 0.0, [hst_bf.k()])
    memset("dve", Vau[:], 1.0, [Vau.k()])
    memset("dve", KTb[:], 0.0, [KTb.k()])

    m0 = ar.mark()
    grev = A("grev", [32, 384], F32)
    tab = A("tab", [32, 16], F32)
    mk2 = [A("maskp", [128, 128], F32), A("masko", [128, 128], F32)]
    btt = A("btt", [128, 16, 128], F32)
    dma_in(grev[:], cin["grev"], [grev.k()])
    dma_in(tab[:], table, [tab.k()])
    dma_in(mk2[0][:], cin["maskp"], [mk2[0].k()])
    dma_in(mk2[1][:], cin["masko"], [mk2[1].k()])
    btq = A("btq", [128, 128, 16], F32)
    for which in range(2):
        for q in range(128):
            pb = q // 32
            st_ = (127 - q) if which == 0 else (255 - q)
            mm(PS[pb][:, (q % 32) * 16:(q % 32) * 16 + 16], grev[:, st_:st_ + 128], tab[:, 0:16], True, True,
               [grev.k(), tab.k()], [PK[pb]], q % 32 == 31)
        for pb in range(4):
            tt("dve", btq[:, pb * 32:(pb + 1) * 32, :], PS[pb][:, 0:512].rearrange("p (q h) -> p q h", h=16),
               mk2[which][:, pb * 32:(pb + 1) * 32].unsqueeze(2).to_broadcast([128, 32, 16]), ALU.add,
               [PK[pb], mk2[which].k()], [btq.k((pb * 32, pb * 32 + 32))])
        cp("dve", btt[:], btq[:].rearrange("p q h -> p h q"), [btq.k()], [btt.k()])
        dma_out(bt_dram[which], btt[:].rearrange("p h q -> p (h q)"), [btt.k()], wkeys=["bt_dram"])
    ar.reset(m0)
    phase_mark = ar.mark()

    def norm_T_multi(items, gi_, dst, rows=128):
        st = {}
        mloc = ar.mark()
        xns = Rot([A("xn", [128, D], F32) for _ in range(3)])
        sqn = Rot([A("sqn", [128, D], BF16) for _ in range(2)])

        def s0(it):
            xap, xkey, col0 = it
            sqb = sqn.next()
            s_ap, s_k = sm()
            st[col0] = (s_ap, s_k, xns.next())
            memset("dve", s_ap[0:rows, 0:1], 0.0, [s_k])
            act(sqb[0:rows, :], xap, AF.Square, [xkey], [sqb.k(), s_k], accum_out=s_ap[0:rows, 0:1])

        def s1(it):
            s_ap, s_k, _ = st[it[2]]
            act(s_ap[0:rows, 1:2], s_ap[0:rows, 0:1], AF.Sqrt, [s_k], [s_k], bias=EPS, scale=1.0 / D)

        def s2(it):
            s_ap, s_k, _ = st[it[2]]
            recip(s_ap[0:rows, 2:3], s_ap[0:rows, 1:2], [s_k], [s_k])

        def s3(it):
            xap, xkey, col0 = it
            s_ap, s_k, xnb = st[col0]
            ts1("dve", xnb[0:rows, :], xap, s_ap[0:rows, 2:3], ALU.mult, [xkey, s_k], [xnb.k()])

        def mk_tr(half):
            def f(it):
                xap, xkey, col0 = it
                s_ap, s_k, xnb = st[col0]
                pb = ps_next((4, 5, 6, 7))
                st[(col0, half)] = pb
                for q in range(4):
                    kt = half * 4 + q
                    tr(PS[pb][:, q * 128:q * 128 + rows], xnb[0:rows, kt * 128:(kt + 1) * 128], ident[0:rows, 0:rows],
                       [xnb.k()], [PK[pb]], q == 3)
            return f

        def mk_ev(half):
            def f(it):
                xap, xkey, col0 = it
                pb = st[(col0, half)]
                pv = PS[pb][:, 0:512].rearrange("p (a b) -> p a b", a=4)[:, :, 0:rows]
                tt("dve", dst[:, half * 4:half * 4 + 4, col0:col0 + rows], pv,
                   gT[:, gi_, half * 4:half * 4 + 4].unsqueeze(2).to_broadcast([128, 4, rows]), ALU.mult,
                   [PK[pb], gT.k()], [dst.k((half * 4, half * 4 + 4))])
            return f

        pipeline(items, [s0, s1, s2, s3, mk_tr(0), mk_tr(1), mk_ev(0), mk_ev(1)])
        ar.reset(mloc)

    def norm_T(xap, xkey, gi_, dst, col0, rows=128):
        norm_T_multi([(xap, xkey, col0)], gi_, dst, rows)

    def rstd_of(xap, xkey, rows=128):
        mloc = ar.mark()
        sqb = A("sqf", [128, D], BF16)
        ar.reset(mloc)
        s_ap, s_k = sm()
        act(sqb[0:rows, :], xap, AF.Square, [xkey], [sqb.k(), s_k], accum_out=s_ap[0:rows, 0:1])
        act(s_ap[0:rows, 1:2], s_ap[0:rows, 0:1], AF.Sqrt, [s_k], [s_k], bias=EPS, scale=1.0 / D)
        recip(s_ap[0:rows, 2:3], s_ap[0:rows, 1:2], [s_k], [s_k])
        return s_ap[0:rows, 2:3], s_k

    def resid_add(xblocks, rows):
        for b, (xap, xkey) in enumerate(xblocks):
            for hf in range(2):
                pb = 2 * b + hf
                tt("dve", xap[:, hf * 512:(hf + 1) * 512], xap[:, hf * 512:(hf + 1) * 512], PS[pb][0:rows, 0:512], ALU.add,
                   [PK[pb], xkey], [xkey])

    def proj_tm_acc(wname, W, nkt_total, ktchunk, src, src_key, xblocks, rows):
        nch = (nkt_total + ktchunk - 1) // ktchunk
        for c in range(nch):
            nk = min(ktchunk, nkt_total - c * ktchunk)
            wv, wk = ws.get(wname, W, c * ktchunk, nk, 0, 1024)
            for b in range(len(xblocks)):
                for kk in range(nk):
                    kt = c * ktchunk + kk
                    for hf in range(2):
                        mm(PS[2 * b + hf][0:rows, 0:512], src[:, kt, b * 128:b * 128 + rows], wv[:, kk, hf * 512:(hf + 1) * 512],
                           kt == 0, kt == nkt_total - 1, [wk, src_key(kt)], [PK[2 * b + hf]],
                           kt == nkt_total - 1 or (kk == nk - 1 and hf == 1))
        resid_add(xblocks, rows)

    def attn_prompt(gi):
        m = ar.mark()
        QT = A("QT", [64, NH, T], BF16)
        BT = [A("BTp", [128, NH, 128], F32), A("BTo", [128, NH, 128], F32)]
        PTs = Rot([A("PT", [128, 2, 4, 128], BF16) for _ in range(4)])
        spb = Rot([A("spb", [128, 512], F32) for _ in range(6)])
        Otoks = [A("Otok", [128, NH, HD], BF16) for _ in range(2)]
        OT = A("OT", [128, KT, T], BF16)
        den = A("den", [128, 8, 16], F32)
        kvo = A("kvo", [128, 512], F32)
        for w in range(2):
            dma_in(BT[w][:].rearrange("p h q -> p (h q)"), bt_dram[w], [BT[w].k()], rkeys=["bt_dram"])
        for b in range(NB):
            r0 = (gi * NB + b) * 128
            dma_in(xg[:, b, :], xp[r0:r0 + 128, :], [xg.k(b)])
        if stop == "a_load":
            raise _Stop()
        norm_T_multi([(xg[:, b, :], xg.k(b), b * 128) for b in range(NB)], 0, hT)
        if stop == "a_norm":
            raise _Stop()
        hk = hT.k((0, 8))
        last = gi == NG - 1
        for c in range(2):
            wv, wk = ws.get("wqkv", wqkv, 0, 8, c * 512, 512)
            for hl in range(8):
                h = c * 8 + hl
                pb = ps_next((0, 1))
                for kt in range(8):
                    mm(PS[pb][0:64, 0:T], wv[:, kt, hl * 64:(hl + 1) * 64], hT[:, kt, 0:T], kt == 0, kt == 7, [wk, hk], [PK[pb]], kt == 7)
                act(QT[:, h, :], PS[pb][0:64, 0:T], AF.Identity, [PK[pb]], [QT.k(h)], scale=0.125)
        if stop == "a_q":
            raise _Stop()
        wv, wk = ws.get("wqkv", wqkv, 0, 8, 1024, 512)
        for g in range(NKV):
            pb = ps_next((0, 1))
            for kt in range(8):
                mm(PS[pb][0:64, 0:T], wv[:, kt, g * 64:(g + 1) * 64], hT[:, kt, 0:T], kt == 0, kt == 7, [wk, hk], [PK[pb]], kt == 7)
            acopy(KTb[:, g, 128:128 + T], PS[pb][0:64, 0:T], [PK[pb]], [KTb.k(g)])
        if stop == "a_k":
            raise _Stop()
        for b in range(NB):
            if stop == "a_v0" and b == 1:
                raise _Stop()
            if stop == "a_v1" and b == 3:
                raise _Stop()
            pb = ps_next((2, 3))
            for kt in range(8):
                mm(PS[pb][:, 0:256], hT[:, kt, b * 128:(b + 1) * 128], wv[:, kt, 256:512], kt == 0, kt == 7, [wk, hk], [PK[pb]], kt == 7)
            acopy(Vau[:, b + 1, :, 0:64], PS[pb][:, 0:256].rearrange("p (g d) -> p g d", g=4), [PK[pb]], [Vau.k(b + 1)])
            if last and b == NB - 1:
                if stop == "a_v2":
                    raise _Stop()
                cp("act", kvo[:, 256:512], PS[pb][:, 0:256], [PK[pb]], [kvo.k()])
                if stop == "a_v3a":
                    raise _Stop()
                pb2 = ps_next((2, 3))
                for kt in range(8):
                    mm(PS[pb2][:, 0:256], hT[:, kt, b * 128:(b + 1) * 128], wv[:, kt, 0:256], kt == 0, kt == 7, [wk, hk], [PK[pb2]], kt == 7)
                cp("act", kvo[:, 0:256], PS[pb2][:, 0:256], [PK[pb2]], [kvo.k()])
                if stop == "a_v3":
                    raise _Stop()
                dma_out(o_pk, kvo[:, 0:256], [kvo.k()])
                dma_out(o_pv, kvo[:, 256:512], [kvo.k()])
        if stop == "a_kv":
            raise _Stop()
        st = {}

        def a0(tk):
            b, g = tk
            first = gi == 0 and b == 0
            pbs = [None, None]
            for w in range(2):
                if w == 0 and first:
                    continue
                pb = ps_next((0, 1, 2, 3))
                pbs[w] = pb
                kc = (b + w) * 128
                mm(PS[pb][:, 0:512].rearrange("p (a q) -> p a q", a=4), KTb[:, g, kc:kc + 128],
                   QT[:, 4 * g:4 * g + 4, b * 128:(b + 1) * 128], True, True, [KTb.k(g), QT.k((4 * g, 4 * g + 4))], [PK[pb]], True)
            st[tk] = {"pbs": pbs, "sp": [None, None], "pt": None}

        def a1(tk):
            b, g = tk
            for w in range(2):
                pb = st[tk]["pbs"][w]
                if pb is None:
                    continue
                sp_ = spb.next()
                st[tk]["sp"][w] = sp_
                tt("dve", sp_[:], PS[pb][:, 0:512], BT[w][:, 4 * g:4 * g + 4, :].rearrange("p h q -> p (h q)"), ALU.add,
                   [PK[pb], BT[w].k()], [sp_.k()])

        def a2(tk):
            pt = PTs.next()
            st[tk]["pt"] = pt
            for w in range(2):
                sp_ = st[tk]["sp"][w]
                if sp_ is None:
                    continue
                act(pt[:, w].rearrange("p h q -> p (h q)"), sp_[:], AF.Exp, [sp_.k()], [pt.k(w)])

        def a3(tk):
            b, g = tk
            first = gi == 0 and b == 0
            pt = st[tk]["pt"]
            po = ps_next((4, 5))
            st[tk]["po"] = po
            for hl in range(4):
                if not first:
                    mm(PS[po][:, hl * 65:(hl + 1) * 65], pt[:, 0, hl, :], Vau[:, b, g, 0:65], True, False, [pt.k(0), Vau.k(b)], [PK[po]], False)
                mm(PS[po][:, hl * 65:(hl + 1) * 65], pt[:, 1, hl, :], Vau[:, b + 1, g, 0:65], first, True, [pt.k(1), Vau.k(b + 1)], [PK[po]], hl == 3)

        def a4(tk):
            b, g = tk
            po = st[tk]["po"]
            Otok = Otoks[b % 2]
            pov = PS[po][:, 0:260].rearrange("p (h e) -> p h e", h=4)
            dsl = (b * NKV + g) % 8
            dn = den[:, dsl, 0:4]
            tt("dve", dn, pov[:, :, 64], esink[:, 4 * g:4 * g + 4], ALU.add, [PK[po], esink.k()], [den.k(dsl)])
            recip(dn, dn, [den.k(dsl)], [den.k(dsl)])
            tt("dve", Otok[:, 4 * g:4 * g + 4, :], pov[:, :, 0:64], dn.unsqueeze(2).to_broadcast([128, 4, 64]), ALU.mult,
               [PK[po], den.k(dsl)], [Otok.k((4 * g, 4 * g + 4))])

        def a5(tk):
            b, g = tk
            if g != NKV - 1:
                return
            Otok = Otoks[b % 2]
            of = Otok[:].rearrange("p h d -> p (h d)")
            pbs = []
            for half in range(2):
                pb = ps_next((6, 7))
                pbs.append(pb)
                for q in range(4):
                    kt = half * 4 + q
                    trb(PSB[pb][:, q * 128:(q + 1) * 128], of[:, kt * 128:(kt + 1) * 128], ident_bf[:], [Otok.k()], [PK[pb]], q == 3)
            st[tk]["tp"] = pbs

        def a6(tk):
            b, g = tk
            if g != NKV - 1:
                return
            for half in range(2):
                pb = st[tk]["tp"][half]
                acopy(OT[:, half * 4:half * 4 + 4, b * 128:(b + 1) * 128], PSB[pb][:, 0:512].rearrange("p (a q) -> p a q", a=4),
                      [PK[pb]], [OT.k((half * 4, half * 4 + 4))])

        pipeline([(b, g) for b in range(NB) for g in range(NKV)], [a0, a1, a2, a3, a4, a5, a6])
        if stop == "a_core":
            raise _Stop()
        cp("pool", KTb[:, :, 0:128], KTb[:, :, T:T + 128], [KTb.k()], [KTb.k()])
        cp("pool", Vau[:, 0], Vau[:, NB], [Vau.k(NB)], [Vau.k(0)])
        proj_tm_acc("wo", wo, 8, 4, OT, lambda kt: OT.k(kt), [(xg[:, b, :], xg.k(b)) for b in range(NB)], 128)
        ar.reset(m)

    def ffn(l, xblocks, Tn, rows, S, hist_src, mode, last):
        m = ar.mark()
        Hc = 2 * S
        Us = Rot([A("U", [128, Hc + Tn], F32) for _ in range(5)])
        tmps = Rot([A("ft", [128, Tn], F32) for _ in range(5)])
        actb = A("actb", [128, 22, Tn], BF16)
        uos = Rot([A("uo", [128, 512], F32) for _ in range(2)])
        norm_T_multi([(xap, xkey, b * 128) for b, (xap, xkey) in enumerate(xblocks)], 1 + 2 * l, hT, rows)
        hk = hT.k((0, 8))
        st = {}

        def s0(j):
            c, q = divmod(j, 4)
            if q == 0:
                st["w"] = ws.get(f"w_up{l}", w_up[l], 0, 8, c * 512, 512)
            wv, wk = st["w"]
            pb = ps_next((0, 1, 2, 3))
            st[j] = [pb, None, None]
            for kt in range(8):
                mm(PS[pb][:, 0:Tn], wv[:, kt, q * 128:(q + 1) * 128], hT[:, kt, 0:Tn], kt == 0, kt == 7, [wk, hk], [PK[pb]], kt == 7)
            if last and q == 3:
                nr = 2 if mode == "p" else TS
                c0 = Tn - 2 if mode == "p" else 0
                pb2 = ps_next((4, 5))
                for kt in range(8):
                    mm(PS[pb2][0:nr, 0:512], hT[:, kt, c0:c0 + nr], wv[:, kt, :], kt == 0, kt == 7, [wk, hk], [PK[pb2]], kt == 7)
                uo = uos.next()
                acopy(uo[0:nr, :], PS[pb2][0:nr, 0:512], [PK[pb2]], [uo.k()])
                if mode == "p":
                    dma_out(o_pffn[l, :, c * 512:(c + 1) * 512], uo[0:2, :], [uo.k()])
                else:
                    dma_out(o_sffn[l, :, c * 512:(c + 1) * 512], uo[32:64, :], [uo.k()])

        def s1(j):
            pb = st[j][0]
            U = Us.next()
            st[j][1] = U
            acopy(U[:, Hc:Hc + Tn], PS[pb][:, 0:Tn], [PK[pb]], [U.k()])

        def s2(j):
            U = st[j][1]
            pb = st[j][0]
            hap, hkey = hist_src(j)
            cp("act", U[:, 0:Hc], hap, [hkey], [U.k()])
            if mode == "p":
                cp("act", fhist[:, l, j, :], U[:, Tn:Tn + 2], [U.k()], [fhist.k(l, j)])
            t = tmps.next()
            st[j][2] = t
            wj = fcw[:, l, j, :]
            if j < 22:
                ts2("pool", t[:], U[:, 2 * S:2 * S + Tn], wj[:, 2:3], wj[:, 3:4], ALU.mult, ALU.add, [U.k(), fcw.k()], [t.k()])
            else:
                act(t[:], U[:, 2 * S:2 * S + Tn], AF.Identity, [U.k(), fcw.k()], [t.k()], scale=wj[:, 2:3], bias=wj[:, 3:4])

        def s3(j):
            U, t = st[j][1], st[j][2]
            wj = fcw[:, l, j, :]
            stt("dve", t[:], U[:, S:S + Tn], wj[:, 1:2], t[:], ALU.mult, ALU.add, [U.k(), fcw.k(), t.k()], [t.k()])

        def s4(j):
            U, t = st[j][1], st[j][2]
            wj = fcw[:, l, j, :]
            stt("dve", t[:], U[:, 0:Tn], wj[:, 0:1], t[:], ALU.mult, ALU.add, [U.k(), fcw.k(), t.k()], [t.k()])

        def s5(j):
            t = st[j][2]
            if j < 22:
                act(actb[:, j, :], t[:], AF.Silu, [t.k()], [actb.k(j)])
            else:
                tt("pool", actb[:, j - 22, :], actb[:, j - 22, :], t[:], ALU.mult, [actb.k(j - 22), t.k()], [actb.k(j - 22)])

        pipeline(list(range(44)), [s0, s1, s2, s3, s4, s5])
        proj_tm_acc(f"w_dn{l}", w_dn[l], 22, 4, actb, lambda kt: actb.k(kt), xblocks, rows)
        ar.reset(m)

    def final_out(xblocks, dsts, rows):
        m = ar.mark()
        ybs = Rot([A("yb", [128, D], F32) for _ in range(2)])
        for (xap, xkey), dst in zip(xblocks, dsts):
            r_ap, r_k = rstd_of(xap, xkey, rows)
            yb = ybs.next()
            stt("dve", yb[0:rows, :], xap, r_ap, gbc[0:rows, :], ALU.mult, ALU.mult, [xkey, r_k, gbc.k()], [yb.k()])
            dma_out(dst, yb[0:rows, :], [yb.k()])
        ar.reset(m)

    def scan_group(R, chunks, BcT, CT, tri_m, ugt_m, tot_m):
        nch = len(chunks)
        s32s = [A("s32", [R, 8, 32], F32) for _ in range(nch)]
        rhsAs = [A("rhsA", [R, 16, R], F32) for _ in range(2)]
        CBms = [A("CBm", [R, 4, R], F32) for _ in range(2)]
        Es = Rot([A("E", [R, 4 * R], F32) for _ in range(2)])
        WTs = [A("WT", [R, 16, R], BF16) for _ in range(2)]
        xdt = A("xdt", [R, 16, HD], BF16)
        xw = A("xw", [R, 16, HD], BF16)
        xD = A("xD", [R, 16, HD], BF16)
        ytmp = A("ytmp", [R, 16, HD], F32)
        ybs = [A("ybh", [R, 1024], F32) for _ in range(2)]
        st = {}

        def names(tk):
            ci, half = tk
            s32 = s32s[ci]
            return (chunks[ci], s32, s32.k(), [s32[:, i, :] for i in range(6)], (2 * ci + half) % 2, 16 * half)

        def only0(f):
            def g(tk):
                if tk[1] == 0:
                    f(tk)
            return g

        def p0(tk):
            ch, s32, k32, (a_, cs, ecs, cd, dte, ssq), si, hs = names(tk)
            tt("dve", a_, ch["dt"], Abc[0:R, :], ALU.mult, [ch["dt_key"], Abc.k()], [k32])
            memset("dve", ssq[:, 0:8], 0.0, [k32])

        def p1(tk):
            ch, s32, k32, (a_, cs, ecs, cd, dte, ssq), si, hs = names(tk)
            pc = ps_next((4, 5, 6, 7))
            mm(PS[pc][0:R, 0:32], tri_m[0:R, 0:R], a_, True, True, [tri_m.k(), k32], [PK[pc]], False)
            mm(PS[pc][0:R, 32:64], tot_m[0:R, 0:R], a_, True, True, [tot_m.k(), k32], [PK[pc]], True)
            cp("dve", s32[:, 6:8, :].rearrange("p a b -> p (a b)"), PS[pc][0:R, 0:64], [PK[pc]], [k32])

        def p2(tk):
            ch, s32, k32, (a_, cs, ecs, cd, dte, ssq), si, hs = names(tk)
            cp("dve", cs, s32[:, 6, :], [k32], [k32])

        def p3(tk):
            ch, s32, k32, (a_, cs, ecs, cd, dte, ssq), si, hs = names(tk)
            act(ecs, cs, AF.Exp, [k32], [k32])
            act(cd, s32[:, 7, :], AF.Exp, [k32], [k32])
            tt("dve", dte, s32[:, 7, :], cs, ALU.subtract, [k32], [k32])

        def p4(tk):
            ch, s32, k32, (a_, cs, ecs, cd, dte, ssq), si, hs = names(tk)
            act(dte, dte, AF.Exp, [k32], [k32])

        def p5(tk):
            ch, s32, k32, (a_, cs, ecs, cd, dte, ssq), si, hs = names(tk)
            tt("dve", dte, dte, ch["dt"], ALU.mult, [k32, ch["dt_key"]], [k32])

        def S0(tk):
            ch, s32, k32, (a_, cs, ecs, cd, dte, ssq), si, hs = names(tk)
            rhsA = rhsAs[si]
            tt("dve", rhsA[:], tri_m[0:R, 0:R].unsqueeze(1).to_broadcast([R, 16, R]),
               a_[:, hs:hs + 16].unsqueeze(2).to_broadcast([R, 16, R]), ALU.mult, [tri_m.k(), k32], [rhsA.k()])

        def CBst(tk):
            ch, s32, k32, _, si, hs = names(tk)
            pcb = ps_next((4, 5, 6, 7))
            for gg in range(4):
                g = 4 * (hs // 16) + gg
                mm(PS[pcb][0:R, gg * R:(gg + 1) * R], BcT[:, g, ch["cb"]], CT[:, g, ch["cb"]], True, True, [BcT.k(g), CT.k(g)], [PK[pcb]], gg == 3)
            tt("dve", CBms[si][:], PS[pcb][0:R, 0:4 * R].rearrange("p (g l) -> p g l", g=4),
               tri_m[0:R, 0:R].unsqueeze(1).to_broadcast([R, 4, R]), ALU.mult, [PK[pcb], tri_m.k()], [CBms[si].k()])

        def mkD(lo):
            def f(tk):
                ch, s32, k32, _, si, hs = names(tk)
                rhsA = rhsAs[si]
                for gg in range(lo, lo + 2):
                    pd = gg
                    for hl in range(4):
                        mm(PS[pd][0:R, hl * R:(hl + 1) * R], ugt_m[0:R, 0:R], rhsA[:, gg * 4 + hl, :], True, True, [ugt_m.k(), rhsA.k()], [PK[pd]], hl == 3)
            return f

        def mkE(lo):
            def f(tk):
                ch, s32, k32, _, si, hs = names(tk)
                for gg in range(lo, lo + 2):
                    pd = gg
                    E = Es.next()
                    act(E[:], PS[pd][0:R, 0:4 * R], AF.Exp, [PK[pd]], [E.k()])
                    tt("pool" if gg % 2 == 0 else "dve", WTs[si][:, gg * 4:gg * 4 + 4, :], E[:].rearrange("p (h l) -> p h l", h=4),
                       CBms[si][:, gg, :].unsqueeze(1).to_broadcast([R, 4, R]), ALU.mult, [E.k(), CBms[si].k()], [WTs[si].k((gg * 4, gg * 4 + 4))])
            return f

        def S3b(tk):
            ch, s32, k32, (a_, cs, ecs, cd, dte, ssq), si, hs = names(tk)
            half = tk[1]
            xsv = ch["xs"][:, hs * HD:(hs + 16) * HD].rearrange("p (h d) -> p h d", h=16)
            bc = lambda v: v[:, hs:hs + 16].unsqueeze(2).to_broadcast([R, 16, HD])
            tt("dve", xw[:], xsv, bc(dte), ALU.mult, [ch["xs_key"], k32], [xw.k()])
            tt("pool", xdt[:], xsv, bc(ch["dt"]), ALU.mult, [ch["xs_key"], ch["dt_key"]], [xdt.k()])
            tt("pool", xD[:], xsv, bc(Dbc[0:R, :]), ALU.mult, [ch["xs_key"], Dbc.k()], [xD.k()])
            pyo = [ps_next((4, 5, 6, 7)), ps_next((4, 5, 6, 7))]
            yo_aps, yo_keys = ch["yoff"](half, pyo)
            for bk in range(2):
                h0 = hs + bk * 8
                tt("dve", ytmp[:, bk * 8:(bk + 1) * 8, :], yo_aps[bk].rearrange("p (h d) -> p h d", h=8),
                   ecs[:, h0:h0 + 8].unsqueeze(2).to_broadcast([R, 8, HD]), ALU.mult, [yo_keys[bk], k32], [ytmp.k((bk * 8, bk * 8 + 8))])
            if ch["state_mm"] is not None:
                pst = ch["state_mm"](half, xw)
                ch["state_upd"](half, pst, cd, k32)

        def S4(tk):
            ch, s32, k32, (a_, cs, ecs, cd, dte, ssq), si, hs = names(tk)
            half = tk[1]
            yb = ybs[si]
            pyd = [ps_next((4, 5, 6, 7)), ps_next((4, 5, 6, 7))]
            for bk in range(2):
                mm(PS[pyd[bk]][0:R, 0:512], ident_bf[0:R, 0:R], xD[:, bk * 8:(bk + 1) * 8, :].rearrange("p h d -> p (h d)"), True, False,
                   [ident_bf.k(), xD.k()], [PK[pyd[bk]]], False)
                for hh in range(8):
                    h16 = bk * 8 + hh
                    mm(PS[pyd[bk]][0:R, hh * 64:(hh + 1) * 64], WTs[si][:, h16, :], xdt[:, h16, :], False, hh == 7,
                       [WTs[si].k(h16), xdt.k()], [PK[pyd[bk]]], hh == 7)
            for bk in range(2):
                tt("dve", yb[:, bk * 512:(bk + 1) * 512], ytmp[:, bk * 8:(bk + 1) * 8, :].rearrange("p h d -> p (h d)"), PS[pyd[bk]][0:R, 0:512], ALU.add,
                   [PK[pyd[bk]], ytmp.k()], [yb.k()])

        def S5(tk):
            ch, s32, k32, (a_, cs, ecs, cd, dte, ssq), si, hs = names(tk)
            half = tk[1]
            yb = ybs[si]
            c0 = hs * HD
            tt("pool", yb[:], yb[:], ch["zs"][:, c0:c0 + 1024], ALU.mult, [yb.k(), ch["zs_key"]], [yb.k()])
            for gg in range(4):
                sqb = sqs.next()
                act(sqb[0:R, 0:256], yb[:, gg * 256:(gg + 1) * 256], AF.Square, [yb.k()], [sqb.k(), k32], accum_out=ssq[:, 4 * half + gg:4 * half + gg + 1])
            q0 = 4 * half
            act(ssq[:, 8 + q0:12 + q0], ssq[:, q0:q0 + 4], AF.Sqrt, [k32], [k32], bias=EPS, scale=1.0 / 256)
            recip(ssq[:, 16 + q0:20 + q0], ssq[:, 8 + q0:12 + q0], [k32], [k32])
            for gg in range(4):
                act(yb[:, gg * 256:(gg + 1) * 256], yb[:, gg * 256:(gg + 1) * 256], AF.Identity, [yb.k(), k32], [yb.k()],
                    scale=ssq[:, 16 + q0 + gg:17 + q0 + gg])

        def S6(tk):
            ch, s32, k32, _, si, hs = names(tk)
            half = tk[1]
            yb = ybs[si]
            for q4 in range(2):
                pb = ps_next((4, 5, 6, 7))
                for q in range(4):
                    kt = q4 * 4 + q
                    tr(PS[pb][:, q * 128:q * 128 + R], yb[:, kt * 128:(kt + 1) * 128], ident[0:R, 0:R], [yb.k()], [PK[pb]], q == 3)
                k0 = half * 8 + q4 * 4
                for q in range(4):
                    act(hT[:, k0 + q, ch["cb"]], PS[pb][:, q * 128:q * 128 + R], AF.Identity, [PK[pb], nwT.k()], [hT.k(k0 + q)],
                        scale=nwT[:, k0 + q:k0 + q + 1])

        tasks = [(ci, half) for ci in range(nch) for half in range(2)]
        stages = [only0(p0), only0(p1), only0(p2), only0(p3), only0(p4), only0(p5), S0, lambda tk: (CBst(tk), mkD(0)(tk)), lambda tk: (mkE(0)(tk), mkD(2)(tk)),
                  lambda tk: (mkE(2)(tk), S3b(tk)), S4, S5, S6]
        pipeline(tasks, stages)

    def ssd_prompt(gi):
        m = ar.mark()
        last = gi == NG - 1
        zs = A("zs", [128, NB, DIN], BF16)
        xs_tok = A("xs_tok", [128, NB, DIN], BF16)
        BcT = A("BcT", [128, SGR, T], BF16)
        CT = A("CT", [128, SGR, T], BF16)
        Btok = A("Btok", [128, NB, SGR * 128], BF16)
        dts = A("dts", [128, NB, 32], F32)
        m2 = ar.mark()
        U3 = Rot([A("U3", [128, 3 + T], F32) for _ in range(5)])
        tmps = Rot([A("st", [128, T], F32) for _ in range(5)])
        fms = Rot([A("fm", [128, T], BF16) for _ in range(3)])
        uos = Rot([A("uo3", [128, 512], F32) for _ in range(2)])
        xblocks = [(xg[:, b, :], xg.k(b)) for b in range(NB)]
        norm_T_multi([(xap, xkey, b * 128) for b, (xap, xkey) in enumerate(xblocks)], 2, hT)
        hk = hT.k((0, 8))
        for c in range(4):
            wv, wk = ws.get("w_in", w_in, 0, 8, c * 512, 512)
            for b in range(NB):
                pb = ps_next((0, 1, 2, 3))
                for kt in range(8):
                    mm(PS[pb][:, 0:512], hT[:, kt, b * 128:(b + 1) * 128], wv[:, kt, :], kt == 0, kt == 7, [wk, hk], [PK[pb]], kt == 7)
                act(zs[:, b, c * 512:(c + 1) * 512], PS[pb][:, 0:512], AF.Silu, [PK[pb]], [zs.k(b, (c * 512, c * 512 + 512))])
        st = {}

        def x0(j):
            c, q = divmod(j, 4)
            if q == 0:
                st["w"] = ws.get("w_in", w_in, 0, 8, 2048 + c * 512, 512)
            wv, wk = st["w"]
            pb = ps_next((0, 1, 2, 3))
            st[j] = {"pb": pb}
            for kt in range(8):
                mm(PS[pb][:, 0:T], wv[:, kt, q * 128:(q + 1) * 128], hT[:, kt, 0:T], kt == 0, kt == 7, [wk, hk], [PK[pb]], kt == 7)
            if last and q == 3:
                pb2 = ps_next((6, 7))
                for kt in range(8):
                    mm(PS[pb2][0:3, 0:512], hT[:, kt, T - 3:T], wv[:, kt, :], kt == 0, kt == 7, [wk, hk], [PK[pb2]], kt == 7)
                uo = uos.next()
                acopy(uo[0:3, :], PS[pb2][0:3, 0:512], [PK[pb2]], [uo.k()])
                dma_out(o_pconv[:, c * 512:(c + 1) * 512], uo[0:3, :], [uo.k()])

        def x1(j):
            U = U3.next()
            st[j]["U"] = U
            acopy(U[:, 3:3 + T], PS[st[j]["pb"]][:, 0:T], [PK[st[j]["pb"]]], [U.k()])

        def x2(j):
            U = st[j]["U"]
            cp("act", U[:, 0:3], chist[:, j, :], [chist.k(j)], [U.k()])
            cp("act", chist[:, j, :], U[:, T:T + 3], [U.k()], [chist.k(j)])
            t = tmps.next()
            st[j]["t"] = t
            wj = scw[:, j, :]
            ts2("pool", t[:], U[:, 3:3 + T], wj[:, 3:4], wj[:, 4:5], ALU.mult, ALU.add, [U.k(), scw.k()], [t.k()])

        def mk_tap(k):
            def f(j):
                U, t = st[j]["U"], st[j]["t"]
                wj = scw[:, j, :]
                stt("dve", t[:], U[:, k:k + T], wj[:, k:k + 1], t[:], ALU.mult, ALU.add, [U.k(), scw.k(), t.k()], [t.k()])
            return f

        def x6(j):
            t = st[j]["t"]
            if j < 24:
                f = fms.next()
                st[j]["f"] = f
                act(f[:], t[:], AF.Silu, [t.k()], [f.k()])
            else:
                act(CT[:, j - 24, :], t[:], AF.Silu, [t.k()], [CT.k(j - 24)])

        def x7(j):
            if j >= 24:
                return
            f = st[j]["f"]
            pb2 = ps_next((4, 5))
            st[j]["pb2"] = pb2
            for b in range(NB):
                trb(PSB[pb2][:, b * 128:(b + 1) * 128], f[:, b * 128:(b + 1) * 128], ident_bf[:], [f.k()], [PK[pb2]], b == NB - 1)
            if j >= 16:
                cp("pool", BcT[:, j - 16, :], f[:], [f.k()], [BcT.k(j - 16)])

        def x8(j):
            if j >= 24:
                return
            pb2 = st[j]["pb2"]
            pv2 = PSB[pb2][:, 0:T].rearrange("p (b c) -> p b c", b=NB)
            if j < 16:
                cp("dve", xs_tok[:, :, j * 128:(j + 1) * 128], pv2, [PK[pb2]], [xs_tok.k()])
            else:
                g = j - 16
                cp("dve", Btok[:, :, g * 128:(g + 1) * 128], pv2, [PK[pb2]], [Btok.k()])

        pipeline(list(range(32)), [x0, x1, x2, mk_tap(2), mk_tap(1), mk_tap(0), x6, x7, x8])
        wv, wk = ws.get("w_in", w_in, 0, 8, 6144, 32)
        for b in range(NB):
            pb = ps_next((0, 1, 2, 3))
            for kt in range(8):
                mm(PS[pb][:, 0:32], hT[:, kt, b * 128:(b + 1) * 128], wv[:, kt, :], kt == 0, kt == 7, [wk, hk], [PK[pb]], kt == 7)
            tt("dve", dts[:, b, :], PS[pb][:, 0:32], dtb[:], ALU.add, [PK[pb], dtb.k()], [dts.k(b)])
            act(dts[:, b, :], dts[:, b, :], AF.Exp, [dts.k(b)], [dts.k(b)])
            act(dts[:, b, :], dts[:, b, :], AF.Ln, [dts.k(b)], [dts.k(b)], bias=1.0)
        ar.reset(m2)
        def mk_yoff(cb):
            def f(half, pyo):
                for gg in range(4):
                    g = 4 * half + gg
                    bk = gg // 2
                    mm(PS[pyo[bk]][:, (gg % 2) * 256:(gg % 2) * 256 + 256], CT[:, g, cb],
                       hst_bf[:, 4 * g:4 * g + 4, :].rearrange("p h d -> p (h d)"), True, True, [CT.k(g), hst_bf.k((4 * g, 4 * g + 4))], [PK[pyo[bk]]], gg % 2 == 1)
                return [PS[pyo[0]][:, 0:512], PS[pyo[1]][:, 0:512]], [PK[pyo[0]], PK[pyo[1]]]
            return f

        def mk_state_mm(b):
            def f(half, xw):
                pst = [ps_next((4, 5, 6, 7)), ps_next((4, 5, 6, 7))]
                for gg in range(4):
                    g = 4 * half + gg
                    bk = gg // 2
                    mm(PS[pst[bk]][:, (gg % 2) * 256:(gg % 2) * 256 + 256], Btok[:, b, g * 128:(g + 1) * 128],
                       xw[:, gg * 4:gg * 4 + 4, :].rearrange("p h d -> p (h d)"), True, True, [Btok.k(b), xw.k()], [PK[pst[bk]]], gg % 2 == 1)
                return pst
            return f

        def state_upd(half, pst, cd, k32):
            hs = 16 * half
            for bk in range(2):
                h0 = hs + bk * 8
                hv = hst[:, h0:h0 + 8, :]
                tt("pool", hv, hv, cd[:, h0:h0 + 8].unsqueeze(2).to_broadcast([128, 8, HD]), ALU.mult, [hst.k((h0, h0 + 8)), k32], [hst.k((h0, h0 + 8))])
                tt("dve", hv, hv, PS[pst[bk]][:, 0:512].rearrange("p (h d) -> p h d", h=8), ALU.add, [hst.k((h0, h0 + 8)), PK[pst[bk]]],
                   [hst.k((h0, h0 + 8))])
                acopy(hst_bf[:, h0:h0 + 8, :], hv, [hst.k((h0, h0 + 8))], [hst_bf.k((h0, h0 + 8))])

        chunks = []
        for b in range(NB):
            cb = slice(b * 128, (b + 1) * 128)
            chunks.append(dict(cb=cb, xs=xs_tok[:, b, :], xs_key=xs_tok.k(b), zs=zs[:, b, :], zs_key=zs.k(b), dt=dts[:, b, :], dt_key=dts.k(b),
                               yoff=mk_yoff(cb), state_mm=mk_state_mm(b), state_upd=state_upd))
        scan_group(128, chunks, BcT, CT, tri, ugt, ones)
        proj_tm_acc("w_out", w_out, 16, 4, hT, lambda kt: hT.k(kt), xblocks, 128)
        if last:
            ar.reset(m2)
            hso = A("hso", [128, 16, 128], F32)
            hf = hst[:].rearrange("p h d -> p (h d)")
            for q4 in range(4):
                pb = ps_next((4, 5, 6, 7))
                for q in range(4):
                    it = q4 * 4 + q
                    tr(PS[pb][:, q * 128:(q + 1) * 128], hf[:, it * 128:(it + 1) * 128], ident[:], [hst.k()], [PK[pb]], q == 3)
                acopy(hso[:, q4 * 4:q4 * 4 + 4, :], PS[pb][:, 0:512].rearrange("p (a q) -> p a q", a=4), [PK[pb]], [hso.k()])
            dma_out(o_pssm.rearrange("(i r) n -> r i n", r=128), hso[:], [hso.k()])
        ar.reset(m)

    XS = xg[0:TS, 0, :]
    XSK = xg.k(0)

    def attn_sample():
        m = ar.mark()
        BT = [A("BTp", [128, NH, 128], F32), A("BTo", [128, NH, 128], F32)]
        QT = A("QTs", [64, NH, TS], BF16)
        KTs = A("KTs", [64, NKV, TS], BF16)
        Vs = A("Vs", [TS, NKV, 66], BF16)
        Kcf = Rot([A("Kcf", [128, 256], F32) for _ in range(2)])
        KcT = A("KcT", [64, SB_PER_CORE, NKV, 128], BF16)
        Vc = A("Vc", [128, SB_PER_CORE, NKV, 66], BF16)
        BTos = A("BTos", [TS, NH, TS], F32)
        PTc = A("PTc", [128, SB_PER_CORE, 4, TS], BF16)
        PTo = A("PTo", [TS, 4, TS], BF16)
        spb = Rot([A("spbs", [128, 256], F32) for _ in range(4)])
        Otok = A("Otoks", [TS, NH, HD], BF16)
        OT = A("OTs", [128, KT, TS], BF16)
        den = A("dens", [TS, 8, 16], F32)
        kvs = A("kvs", [TS, 512], F32)
        mrow = A("mrow", [1, SB_PER_CORE * 256], BF16)
        mrow_f = A("mrow_f", [1, SB_PER_CORE * 256], F32)
        samem = A("samem", [TS, 16], F32)
        sel = A("sel", [4, TS], F32)
        for w in range(2):
            dma_in(BT[w][:].rearrange("p h q -> p (h q)"), bt_dram[w], [BT[w].k()], rkeys=["bt_dram"])
        dma_in(mrow_f[:], cin["s_mrow"], [mrow_f.k()])
        cp("dve", mrow[:], mrow_f[:], [mrow_f.k()], [mrow.k()])
        dma_in(samem[:], cin["s_samem"], [samem.k()])
        dma_in(sel[:], cin["s_sel"], [sel.k()])
        dma_in(XS, xs_in, [XSK])
        memset("dve", Vs[:], 1.0, [Vs.k()])
        memset("dve", Vc[:], 1.0, [Vc.k()])
        dma_out(o_sk[:, 0:124, :], ck_in[:, 4:128, :], [])
        dma_out(o_sv[:, 0:124, :], cv_in[:, 4:128, :], [])
        for bq in range(SB_PER_CORE):
            dstv = Vc[:, bq, :, 0:64]
            srcv = cv_in[bq].rearrange("k (g d) -> k g d", g=4)
            P.dma("pool", [lambda e, dstv=dstv, srcv=srcv: e.dma_start(out=dstv, in_=srcv)], writes=[Vc.k(bq)])
        for bq in range(SB_PER_CORE):
            kc = Kcf.next()
            dma_in(kc[:], ck_in[bq], [kc.k()])
            pb = ps_next((0, 1, 2, 3))
            for g in range(NKV):
                tr(PS[pb][0:64, g * 128:(g + 1) * 128], kc[:, g * 64:(g + 1) * 64], ident[:], [kc.k()], [PK[pb]], g == 3)
            acopy(KcT[:, bq].rearrange("p g k -> p (g k)"), PS[pb][0:64, 0:512], [PK[pb]], [KcT.k(bq)])
        norm_T(XS, XSK, 0, hT, 0, TS)
        hk = hT.k((0, 8))
        for c in range(2):
            wv, wk = ws.get("wqkv", wqkv, 0, 8, c * 512, 512)
            for hl in range(8):
                h = c * 8 + hl
                pb = ps_next((0, 1))
                for kt in range(8):
                    mm(PS[pb][0:64, 0:TS], wv[:, kt, hl * 64:(hl + 1) * 64], hT[:, kt, 0:TS], kt == 0, kt == 7, [wk, hk], [PK[pb]], kt == 7)
                act(QT[:, h, :], PS[pb][0:64, 0:TS], AF.Identity, [PK[pb]], [QT.k(h)], scale=0.125)
        wv, wk = ws.get("wqkv", wqkv, 0, 8, 1024, 512)
        for g in range(NKV):
            pb = ps_next((0, 1))
            for kt in range(8):
                mm(PS[pb][0:64, 0:TS], wv[:, kt, g * 64:(g + 1) * 64], hT[:, kt, 0:TS], kt == 0, kt == 7, [wk, hk], [PK[pb]], kt == 7)
            acopy(KTs[:, g, :], PS[pb][0:64, 0:TS], [PK[pb]], [KTs.k(g)])
        pb = ps_next((2, 3))
        for kt in range(8):
            mm(PS[pb][0:TS, 0:512], hT[:, kt, 0:TS], wv[:, kt, :], kt == 0, kt == 7, [wk, hk], [PK[pb]], kt == 7)
        acopy(kvs[:], PS[pb][0:TS, 0:512], [PK[pb]], [kvs.k()])
        acopy(Vs[:, :, 0:64], PS[pb][0:TS, 256:512].rearrange("p (g d) -> p g d", g=4), [PK[pb]], [Vs.k()])
        for t in range(4):
            dma_out(o_sk[:, 124 + t, :], kvs[16 * t:16 * t + 16, 0:256], [kvs.k()])
            dma_out(o_sv[:, 124 + t, :], kvs[16 * t:16 * t + 16, 256:512], [kvs.k()])
        pb = ps_next((2, 3))
        mm(PS[pb][0:TS, 0:64].rearrange("p (h t) -> p h t", h=16), sel[:], BT[1][0:4, :, 0:4], True, True, [sel.k(), BT[1].k()], [PK[pb]], True)
        cp("dve", BTos[:].rearrange("p h (t b) -> p h t b", t=4),
           PS[pb][0:TS, 0:64].rearrange("p (h t) -> p h t", h=16).unsqueeze(3).to_broadcast([TS, NH, 4, 16]), [PK[pb]], [BTos.k()])
        for h in range(NH):
            tt("dve", BTos[:, h, :].rearrange("p (t b) -> p t b", t=4), BTos[:, h, :].rearrange("p (t b) -> p t b", t=4),
               samem[:].unsqueeze(1).to_broadcast([TS, 4, 16]), ALU.add, [BTos.k(h), samem.k()], [BTos.k(h)])
        for g in range(NKV):
            stq = {}

            def q0(bq, g=g):
                pb = ps_next((0, 1, 2, 3))
                stq[bq] = [pb, None]
                mm(PS[pb][:, 0:256].rearrange("p (a q) -> p a q", a=4), KcT[:, bq, g, :], QT[:, 4 * g:4 * g + 4, :], True, False,
                   [KcT.k(bq), QT.k((4 * g, 4 * g + 4))], [PK[pb]], False)
                mm(PS[pb][:, 0:256], ones_bf[0:1, :], mrow[0:1, bq * 256:(bq + 1) * 256], False, True, [ones_bf.k(), mrow.k()], [PK[pb]], True)

            def q1(bq, g=g):
                pb = stq[bq][0]
                sp_ = spb.next()
                stq[bq][1] = sp_
                tt("dve", sp_[:].rearrange("p (h t b) -> p h t b", h=4, t=4), PS[pb][:, 0:256].rearrange("p (h t b) -> p h t b", h=4, t=4),
                   BT[0][:, 4 * g:4 * g + 4, 0:4].unsqueeze(3).to_broadcast([128, 4, 4, 16]), ALU.add, [PK[pb], BT[0].k()], [sp_.k()])

            def q2(bq, g=g):
                sp_ = stq[bq][1]
                act(PTc[:, bq].rearrange("p h q -> p (h q)"), sp_[:], AF.Exp, [sp_.k()], [PTc.k(bq)])

            pipeline(list(range(SB_PER_CORE)), [q0, q1, q2])
            pb = ps_next((0, 1, 2, 3))
            mm(PS[pb][0:TS, 0:256].rearrange("p (a q) -> p a q", a=4), KTs[:, g, :], QT[:, 4 * g:4 * g + 4, :], True, True,
               [KTs.k(g), QT.k((4 * g, 4 * g + 4))], [PK[pb]], True)
            sp_ = spb.next()
            tt("dve", sp_[0:TS, :], PS[pb][0:TS, 0:256], BTos[:, 4 * g:4 * g + 4, :].rearrange("p h q -> p (h q)"), ALU.add,
               [PK[pb], BTos.k()], [sp_.k()])
            act(PTo[:].rearrange("p h q -> p (h q)"), sp_[0:TS, :], AF.Exp, [sp_.k()], [PTo.k()])
            po = ps_next((4, 5))
            for hl in range(4):
                for bq in range(SB_PER_CORE):
                    mm(PS[po][0:TS, hl * 65:(hl + 1) * 65], PTc[:, bq, hl, :], Vc[:, bq, g, 0:65], bq == 0, False, [PTc.k(bq), Vc.k(bq)], [PK[po]], False)
                mm(PS[po][0:TS, hl * 65:(hl + 1) * 65], PTo[:, hl, :], Vs[:, g, 0:65], False, True, [PTo.k(), Vs.k()], [PK[po]], hl == 3)
            pov = PS[po][0:TS, 0:260].rearrange("p (h e) -> p h e", h=4)
            dn = den[:, g, 0:4]
            tt("dve", dn, pov[:, :, 64], esink[0:TS, 4 * g:4 * g + 4], ALU.add, [PK[po], esink.k()], [den.k(g)])
            recip(dn, dn, [den.k(g)], [den.k(g)])
            tt("dve", Otok[:, 4 * g:4 * g + 4, :], pov[:, :, 0:64], dn.unsqueeze(2).to_broadcast([TS, 4, 64]), ALU.mult,
               [PK[po], den.k(g)], [Otok.k((4 * g, 4 * g + 4))])
        of = Otok[:].rearrange("p h d -> p (h d)")
        for half in range(2):
            pb = ps_next((6, 7))
            for q in range(4):
                kt = half * 4 + q
                trb(PSB[pb][:, q * 128:q * 128 + TS], of[:, kt * 128:(kt + 1) * 128], ident_bf[0:TS, 0:TS], [Otok.k()], [PK[pb]], q == 3)
            acopy(OT[:, half * 4:half * 4 + 4, :], PSB[pb][:, 0:512].rearrange("p (a q) -> p a q", a=4)[:, :, 0:TS],
                  [PK[pb]], [OT.k((half * 4, half * 4 + 4))])
        proj_tm_acc("wo", wo, 8, 4, OT, lambda kt: OT.k(kt), [(XS, XSK)], TS)
        ar.reset(m)

    def ffn_sample(l):
        m = ar.mark()
        Uh = A("Uh", [128, 44, 32], F32)
        stg = Rot([A("stg", [32, 512], F32) for _ in range(2)])
        for c in range(11):
            st = stg.next()
            dma_in(st[:], sffn_in[l, :, c * 512:(c + 1) * 512], [st.k()])
            pb = ps_next((4, 5))
            for q in range(4):
                tr(PS[pb][:, q * 32:(q + 1) * 32], st[:, q * 128:(q + 1) * 128], ident[0:32, 0:32], [st.k()], [PK[pb]], q == 3)
            acopy(Uh[:, c * 4:(c + 1) * 4, :], PS[pb][:, 0:128].rearrange("p (a q) -> p a q", a=4), [PK[pb]], [Uh.k((c * 4, c * 4 + 4))])
        ffn(l, [(XS, XSK)], TS, TS, 16, lambda j: (Uh[:, j, :], Uh.k(j)), "s", True)
        ar.reset(m)

    def ssd_sample():
        m = ar.mark()
        R = TS
        zs = A("zss", [R, DIN], BF16)
        xs_tok = A("xs_toks", [R, DIN], BF16)
        BcT = A("BcTs", [128, SGR, R], BF16)
        CT = A("CTs", [128, SGR, R], BF16)
        Btok = A("Btoks", [R, SGR * 128], BF16)
        dts = A("dtss", [R, 32], F32)
        Uh3 = A("Uh3", [128, 32, 48], F32)
        s_tri = A("s_tri", [R, R], F32)
        s_ugt = A("s_ugt", [R, R], F32)
        s_same = A("s_same", [R, R], F32)
        seqsel = A("seqsel", [R, 16], F32)
        seqrow = A("seqrow", [128, 16, R], F32)
        rep = A("rep", [32, 16, 128], F32)
        cd_col = A("cd_col", [128, 16, 16], F32)
        cdT = A("cdT", [32, 16], F32)
        xw_all = A("xw_all", [R, DIN], BF16)
        sx = A("sx", [R, 8, 32], F32)
        yoff = A("yoff", [R, DIN], F32)
        for name, t_ in (("s_tri", s_tri), ("s_ugt", s_ugt), ("s_same", s_same), ("s_seqsel", seqsel)):
            dma_in(t_[:], cin[name], [t_.k()])
        dma_in(seqrow[:].rearrange("p b q -> p (b q)"), cin["s_seqrow"], [seqrow.k()])
        dma_in(rep[:].rearrange("p i m -> p (i m)"), cin["s_rep"], [rep.k()])
        m2 = ar.mark()
        stg = Rot([A("stg3", [48, 512], F32) for _ in range(2)])
        U3 = Rot([A("U3s", [128, 48 + R], F32) for _ in range(5)])
        tmps = Rot([A("sts", [128, R], F32) for _ in range(5)])
        fms = Rot([A("fms", [128, R], BF16) for _ in range(3)])
        xbo = Rot([A("xbo", [R, 512], F32) for _ in range(2)])
        for c in range(8):
            st = stg.next()
            dma_in(st[:], sconv_in[:, c * 512:(c + 1) * 512], [st.k()])
            pb = ps_next((4, 5))
            for q in range(4):
                tr(PS[pb][:, q * 48:(q + 1) * 48], st[:, q * 128:(q + 1) * 128], ident[0:48, 0:48], [st.k()], [PK[pb]], q == 3)
            acopy(Uh3[:, c * 4:(c + 1) * 4, :], PS[pb][:, 0:192].rearrange("p (a q) -> p a q", a=4), [PK[pb]], [Uh3.k((c * 4, c * 4 + 4))])
        norm_T(XS, XSK, 2, hT, 0, R)
        hk = hT.k((0, 8))
        for c in range(4):
            wv, wk = ws.get("w_in", w_in, 0, 8, c * 512, 512)
            pb = ps_next((0, 1, 2, 3))
            for kt in range(8):
                mm(PS[pb][0:R, 0:512], hT[:, kt, 0:R], wv[:, kt, :], kt == 0, kt == 7, [wk, hk], [PK[pb]], kt == 7)
            act(zs[:, c * 512:(c + 1) * 512], PS[pb][0:R, 0:512], AF.Silu, [PK[pb]], [zs.k((c * 512, c * 512 + 512))])
        stx = {}

        def x0(j):
            c, q = divmod(j, 4)
            if q == 0:
                stx["w"] = ws.get("w_in", w_in, 0, 8, 2048 + c * 512, 512)
            wv, wk = stx["w"]
            pb = ps_next((0, 1, 2, 3))
            stx[j] = {"pb": pb}
            for kt in range(8):
                mm(PS[pb][:, 0:R], wv[:, kt, q * 128:(q + 1) * 128], hT[:, kt, 0:R], kt == 0, kt == 7, [wk, hk], [PK[pb]], kt == 7)
            if q == 3:
                pb2 = ps_next((6, 7))
                for kt in range(8):
                    mm(PS[pb2][0:R, 0:512], hT[:, kt, 0:R], wv[:, kt, :], kt == 0, kt == 7, [wk, hk], [PK[pb2]], kt == 7)
                xo = xbo.next()
                acopy(xo[:], PS[pb2][0:R, 0:512], [PK[pb2]], [xo.k()])
                dma_out(o_sconv[:, c * 512:(c + 1) * 512], xo[16:64, :], [xo.k()])

        def x1(j):
            U = U3.next()
            stx[j]["U"] = U
            acopy(U[:, 48:48 + R], PS[stx[j]["pb"]][:, 0:R], [PK[stx[j]["pb"]]], [U.k()])
            cp("act", U[:, 0:48], Uh3[:, j, :], [Uh3.k(j)], [U.k()])

        def x2(j):
            U = stx[j]["U"]
            t = tmps.next()
            stx[j]["t"] = t
            wj = scw[:, j, :]
            ts2("pool", t[:], U[:, 48:48 + R], wj[:, 3:4], wj[:, 4:5], ALU.mult, ALU.add, [U.k(), scw.k()], [t.k()])

        def mk_tap(k):
            def f(j):
                U, t = stx[j]["U"], stx[j]["t"]
                wj = scw[:, j, :]
                stt("dve", t[:], U[:, 16 * k:16 * k + R], wj[:, k:k + 1], t[:], ALU.mult, ALU.add, [U.k(), scw.k(), t.k()], [t.k()])
            return f

        def x6(j):
            t = stx[j]["t"]
            if j < 24:
                f = fms.next()
                stx[j]["f"] = f
                act(f[:], t[:], AF.Silu, [t.k()], [f.k()])
            else:
                act(CT[:, j - 24, :], t[:], AF.Silu, [t.k()], [CT.k(j - 24)])

        def x7(j):
            if j >= 24:
                return
            f = stx[j]["f"]
            pb2 = ps_next((4, 5))
            stx[j]["pb2"] = pb2
            trb(PSB[pb2][0:R, 0:128], f[:], ident_bf[:], [f.k()], [PK[pb2]], True)
            if j >= 16:
                cp("pool", BcT[:, j - 16, :], f[:], [f.k()], [BcT.k(j - 16)])

        def x8(j):
            if j >= 24:
                return
            pb2 = stx[j]["pb2"]
            if j < 16:
                cp("dve", xs_tok[:, j * 128:(j + 1) * 128], PSB[pb2][0:R, 0:128], [PK[pb2]], [xs_tok.k((j * 128, j * 128 + 128))])
            else:
                g = j - 16
                cp("dve", Btok[:, g * 128:(g + 1) * 128], PSB[pb2][0:R, 0:128], [PK[pb2]], [Btok.k((g * 128, g * 128 + 128))])

        pipeline(list(range(32)), [x0, x1, x2, mk_tap(2), mk_tap(1), mk_tap(0), x6, x7, x8])
        wv, wk = ws.get("w_in", w_in, 0, 8, 6144, 32)
        pb = ps_next((0, 1, 2, 3))
        for kt in range(8):
            mm(PS[pb][0:R, 0:32], hT[:, kt, 0:R], wv[:, kt, :], kt == 0, kt == 7, [wk, hk], [PK[pb]], kt == 7)
        tt("dve", dts[:], PS[pb][0:R, 0:32], dtb[0:R, :], ALU.add, [PK[pb], dtb.k()], [dts.k()])
        act(dts[:], dts[:], AF.Exp, [dts.k()], [dts.k()])
        act(dts[:], dts[:], AF.Ln, [dts.k()], [dts.k()], bias=1.0)
        ar.reset(m2)
        a_, cs, tot, dte = (sx[:, i, :] for i in range(4))
        kx = sx.k()
        tt("dve", a_, dts[:], Abc[0:R, :], ALU.mult, [dts.k(), Abc.k()], [kx])
        pc = ps_next((0, 1, 2, 3))
        mm(PS[pc][0:R, 0:32], s_tri[:], a_, True, True, [s_tri.k(), kx], [PK[pc]], False)
        mm(PS[pc][0:R, 32:64], s_same[:], a_, True, True, [s_same.k(), kx], [PK[pc]], False)
        mm(PS[pc][0:32, 64:80], a_, seqsel[:], True, True, [kx, seqsel.k()], [PK[pc]], True)
        cp("dve", sx[:, 1:3, :].rearrange("p a b -> p (a b)"), PS[pc][0:R, 0:64], [PK[pc]], [kx])
        cp("dve", cdT[:], PS[pc][0:32, 64:80], [PK[pc]], [cdT.k()])
        act(cdT[:], cdT[:], AF.Exp, [cdT.k()], [cdT.k()])
        tt("dve", dte, tot, cs, ALU.subtract, [kx], [kx])
        act(dte, dte, AF.Exp, [kx], [kx])
        tt("dve", dte, dte, dts[:], ALU.mult, [kx, dts.k()], [kx])
        tt("dve", xw_all[:].rearrange("p (h d) -> p h d", h=SH), xs_tok[:].rearrange("p (h d) -> p h d", h=SH),
           dte.unsqueeze(2).to_broadcast([R, SH, HD]), ALU.mult, [xs_tok.k(), kx], [xw_all.k()])
        pc = ps_next((0, 1, 2, 3))
        for it in range(16):
            mm(PS[pc][:, it * 16:(it + 1) * 16], rep[:, it, :], cdT[:], True, True, [rep.k(), cdT.k()], [PK[pc]], it == 15)
        cp("dve", cd_col[:].rearrange("p i b -> p (i b)"), PS[pc][:, 0:256], [PK[pc]], [cd_col.k()])
        m3 = ar.mark()
        h0n = [A("h0n", [128, 16, 128], F32) for _ in range(3)]
        h0T = [A("h0T", [128, DIN], BF16) for _ in range(2)]
        hnew = [A("hnew", [128, 16, 128], F32) for _ in range(2)]
        CmT = [A("CmT", [128, SGR, R], BF16) for _ in range(3)]
        Bm = [A("Bm", [R, SGR * 128], BF16) for _ in range(3)]

        def b0(bq):
            hn, cm, bm = h0n[bq % 3], CmT[bq % 3], Bm[bq % 3]
            dma_in(hn[:], sssm_in[bq].rearrange("(i r) n -> r i n", r=128), [hn.k()])
            tt("pool", cm[:], CT[:], seqrow[:, bq, :].unsqueeze(1).to_broadcast([128, SGR, R]), ALU.mult, [CT.k(), seqrow.k()], [cm.k()])
            ts1("dve", bm[:], Btok[:], seqsel[:, bq:bq + 1], ALU.mult, [Btok.k(), seqsel.k()], [bm.k()])

        def b1(bq):
            hn, ht = h0n[bq % 3], h0T[bq % 2]
            for q4 in range(4):
                pb = ps_next((0, 1, 2, 3))
                for q in range(4):
                    tr(PS[pb][:, q * 128:(q + 1) * 128], hn[:, q4 * 4 + q, :], ident[:], [hn.k()], [PK[pb]], q == 3)
                acopy(ht[:, q4 * 512:(q4 + 1) * 512], PS[pb][:, 0:512], [PK[pb]], [ht.k((q4 * 512, q4 * 512 + 512))])

        def b2(bq):
            hn, ht, hw, cm, bm = h0n[bq % 3], h0T[bq % 2], hnew[bq % 2], CmT[bq % 3], Bm[bq % 3]
            for g in range(SGR):
                mm(PS[4 + g // 2][0:R, (g % 2) * 256:(g % 2) * 256 + 256], cm[:, g, :], ht[:, g * 256:(g + 1) * 256], bq == 0 and g % 2 == 0,
                   bq == SB_PER_CORE - 1 and g % 2 == 1, [cm.k(g), ht.k((g * 256, g * 256 + 256))], [PK[4 + g // 2]], g % 2 == 1)
            for q4 in range(4):
                pb = ps_next((0, 1, 2, 3))
                for q in range(4):
                    it = q4 * 4 + q
                    mm(PS[pb][:, q * 128:(q + 1) * 128], xw_all[:, it * 128:(it + 1) * 128], bm[:, (it // 2) * 128:(it // 2 + 1) * 128], True, True,
                       [xw_all.k(), bm.k()], [PK[pb]], q == 3)
                for q in range(4):
                    it = q4 * 4 + q
                    stt("dve", hw[:, it, :], hn[:, it, :], cd_col[:, it, bq:bq + 1], PS[pb][:, q * 128:(q + 1) * 128], ALU.mult, ALU.add,
                        [hn.k(it), cd_col.k(), PK[pb]], [hw.k(it)])

        def b3(bq):
            hw = hnew[bq % 2]
            dma_out(o_sssm[bq].rearrange("(i r) n -> r i n", r=128), hw[:], [hw.k()])

        pipeline(list(range(SB_PER_CORE)), [b0, b1, b2, b3])
        for q4 in range(4):
            acopy(yoff[:, q4 * 512:(q4 + 1) * 512], PS[4 + q4][0:R, 0:512], [PK[4 + q4]], [yoff.k((q4 * 512, q4 * 512 + 512))])
        ar.reset(m3)
        def yoff_sample(half, pyo):
            c0 = half * 1024
            return [yoff[:, c0:c0 + 512], yoff[:, c0 + 512:c0 + 1024]], [yoff.k(), yoff.k()]

        scan_group(R, [dict(cb=slice(0, R), xs=xs_tok[:], xs_key=xs_tok.k(), zs=zs[:], zs_key=zs.k(), dt=dts[:], dt_key=dts.k(),
                            yoff=yoff_sample, state_mm=None, state_upd=None)], BcT, CT, s_tri, s_ugt, s_same)
        proj_tm_acc("w_out", w_out, 16, 4, hT, lambda kt: hT.k(kt), [(XS, XSK)], R)
        ar.reset(m)

    def dbg(name, ap, key):
        if name in dbg_out:
            dma_out(dbg_out[name], ap, [key])

    for gi in range(NG):
        if stop == "init":
            break
        xblocks = [(xg[:, b, :], xg.k(b)) for b in range(NB)]
        try:
            attn_prompt(gi)
        except _Stop:
            break
        if stop == "attn":
            break
        if gi == 0:
            dbg("x_attn", xg[:].rearrange("p b c -> p (b c)"), xg.k())
        ffn(0, xblocks, T, 128, 1, lambda j: (fhist[:, 0, j, :], fhist.k(0, j)), "p", gi == NG - 1)
        if gi == 0:
            dbg("x_ffn0", xg[:].rearrange("p b c -> p (b c)"), xg.k())
        if stop == "ffn0":
            break
        ssd_prompt(gi)
        if stop == "ssd":
            break
        if gi == 0:
            dbg("x_ssd", xg[:].rearrange("p b c -> p (b c)"), xg.k())
        ffn(1, xblocks, T, 128, 1, lambda j: (fhist[:, 1, j, :], fhist.k(1, j)), "p", gi == NG - 1)
        final_out(xblocks, [o_y[(gi * NB + b) * 128:(gi * NB + b + 1) * 128, :] for b in range(NB)], 128)

    if do_sample and stop is None:
        attn_sample()
        dbg("xs_attn", XS, XSK)
        ffn_sample(0)
        dbg("xs_ffn0", XS, XSK)
        ssd_sample()
        dbg("xs_ssd", XS, XSK)
        ffn_sample(1)
        final_out([(XS, XSK)], [o_ys], TS)

    P.finish("sp")
    with contextlib.ExitStack() as ctx:
        P.emit(ctx)
    return nc, ws.req, ar.hi


_CACHE = {}


def _get_program(**kw):
    key = tuple(sorted((k, str(v)) for k, v in kw.items()))
    if key not in _CACHE:
        _, req, _ = build_program(order=None, **kw)
        nc, req2, hi = build_program(order=req, **kw)
        assert [r[1:] for r in req] == [r[1:] for r in req2]
        _CACHE[key] = nc
    return _CACHE[key]


def _core_inputs(inp, c, consts):
    f = np.ascontiguousarray
    s = c % 4
    b0 = c * SB_PER_CORE
    bs = slice(b0, b0 + SB_PER_CORE)
    m = {}
    m["xp"] = f(inp["x_prompt"][s])
    m["xs"] = f(np.transpose(inp["x_sample"][bs], (1, 0, 2)).reshape(TS, D))
    m["ck"] = f(inp["cache_k_win"][0, bs].reshape(SB_PER_CORE, 128, 256))
    m["cv"] = f(inp["cache_v_win"][0, bs].reshape(SB_PER_CORE, 128, 256))
    m["sconv"] = f(np.transpose(inp["state_ssm_conv"][0, bs], (1, 0, 2)).reshape(3 * SB_PER_CORE, CONVD))
    m["sssm"] = f(inp["state_ssm"][0, bs].reshape(SB_PER_CORE, 2048, 128))
    m["sffn"] = f(np.transpose(inp["state_ffn_conv"][:, bs], (0, 2, 1, 3)).reshape(2, 2 * SB_PER_CORE, NUP))
    m["table"] = f(inp["rel_bias_table"])
    m["wqkv"] = f(inp["attn_wqkv"][0])
    m["wo"] = f(inp["attn_wo"][0])
    m["w_in"] = f(inp["ssm_w_in"][0])
    m["w_out"] = f(inp["ssm_w_out"][0])
    for l in range(2):
        m[f"w_up{l}"] = f(inp["ffn_w_up"][l])
        m[f"w_dn{l}"] = f(inp["ffn_w_down"][l])
    g = np.stack([inp["norm_mix"][0], inp["norm_ffn"][0], inp["norm_mix"][1], inp["norm_ffn"][1]])
    m["gT"] = f(np.transpose(g.reshape(4, KT, 128), (2, 0, 1)).reshape(128, 4 * KT))
    m["gfin"] = f(inp["norm_final"].reshape(1, D))
    m["sinks"] = f(inp["attn_sinks"].reshape(1, 16))
    fw = np.concatenate([inp["ffn_conv_w"], inp["ffn_conv_b"][:, None, :]], axis=1)
    m["fcw"] = f(np.transpose(fw.reshape(2, 4, 44, 128), (3, 0, 2, 1)).reshape(128, 2 * 44 * 4))
    sw = np.concatenate([inp["ssm_conv_w"][0], inp["ssm_conv_b"][0][None, :]], axis=0)
    m["scw"] = f(np.transpose(sw.reshape(5, 32, 128), (2, 1, 0)).reshape(128, 32 * 5))
    m["ssmv"] = f(np.concatenate([inp["ssm_dt_bias"][0], inp["ssm_A_log"][0], inp["ssm_D"][0]]).reshape(1, 96))
    m["nwT"] = f(inp["ssm_norm"][0].reshape(16, 128).T)
    for k, v in consts.items():
        m["c_" + k] = v
    return {k: np.asarray(v, dtype=np.float32) for k, v in m.items()}


def kernel(**inputs):
    inp = {k: np.asarray(v) for k, v in inputs.items()}
    nc = _get_program(NG=8, do_sample=True)
    consts = host_consts()
    in_maps = [_core_inputs(inp, c, consts) for c in range(NCORES)]
    res = run_bass_kernel_spmd(nc, in_maps, core_ids=list(range(NCORES))).results
    B = 4
    y_prompt = np.stack([res[s]["o_y"] for s in range(B)])
    p_k = np.stack([res[s]["o_pk"].reshape(128, 4, 64) for s in range(B)])[None]
    p_v = np.stack([res[s]["o_pv"].reshape(128, 4, 64) for s in range(B)])[None]
    p_conv = np.stack([res[s]["o_pconv"] for s in range(B)])[None]
    p_ssm = np.stack([res[s]["o_pssm"].reshape(32, 64, 128) for s in range(B)])[None]
    p_ffn = np.stack([res[s]["o_pffn"] for s in range(B)], axis=1)
    cat = lambda fn: np.concatenate([fn(res[c]) for c in range(NCORES)], axis=0)
    y_sample = cat(lambda r: np.transpose(r["o_ys"].reshape(4, SB_PER_CORE, D), (1, 0, 2)))
    s_k = cat(lambda r: r["o_sk"].reshape(SB_PER_CORE, 128, 4, 64))[None]
    s_v = cat(lambda r: r["o_sv"].reshape(SB_PER_CORE, 128, 4, 64))[None]
    s_conv = cat(lambda r: np.transpose(r["o_sconv"].reshape(3, SB_PER_CORE, CONVD), (1, 0, 2)))[None]
    s_ssm = cat(lambda r: r["o_sssm"].reshape(SB_PER_CORE, 32, 64, 128))[None]
    s_ffn = np.concatenate([np.transpose(res[c]["o_sffn"].reshape(2, 2, SB_PER_CORE, NUP), (0, 2, 1, 3)) for c in range(NCORES)], axis=1)
    outs = (y_prompt, y_sample, p_k, p_v, p_conv, p_ssm, p_ffn, s_k, s_v, s_conv, s_ssm, s_ffn)
    return tuple(np.ascontiguousarray(o, dtype=np.float32) for o in outs)
```

```python
import contextlib
import math
import numpy as np
import concourse.bass as bass
import concourse.mybir as mybir
from concourse.bass_utils import run_bass_kernel_spmd

F32 = mybir.dt.float32
BF16 = mybir.dt.bfloat16
AF = mybir.ActivationFunctionType
ALU = mybir.AluOpType

D = 1024
KT = 8
SEQ = 4096
DFF = 2816
NUP = 5632
DIN = 2048
CONVD = 4096
SIN = 6176
NH = 16
NKV = 4
HD = 64
SH = 32
SGR = 8
NST = 128
EPS = 1e-6
NEG = -30000.0
NCORES = 8
SB_PER_CORE = 16
TS = 64

ENGS = ("pe", "act", "dve", "pool", "sp")
EPOCH = 30000
GRAN = 64


class Prog:
    def __init__(self, nc, n_dma_sems=60, same_engine_sync=True):
        self.nc = nc
        self.ops = {e: [] for e in ENGS}
        self.count = {e: 0 for e in ENGS}
        self.seen = {e: {} for e in ENGS}
        self.lw = {}
        self.rd = {}
        self.n_dma_sems = n_dma_sems
        self.dma_rr = [0, 0]
        self.dma_val = [0] * n_dma_sems
        self.same_engine_sync = same_engine_sync
        self.max_epoch = {e: 0 for e in ENGS}
        self.pend_r = {e: [] for e in ENGS}
        self.pend_w = {e: [] for e in ENGS}

    @staticmethod
    def _cells(keys):
        out = []
        for k in keys:
            if isinstance(k, str):
                out.append(k)
            else:
                lo, hi = k
                out.extend(range(lo // GRAN, (hi - 1) // GRAN + 1))
        return out

    def _need(self, eng, dep):
        semkey, val, peng = dep
        if peng == eng and semkey[0] == "e" and (eng == "pe" or not self.same_engine_sync):
            return
        cur = self.seen[eng].get(semkey, 0)
        if cur >= val:
            return
        self.seen[eng][semkey] = val
        self.ops[eng].append(("wait", (semkey, val)))

    def _deps(self, eng, rc, wc):
        lw, rd = self.lw, self.rd
        for c in rc:
            d = lw.get(c)
            if d is not None:
                self._need(eng, d)
        for c in wc:
            d = lw.get(c)
            if d is not None:
                self._need(eng, d)
            r = rd.get(c)
            if r:
                for semkey, (val, peng) in r.items():
                    self._need(eng, (semkey, val, peng))

    def _commit(self, tok, rc, wc):
        semkey, val, eng = tok
        for c in rc:
            self.rd.setdefault(c, {})[semkey] = (val, eng)
        for c in wc:
            self.lw[c] = tok
            self.rd[c] = {}

    def op(self, eng, fn, reads=(), writes=()):
        rc, wc = self._cells(reads), self._cells(writes)
        self._deps(eng, rc, wc)
        idx = self.count[eng]
        self.count[eng] += 1
        ep = idx // EPOCH
        self.max_epoch[eng] = max(self.max_epoch[eng], ep)
        semkey = ("e", eng, ep)
        self.ops[eng].append(("op", (fn, semkey)))
        self._commit((semkey, idx % EPOCH + 1, eng), rc, wc)

    def raw(self, eng, fn, reads=(), writes=()):
        rc, wc = self._cells(reads), self._cells(writes)
        self._deps(eng, rc, wc)
        idx = self.count[eng]
        ep = idx // EPOCH
        self.max_epoch[eng] = max(self.max_epoch[eng], ep)
        self._commit((("e", eng, ep), idx % EPOCH + 1, eng), rc, wc)
        self.ops[eng].append(("raw", fn))

    def dma(self, eng, fns, reads=(), writes=(), inc=16):
        rc, wc = self._cells(reads), self._cells(writes)
        self._deps(eng, rc, wc)
        nhw = 24
        if eng == "pool":
            s = nhw + self.dma_rr[1] % (self.n_dma_sems - nhw)
            self.dma_rr[1] += 1
        else:
            s = self.dma_rr[0] % nhw
            self.dma_rr[0] += 1
        semkey = ("d", s)
        prev = self.dma_val[s]
        if prev > 0:
            self._need(eng, (semkey, prev, "dma"))
        newv = prev + inc * len(fns)
        self.dma_val[s] = newv
        self.ops[eng].append(("dma", (fns, semkey, inc)))
        self._commit((semkey, newv, "dma"), rc, wc)

    def finish(self, eng="sp"):
        for s in range(self.n_dma_sems):
            if self.dma_val[s] > 0:
                self._need(eng, (("d", s), self.dma_val[s], "dma"))
        for e in ENGS:
            if self.count[e] > 0:
                idx = self.count[e] - 1
                self._need(eng, (("e", e, idx // EPOCH), idx % EPOCH + 1, e))

    def emit(self, ctx):
        nc = self.nc
        sems = {}
        for e in ENGS:
            if self.count[e] > 0:
                for ep in range(self.max_epoch[e] + 1):
                    sems[("e", e, ep)] = ctx.enter_context(nc.semaphore(f"s_{e}_{ep}"))
        for s in range(self.n_dma_sems):
            if self.dma_val[s] > 0:
                sems[("d", s)] = ctx.enter_context(nc.semaphore(f"s_dma_{s}"))
        block = ctx.enter_context(nc.Block())
        ops = self.ops

        def run(e, lst):
            for kind, pl in lst:
                if kind == "wait":
                    e.wait_ge(sems[pl[0]], pl[1])
                elif kind == "op":
                    pl[0](e).then_inc(sems[pl[1]], 1)
                elif kind == "raw":
                    pl(e)
                else:
                    fns, semkey, inc = pl
                    for fn in fns:
                        fn(e).then_inc(sems[semkey], inc)

        @block.tensor
        def _(e):
            run(e, ops["pe"])

        @block.scalar
        def _(e):
            run(e, ops["act"])

        @block.vector
        def _(e):
            run(e, ops["dve"])

        @block.gpsimd
        def _(e):
            run(e, ops["pool"])

        @block.sync
        def _(e):
            run(e, ops["sp"])


class SB:
    def __init__(self, nc, name, shape, dtype, off):
        self.t = nc.alloc_sbuf_tensor_at(name, list(shape), dtype, offset=off)
        self.off = off
        self.shape = list(shape)
        self.esz = 4 if dtype == F32 else 2
        self.nbytes = int(np.prod(shape[1:])) * self.esz

    def __getitem__(self, k):
        return self.t[k]

    def k(self, *idx):
        lo, span = 0, int(np.prod(self.shape[1:]))
        dims = self.shape[1:]
        for d, i in zip(dims, idx):
            span //= d
            if isinstance(i, tuple):
                lo += i[0] * span
                n = i[1] - i[0]
                return (self.off + lo * self.esz, self.off + (lo + n * span) * self.esz)
            lo += i * span
        return (self.off + lo * self.esz, self.off + (lo + span) * self.esz)


class Arena:
    def __init__(self, nc, base, limit):
        self.nc, self.base, self.limit, self.cur, self.n = nc, base, limit, base, 0
        self.hi = base

    def alloc(self, name, shape, dtype):
        esz = 4 if dtype == F32 else 2
        nbytes = int(np.prod(shape[1:])) * esz
        off = (self.cur + 127) // 128 * 128
        assert off + nbytes <= self.limit, f"SBUF overflow allocating {name}: {off}+{nbytes} > {self.limit}"
        self.cur = off + nbytes
        self.hi = max(self.hi, self.cur)
        self.n += 1
        return SB(self.nc, f"{name}_{self.n}", shape, dtype, off)

    def mark(self):
        return self.cur

    def reset(self, m):
        self.cur = m


def _t5_bucket_np(n):
    n = np.maximum(n, 0)
    nf = np.maximum(n, 1).astype(np.float32)
    v = (np.log(nf / np.float32(16)) / np.float32(math.log(128 / 16)) * np.float32(16)).astype(np.float32)
    large = np.minimum(16 + v.astype(np.int32), 31)
    return np.where(n < 16, n, large)


def host_consts():
    c = {}
    i = np.arange(128)
    c["ident"] = np.eye(128, dtype=np.float32)
    c["tri"] = (i[:, None] <= i[None, :]).astype(np.float32)
    c["ugt"] = (i[:, None] > i[None, :]).astype(np.float32)
    c["ones"] = np.ones((128, 128), np.float32)
    c["maskp"] = np.where(i[:, None] > i[None, :], 0.0, NEG).astype(np.float32)
    c["masko"] = np.where(i[:, None] <= i[None, :], 0.0, NEG).astype(np.float32)
    d = np.arange(256)
    bk = _t5_bucket_np(d)
    G = np.zeros((32, 384), np.float32)
    G[bk, 255 - d] = 1.0
    c["grev"] = G
    j = np.arange(64)
    tj, bj = j // 16, j % 16
    same = bj[:, None] == bj[None, :]
    c["s_tri"] = (same & (tj[:, None] <= tj[None, :])).astype(np.float32)
    c["s_ugt"] = (same & (tj[:, None] > tj[None, :])).astype(np.float32)
    c["s_same"] = same.astype(np.float32)
    c["s_seqsel"] = (bj[:, None] == np.arange(16)[None, :]).astype(np.float32)
    c["s_samem"] = np.where(bj[:, None] == np.arange(16)[None, :], 0.0, NEG).astype(np.float32)
    mrow = np.where(np.arange(16)[:, None] == bj[None, :], 0.0, NEG).astype(np.float32)
    c["s_mrow"] = np.tile(mrow[:, None, :], (1, 4, 1)).reshape(1, 16 * 256)
    c["s_seqrow"] = np.tile((np.arange(16)[:, None] == bj[None, :]).astype(np.float32).reshape(1, 1024), (128, 1))
    sel = np.zeros((4, 64), np.float32)
    sel[tj, j] = 1.0
    c["s_sel"] = sel
    R = np.zeros((32, 16, 128), np.float32)
    for it in range(16):
        for m in range(128):
            R[2 * it + m // 64, it, m] = 1.0
    c["s_rep"] = R.reshape(32, 2048)
    return c


CONST_SHAPES = {k: v.shape for k, v in host_consts().items()}


class _Stop(Exception):
    pass


def pipeline(tasks, stages):
    n, k = len(tasks), len(stages)
    for step in range(n + k - 1):
        for s in range(k - 1, -1, -1):
            i = step - s
            if 0 <= i < n:
                stages[s](tasks[i])


class Rot:
    def __init__(self, items):
        self.items, self.i = items, 0

    def next(self):
        x = self.items[self.i % len(self.items)]
        self.i += 1
        return x


def build_program(order=None, NG=8, do_sample=True, debug=(), NB=4, NSLOT=4, same_engine_sync=True, stop=None, wreuse=True):
    T = NB * 128
    nc = bass.Bass("TRN2", target_bir_lowering=False)

    def din(name, shape):
        return nc.dram_tensor(name, list(shape), F32, kind="ExternalInput").ap()

    def dout(name, shape):
        return nc.dram_tensor(name, list(shape), F32, kind="ExternalOutput").ap()

    xp = din("xp", [SEQ, D])
    xs_in = din("xs", [TS, D])
    ck_in = din("ck", [SB_PER_CORE, 128, 256])
    cv_in = din("cv", [SB_PER_CORE, 128, 256])
    sconv_in = din("sconv", [3 * SB_PER_CORE, CONVD])
    sssm_in = din("sssm", [SB_PER_CORE, 2048, 128])
    sffn_in = din("sffn", [2, 2 * SB_PER_CORE, NUP])
    table = din("table", [32, 16])
    wqkv = din("wqkv", [D, 1536])
    wo = din("wo", [D, D])
    w_in = din("w_in", [D, SIN])
    w_out = din("w_out", [DIN, D])
    w_up = [din("w_up0", [D, NUP]), din("w_up1", [D, NUP])]
    w_dn = [din("w_dn0", [DFF, D]), din("w_dn1", [DFF, D])]
    gT_in = din("gT", [128, 4 * KT])
    gfin_in = din("gfin", [1, D])
    sinks_in = din("sinks", [1, 16])
    fcw_in = din("fcw", [128, 2 * 44 * 4])
    scw_in = din("scw", [128, 32 * 5])
    ssmv_in = din("ssmv", [1, 96])
    nwT_in = din("nwT", [128, 16])
    cin = {k: din("c_" + k, list(s)) for k, s in CONST_SHAPES.items()}

    o_y = dout("o_y", [SEQ, D])
    o_pk = dout("o_pk", [128, 256])
    o_pv = dout("o_pv", [128, 256])
    o_pconv = dout("o_pconv", [3, CONVD])
    o_pssm = dout("o_pssm", [2048, 128])
    o_pffn = dout("o_pffn", [2, 2, NUP])
    o_ys = dout("o_ys", [TS, D])
    o_sk = dout("o_sk", [SB_PER_CORE, 128, 256])
    o_sv = dout("o_sv", [SB_PER_CORE, 128, 256])
    o_sconv = dout("o_sconv", [3 * SB_PER_CORE, CONVD])
    o_sssm = dout("o_sssm", [SB_PER_CORE, 2048, 128])
    o_sffn = dout("o_sffn", [2, 2 * SB_PER_CORE, NUP])
    dbg_out = {name: dout("dbg_" + name, shape) for name, shape in debug}
    bt_dram = nc.dram_tensor("bt_scr", [2, 128, 2048], F32).ap()

    P = Prog(nc, same_engine_sync=same_engine_sync)
    ar = Arena(nc, 16640, 229000)
    A = ar.alloc

    PS = [nc.alloc_psum_tensor(f"psb{i}", [128, 512], F32) for i in range(8)]
    PK = [f"ps{i}" for i in range(8)]

    xg = A("xg", [128, NB, D], F32)
    hT = A("hT", [128, 16, T], BF16)
    ring = [A(f"ring{i}", [128, 4096], BF16) for i in range(NSLOT)]
    hst = A("hst", [128, SH, HD], F32)
    hst_bf = A("hst_bf", [128, SH, HD], BF16)
    ident = A("ident", [128, 128], F32)
    tri = A("tri", [128, 128], F32)
    ugt = A("ugt", [128, 128], F32)
    ones = A("ones", [128, 128], F32)
    ident_bf = A("ident_bf", [128, 128], BF16)
    ones_bf = A("ones_bf", [128, 128], BF16)
    gbc = A("gbc", [128, D], F32)
    KTb = A("KTb", [64, NKV, 128 + T], BF16)
    Vau = A("Vau", [128, NB + 1, NKV, 66], BF16)
    fhist = A("fhist", [128, 2, 44, 2], F32)
    chist = A("chist", [128, 32, 3], F32)
    gT = A("gT", [128, 4, KT], F32)
    fcw = A("fcw", [128, 2, 44, 4], F32)
    scw = A("scw", [128, 32, 5], F32)
    nwT = A("nwT", [128, 16], F32)
    esink = A("esink", [128, 16], F32)
    dtb = A("dtb", [128, 32], F32)
    Abc = A("Abc", [128, 32], F32)
    Dbc = A("Dbc", [128, 32], F32)
    small = A("small", [128, 16, 16], F32)
    sqs = Rot([A("sq", [128, 256], BF16) for _ in range(2)])
    smi = [0]

    def sm():
        i = smi[0] % 16
        smi[0] += 1
        return small[:, i, :], small.k(i)

    def mm(out, lhsT, rhs, start, stop, reads, writes, track):
        fn = lambda e: e.matmul(out, lhsT, rhs, start=start, stop=stop)
        (P.op if track else P.raw)("pe", fn, reads=reads, writes=writes)

    def tr(out, in_, idn, reads, writes, track):
        fn = lambda e: e.transpose(out, in_, idn)
        (P.op if track else P.raw)("pe", fn, reads=list(reads) + [ident.k()], writes=writes)

    PSB = [p.bitcast(BF16) for p in PS]

    def trb(out, in_, idn, reads, writes, track):
        fn = lambda e: e.transpose(out, in_, idn)
        (P.op if track else P.raw)("pe", fn, reads=list(reads) + [ident_bf.k()], writes=writes)

    def act(out, in_, func, reads, writes, **kw):
        P.op("act", lambda e: e.activation(out, in_, func, **kw), reads=reads, writes=writes)

    def acopy(out, in_, reads, writes):
        P.op("act", lambda e: e.copy(out, in_), reads=reads, writes=writes)

    def cp(eng, out, in_, reads, writes):
        if eng == "act":
            return acopy(out, in_, reads, writes)
        P.op(eng, lambda e: e.tensor_copy(out, in_), reads=reads, writes=writes)

    def tt(eng, out, in0, in1, op, reads, writes):
        P.op(eng, lambda e: e.tensor_tensor(out, in0, in1, op), reads=reads, writes=writes)

    def ts2(eng, out, in0, s1, s2, op0, op1, reads, writes):
        P.op(eng, lambda e: e.tensor_scalar(out, in0, s1, s2, op0=op0, op1=op1), reads=reads, writes=writes)

    def ts1(eng, out, in0, s1, op0, reads, writes):
        P.op(eng, lambda e: e.tensor_single_scalar(out, in0, s1, op0), reads=reads, writes=writes)

    def stt(eng, out, in0, scalar, in1, op0, op1, reads, writes):
        P.op(eng, lambda e: e.scalar_tensor_tensor(out, in0, scalar, in1, op0=op0, op1=op1), reads=reads, writes=writes)

    def recip(out, in_, reads, writes):
        P.op("dve", lambda e: e.reciprocal(out, in_), reads=reads, writes=writes)

    def memset(eng, ap, val, writes):
        P.op(eng, lambda e: e.memset(ap, val), writes=writes)

    def dma_in(dst_ap, src_ap, wkeys, eng="sp", rkeys=()):
        P.dma(eng, [lambda e: e.dma_start(out=dst_ap, in_=src_ap)], reads=rkeys, writes=wkeys)

    def dma_out(dst_ap, src_ap, rkeys, eng="sp", wkeys=()):
        P.dma(eng, [lambda e: e.dma_start(out=dst_ap, in_=src_ap)], reads=rkeys, writes=wkeys)

    ps_rr = {}

    def ps_next(pool):
        i = ps_rr.get(pool, 0)
        ps_rr[pool] = i + 1
        return pool[i % len(pool)]

    class WS:
        def __init__(self):
            self.req = []
            self.issued = 0
            self.i = 0
            self.scr = {}
            self.reuse = wreuse

        def _issue(self, r):
            wname, W, kt0, nkt, c0, ncol = order[r]
            slot = ring[r % NSLOT]
            ck = (wname, kt0, nkt, c0, ncol)
            flat = slot[:, 0:nkt * ncol]
            if ck not in self.scr:
                dst = flat.rearrange("p (k c) -> p k c", k=nkt)
                src = W[kt0 * 128:(kt0 + nkt) * 128, c0:c0 + ncol].rearrange("(k p) c -> p k c", p=128)
                P.dma("pool", [lambda e: e.dma_start(out=dst, in_=src)], writes=[slot.k()])
                if self.reuse:
                    scr = nc.dram_tensor(f"wscr{len(self.scr)}", [128, nkt * ncol], BF16).ap()
                    self.scr[ck] = (scr, f"wscr{len(self.scr)}")
                    P.dma("sp", [lambda e: e.dma_start(out=scr, in_=flat)], reads=[slot.k()], writes=[self.scr[ck][1]])
            else:
                scr, skey = self.scr[ck]
                P.dma("sp", [lambda e: e.dma_start(out=flat, in_=scr)], reads=[skey], writes=[slot.k()])

        def get(self, wname, W, kt0, nkt, c0, ncol):
            self.req.append((wname, kt0, nkt, c0, ncol))
            r = self.i
            self.i += 1
            if order is None:
                slot = ring[0]
            else:
                assert order[r][2:] == (kt0, nkt, c0, ncol), (order[r][2:], (kt0, nkt, c0, ncol))
                while self.issued < min(r + NSLOT, len(order)):
                    self._issue(self.issued)
                    self.issued += 1
                slot = ring[r % NSLOT]
            return slot[:, 0:nkt * ncol].rearrange("p (k c) -> p k c", k=nkt), slot.k()

    wmap = {"wqkv": wqkv, "wo": wo, "w_in": w_in, "w_out": w_out, "w_up0": w_up[0], "w_up1": w_up[1],
            "w_dn0": w_dn[0], "w_dn1": w_dn[1]}
    if order is not None:
        order = [(o[0], wmap[o[0]]) + tuple(o[1:]) for o in order]
    ws = WS()

    dma_in(ident[:], cin["ident"], [ident.k()])
    dma_in(tri[:], cin["tri"], [tri.k()])
    dma_in(ugt[:], cin["ugt"], [ugt.k()])
    dma_in(ones[:], cin["ones"], [ones.k()])
    dma_in(gT[:], gT_in.rearrange("p (i k) -> p i k", i=4), [gT.k()])
    dma_in(fcw[:], fcw_in.rearrange("p (l j w) -> p l j w", l=2, j=44), [fcw.k()])
    dma_in(scw[:], scw_in.rearrange("p (j w) -> p j w", j=32), [scw.k()])
    dma_in(nwT[:], nwT_in, [nwT.k()])
    dma_in(gbc[:], gfin_in[0:1, :].to_broadcast([128, D]), [gbc.k()])
    dma_in(esink[:], sinks_in[0:1, :].to_broadcast([128, 16]), [esink.k()])
    dma_in(dtb[:], ssmv_in[0:1, 0:32].to_broadcast([128, 32]), [dtb.k()])
    dma_in(Abc[:], ssmv_in[0:1, 32:64].to_broadcast([128, 32]), [Abc.k()])
    dma_in(Dbc[:], ssmv_in[0:1, 64:96].to_broadcast([128, 32]), [Dbc.k()])
    cp("dve", ident_bf[:], ident[:], [ident.k()], [ident_bf.k()])
    cp("dve", ones_bf[:], ones[:], [ones.k()], [ones_bf.k()])
    act(esink[:], esink[:], AF.Exp, [esink.k()], [esink.k()])
    act(Abc[:], Abc[:], AF.Exp, [Abc.k()], [Abc.k()])
    ts1("dve", Abc[:], Abc[:], -1.0, ALU.mult, [Abc.k()], [Abc.k()])
    memset("dve", fhist[:], 0.0, [fhist.k()])
    memset("dve", chist[:], 0.0, [chist.k()])
    memset("dve", hst[:], 0.0, [hst.k()])
    memset("dve", hst_bf[:], 0.0, [hst_bf.k()])
    memset("dve", Vau[:], 1.0, [Vau.k()])
    memset("dve", KTb[:], 0.0, [KTb.k()])

    m0 = ar.mark()
    grev = A("grev", [32, 384], F32)
    tab = A("tab", [32, 16], F32)
    mk2 = [A("maskp", [128, 128], F32), A("masko", [128, 128], F32)]
    btt = A("btt", [128, 16, 128], F32)
    dma_in(grev[:], cin["grev"], [grev.k()])
    dma_in(tab[:], table, [tab.k()])
    dma_in(mk2[0][:], cin["maskp"], [mk2[0].k()])
    dma_in(mk2[1][:], cin["masko"], [mk2[1].k()])
    btq = A("btq", [128, 128, 16], F32)
    for which in range(2):
        for q in range(128):
            pb = q // 32
            st_ = (127 - q) if which == 0 else (255 - q)
            mm(PS[pb][:, (q % 32) * 16:(q % 32) * 16 + 16], grev[:, st_:st_ + 128], tab[:, 0:16], True, True,
               [grev.k(), tab.k()], [PK[pb]], q % 32 == 31)
        for pb in range(4):
            tt("dve", btq[:, pb * 32:(pb + 1) * 32, :], PS[pb][:, 0:512].rearrange("p (q h) -> p q h", h=16),
               mk2[which][:, pb * 32:(pb + 1) * 32].unsqueeze(2).to_broadcast([128, 32, 16]), ALU.add,
               [PK[pb], mk2[which].k()], [btq.k((pb * 32, pb * 32 + 32))])
        cp("dve", btt[:], btq[:].rearrange("p q h -> p h q"), [btq.k()], [btt.k()])
        dma_out(bt_dram[which], btt[:].rearrange("p h q -> p (h q)"), [btt.k()], wkeys=["bt_dram"])
    ar.reset(m0)
    phase_mark = ar.mark()

    def norm_T_multi(items, gi_, dst, rows=128):
        st = {}
        mloc = ar.mark()
        xns = Rot([A("xn", [128, D], F32) for _ in range(3)])
        sqn = Rot([A("sqn", [128, D], BF16) for _ in range(2)])

        def s0(it):
            xap, xkey, col0 = it
            sqb = sqn.next()
            s_ap, s_k = sm()
            st[col0] = (s_ap, s_k, xns.next())
            memset("dve", s_ap[0:rows, 0:1], 0.0, [s_k])
            act(sqb[0:rows, :], xap, AF.Square, [xkey], [sqb.k(), s_k], accum_out=s_ap[0:rows, 0:1])

        def s1(it):
            s_ap, s_k, _ = st[it[2]]
            act(s_ap[0:rows, 1:2], s_ap[0:rows, 0:1], AF.Sqrt, [s_k], [s_k], bias=EPS, scale=1.0 / D)

        def s2(it):
            s_ap, s_k, _ = st[it[2]]
            recip(s_ap[0:rows, 2:3], s_ap[0:rows, 1:2], [s_k], [s_k])

        def s3(it):
            xap, xkey, col0 = it
            s_ap, s_k, xnb = st[col0]
            ts1("dve", xnb[0:rows, :], xap, s_ap[0:rows, 2:3], ALU.mult, [xkey, s_k], [xnb.k()])

        def mk_tr(half):
            def f(it):
                xap, xkey, col0 = it
                s_ap, s_k, xnb = st[col0]
                pb = ps_next((4, 5, 6, 7))
                st[(col0, half)] = pb
                for q in range(4):
                    kt = half * 4 + q
                    tr(PS[pb][:, q * 128:q * 128 + rows], xnb[0:rows, kt * 128:(kt + 1) * 128], ident[0:rows, 0:rows],
                       [xnb.k()], [PK[pb]], q == 3)
            return f

        def mk_ev(half):
            def f(it):
                xap, xkey, col0 = it
                pb = st[(col0, half)]
                pv = PS[pb][:, 0:512].rearrange("p (a b) -> p a b", a=4)[:, :, 0:rows]
                tt("dve", dst[:, half * 4:half * 4 + 4, col0:col0 + rows], pv,
                   gT[:, gi_, half * 4:half * 4 + 4].unsqueeze(2).to_broadcast([128, 4, rows]), ALU.mult,
                   [PK[pb], gT.k()], [dst.k((half * 4, half * 4 + 4))])
            return f

        pipeline(items, [s0, s1, s2, s3, mk_tr(0), mk_tr(1), mk_ev(0), mk_ev(1)])
        ar.reset(mloc)

    def norm_T(xap, xkey, gi_, dst, col0, rows=128):
        norm_T_multi([(xap, xkey, col0)], gi_, dst, rows)

    def rstd_of(xap, xkey, rows=128):
        mloc = ar.mark()
        sqb = A("sqf", [128, D], BF16)
        ar.reset(mloc)
        s_ap, s_k = sm()
        act(sqb[0:rows, :], xap, AF.Square, [xkey], [sqb.k(), s_k], accum_out=s_ap[0:rows, 0:1])
        act(s_ap[0:rows, 1:2], s_ap[0:rows, 0:1], AF.Sqrt, [s_k], [s_k], bias=EPS, scale=1.0 / D)
        recip(s_ap[0:rows, 2:3], s_ap[0:rows, 1:2], [s_k], [s_k])
        return s_ap[0:rows, 2:3], s_k

    def resid_add(xblocks, rows):
        for b, (xap, xkey) in enumerate(xblocks):
            for hf in range(2):
                pb = 2 * b + hf
                tt("dve", xap[:, hf * 512:(hf + 1) * 512], xap[:, hf * 512:(hf + 1) * 512], PS[pb][0:rows, 0:512], ALU.add,
                   [PK[pb], xkey], [xkey])

    def proj_tm_acc(wname, W, nkt_total, ktchunk, src, src_key, xblocks, rows):
        nch = (nkt_total + ktchunk - 1) // ktchunk
        for c in range(nch):
            nk = min(ktchunk, nkt_total - c * ktchunk)
            wv, wk = ws.get(wname, W, c * ktchunk, nk, 0, 1024)
            for b in range(len(xblocks)):
                for kk in range(nk):
                    kt = c * ktchunk + kk
                    for hf in range(2):
                        mm(PS[2 * b + hf][0:rows, 0:512], src[:, kt, b * 128:b * 128 + rows], wv[:, kk, hf * 512:(hf + 1) * 512],
                           kt == 0, kt == nkt_total - 1, [wk, src_key(kt)], [PK[2 * b + hf]],
                           kt == nkt_total - 1 or (kk == nk - 1 and hf == 1))
        resid_add(xblocks, rows)

    def attn_prompt(gi):
        m = ar.mark()
        QT = A("QT", [64, NH, T], BF16)
        BT = [A("BTp", [128, NH, 128], F32), A("BTo", [128, NH, 128], F32)]
        PTs = Rot([A("PT", [128, 2, 4, 128], BF16) for _ in range(4)])
        spb = Rot([A("spb", [128, 512], F32) for _ in range(6)])
        Otoks = [A("Otok", [128, NH, HD], BF16) for _ in range(2)]
        OT = A("OT", [128, KT, T], BF16)
        den = A("den", [128, 8, 16], F32)
        kvo = A("kvo", [128, 512], F32)
        for w in range(2):
            dma_in(BT[w][:].rearrange("p h q -> p (h q)"), bt_dram[w], [BT[w].k()], rkeys=["bt_dram"])
        for b in range(NB):
            r0 = (gi * NB + b) * 128
            dma_in(xg[:, b, :], xp[r0:r0 + 128, :], [xg.k(b)])
        if stop == "a_load":
            raise _Stop()
        norm_T_multi([(xg[:, b, :], xg.k(b), b * 128) for b in range(NB)], 0, hT)
        if stop == "a_norm":
            raise _Stop()
        hk = hT.k((0, 8))
        last = gi == NG - 1
        for c in range(2):
            wv, wk = ws.get("wqkv", wqkv, 0, 8, c * 512, 512)
            for hl in range(8):
                h = c * 8 + hl
                pb = ps_next((0, 1))
                for kt in range(8):
                    mm(PS[pb][0:64, 0:T], wv[:, kt, hl * 64:(hl + 1) * 64], hT[:, kt, 0:T], kt == 0, kt == 7, [wk, hk], [PK[pb]], kt == 7)
                act(QT[:, h, :], PS[pb][0:64, 0:T], AF.Identity, [PK[pb]], [QT.k(h)], scale=0.125)
        if stop == "a_q":
            raise _Stop()
        wv, wk = ws.get("wqkv", wqkv, 0, 8, 1024, 512)
        for g in range(NKV):
            pb = ps_next((0, 1))
            for kt in range(8):
                mm(PS[pb][0:64, 0:T], wv[:, kt, g * 64:(g + 1) * 64], hT[:, kt, 0:T], kt == 0, kt == 7, [wk, hk], [PK[pb]], kt == 7)
            acopy(KTb[:, g, 128:128 + T], PS[pb][0:64, 0:T], [PK[pb]], [KTb.k(g)])
        if stop == "a_k":
            raise _Stop()
        for b in range(NB):
            if stop == "a_v0" and b == 1:
                raise _Stop()
            if stop == "a_v1" and b == 3:
                raise _Stop()
            pb = ps_next((2, 3))
            for kt in range(8):
                mm(PS[pb][:, 0:256], hT[:, kt, b * 128:(b + 1) * 128], wv[:, kt, 256:512], kt == 0, kt == 7, [wk, hk], [PK[pb]], kt == 7)
            acopy(Vau[:, b + 1, :, 0:64], PS[pb][:, 0:256].rearrange("p (g d) -> p g d", g=4), [PK[pb]], [Vau.k(b + 1)])
            if last and b == NB - 1:
                if stop == "a_v2":
                    raise _Stop()
                cp("act", kvo[:, 256:512], PS[pb][:, 0:256], [PK[pb]], [kvo.k()])
                if stop == "a_v3a":
                    raise _Stop()
                pb2 = ps_next((2, 3))
                for kt in range(8):
                    mm(PS[pb2][:, 0:256], hT[:, kt, b * 128:(b + 1) * 128], wv[:, kt, 0:256], kt == 0, kt == 7, [wk, hk], [PK[pb2]], kt == 7)
                cp("act", kvo[:, 0:256], PS[pb2][:, 0:256], [PK[pb2]], [kvo.k()])
                if stop == "a_v3":
                    raise _Stop()
                dma_out(o_pk, kvo[:, 0:256], [kvo.k()])
                dma_out(o_pv, kvo[:, 256:512], [kvo.k()])
        if stop == "a_kv":
            raise _Stop()
        st = {}

        def a0(tk):
            b, g = tk
            first = gi == 0 and b == 0
            pbs = [None, None]
            for w in range(2):
                if w == 0 and first:
                    continue
                pb = ps_next((0, 1, 2, 3))
                pbs[w] = pb
                kc = (b + w) * 128
                mm(PS[pb][:, 0:512].rearrange("p (a q) -> p a q", a=4), KTb[:, g, kc:kc + 128],
                   QT[:, 4 * g:4 * g + 4, b * 128:(b + 1) * 128], True, True, [KTb.k(g), QT.k((4 * g, 4 * g + 4))], [PK[pb]], True)
            st[tk] = {"pbs": pbs, "sp": [None, None], "pt": None}

        def a1(tk):
            b, g = tk
            for w in range(2):
                pb = st[tk]["pbs"][w]
                if pb is None:
                    continue
                sp_ = spb.next()
                st[tk]["sp"][w] = sp_
                tt("dve", sp_[:], PS[pb][:, 0:512], BT[w][:, 4 * g:4 * g + 4, :].rearrange("p h q -> p (h q)"), ALU.add,
                   [PK[pb], BT[w].k()], [sp_.k()])

        def a2(tk):
            pt = PTs.next()
            st[tk]["pt"] = pt
            for w in range(2):
                sp_ = st[tk]["sp"][w]
                if sp_ is None:
                    continue
                act(pt[:, w].rearrange("p h q -> p (h q)"), sp_[:], AF.Exp, [sp_.k()], [pt.k(w)])

        def a3(tk):
            b, g = tk
            first = gi == 0 and b == 0
            pt = st[tk]["pt"]
            po = ps_next((4, 5))
            st[tk]["po"] = po
            for hl in range(4):
                if not first:
                    mm(PS[po][:, hl * 65:(hl + 1) * 65], pt[:, 0, hl, :], Vau[:, b, g, 0:65], True, False, [pt.k(0), Vau.k(b)], [PK[po]], False)
                mm(PS[po][:, hl * 65:(hl + 1) * 65], pt[:, 1, hl, :], Vau[:, b + 1, g, 0:65], first, True, [pt.k(1), Vau.k(b + 1)], [PK[po]], hl == 3)

        def a4(tk):
            b, g = tk
            po = st[tk]["po"]
            Otok = Otoks[b % 2]
            pov = PS[po][:, 0:260].rearrange("p (h e) -> p h e", h=4)
            dsl = (b * NKV + g) % 8
            dn = den[:, dsl, 0:4]
            tt("dve", dn, pov[:, :, 64], esink[:, 4 * g:4 * g + 4], ALU.add, [PK[po], esink.k()], [den.k(dsl)])
            recip(dn, dn, [den.k(dsl)], [den.k(dsl)])
            tt("dve", Otok[:, 4 * g:4 * g + 4, :], pov[:, :, 0:64], dn.unsqueeze(2).to_broadcast([128, 4, 64]), ALU.mult,
               [PK[po], den.k(dsl)], [Otok.k((4 * g, 4 * g + 4))])

        def a5(tk):
            b, g = tk
            if g != NKV - 1:
                return
            Otok = Otoks[b % 2]
            of = Otok[:].rearrange("p h d -> p (h d)")
            pbs = []
            for half in range(2):
                pb = ps_next((6, 7))
                pbs.append(pb)
                for q in range(4):
                    kt = half * 4 + q
                    trb(PSB[pb][:, q * 128:(q + 1) * 128], of[:, kt * 128:(kt + 1) * 128], ident_bf[:], [Otok.k()], [PK[pb]], q == 3)
            st[tk]["tp"] = pbs

        def a6(tk):
            b, g = tk
            if g != NKV - 1:
                return
            for half in range(2):
                pb = st[tk]["tp"][half]
                acopy(OT[:, half * 4:half * 4 + 4, b * 128:(b + 1) * 128], PSB[pb][:, 0:512].rearrange("p (a q) -> p a q", a=4),
                      [PK[pb]], [OT.k((half * 4, half * 4 + 4))])

        pipeline([(b, g) for b in range(NB) for g in range(NKV)], [a0, a1, a2, a3, a4, a5, a6])
        if stop == "a_core":
            raise _Stop()
        cp("pool", KTb[:, :, 0:128], KTb[:, :, T:T + 128], [KTb.k()], [KTb.k()])
        cp("pool", Vau[:, 0], Vau[:, NB], [Vau.k(NB)], [Vau.k(0)])
        proj_tm_acc("wo", wo, 8, 4, OT, lambda kt: OT.k(kt), [(xg[:, b, :], xg.k(b)) for b in range(NB)], 128)
        ar.reset(m)

    def ffn(l, xblocks, Tn, rows, S, hist_src, mode, last):
        m = ar.mark()
        Hc = 2 * S
        Us = Rot([A("U", [128, Hc + Tn], F32) for _ in range(5)])
        tmps = Rot([A("ft", [128, Tn], F32) for _ in range(5)])
        actb = A("actb", [128, 22, Tn], BF16)
        uos = Rot([A("uo", [128, 512], F32) for _ in range(2)])
        norm_T_multi([(xap, xkey, b * 128) for b, (xap, xkey) in enumerate(xblocks)], 1 + 2 * l, hT, rows)
        hk = hT.k((0, 8))
        st = {}

        def s0(j):
            c, q = divmod(j, 4)
            if q == 0:
                st["w"] = ws.get(f"w_up{l}", w_up[l], 0, 8, c * 512, 512)
            wv, wk = st["w"]
            pb = ps_next((0, 1, 2, 3))
            st[j] = [pb, None, None]
            for kt in range(8):
                mm(PS[pb][:, 0:Tn], wv[:, kt, q * 128:(q + 1) * 128], hT[:, kt, 0:Tn], kt == 0, kt == 7, [wk, hk], [PK[pb]], kt == 7)
            if last and q == 3:
                nr = 2 if mode == "p" else TS
                c0 = Tn - 2 if mode == "p" else 0
                pb2 = ps_next((4, 5))
                for kt in range(8):
                    mm(PS[pb2][0:nr, 0:512], hT[:, kt, c0:c0 + nr], wv[:, kt, :], kt == 0, kt == 7, [wk, hk], [PK[pb2]], kt == 7)
                uo = uos.next()
                acopy(uo[0:nr, :], PS[pb2][0:nr, 0:512], [PK[pb2]], [uo.k()])
                if mode == "p":
                    dma_out(o_pffn[l, :, c * 512:(c + 1) * 512], uo[0:2, :], [uo.k()])
                else:
                    dma_out(o_sffn[l, :, c * 512:(c + 1) * 512], uo[32:64, :], [uo.k()])

        def s1(j):
            pb = st[j][0]
            U = Us.next()
            st[j][1] = U
            acopy(U[:, Hc:Hc + Tn], PS[pb][:, 0:Tn], [PK[pb]], [U.k()])

        def s2(j):
            U = st[j][1]
            pb = st[j][0]
            hap, hkey = hist_src(j)
            cp("act", U[:, 0:Hc], hap, [hkey], [U.k()])
            if mode == "p":
                cp("act", fhist[:, l, j, :], U[:, Tn:Tn + 2], [U.k()], [fhist.k(l, j)])
            t = tmps.next()
            st[j][2] = t
            wj = fcw[:, l, j, :]
            if j < 22:
                ts2("pool", t[:], U[:, 2 * S:2 * S + Tn], wj[:, 2:3], wj[:, 3:4], ALU.mult, ALU.add, [U.k(), fcw.k()], [t.k()])
            else:
                act(t[:], U[:, 2 * S:2 * S + Tn], AF.Identity, [U.k(), fcw.k()], [t.k()], scale=wj[:, 2:3], bias=wj[:, 3:4])

        def s3(j):
            U, t = st[j][1], st[j][2]
            wj = fcw[:, l, j, :]
            stt("dve", t[:], U[:, S:S + Tn], wj[:, 1:2], t[:], ALU.mult, ALU.add, [U.k(), fcw.k(), t.k()], [t.k()])

        def s4(j):
            U, t = st[j][1], st[j][2]
            wj = fcw[:, l, j, :]
            stt("dve", t[:], U[:, 0:Tn], wj[:, 0:1], t[:], ALU.mult, ALU.add, [U.k(), fcw.k(), t.k()], [t.k()])

        def s5(j):
            t = st[j][2]
            if j < 22:
                act(actb[:, j, :], t[:], AF.Silu, [t.k()], [actb.k(j)])
            else:
                tt("pool", actb[:, j - 22, :], actb[:, j - 22, :], t[:], ALU.mult, [actb.k(j - 22), t.k()], [actb.k(j - 22)])

        pipeline(list(range(44)), [s0, s1, s2, s3, s4, s5])
        proj_tm_acc(f"w_dn{l}", w_dn[l], 22, 4, actb, lambda kt: actb.k(kt), xblocks, rows)
        ar.reset(m)

    def final_out(xblocks, dsts, rows):
        m = ar.mark()
        ybs = Rot([A("yb", [128, D], F32) for _ in range(2)])
        for (xap, xkey), dst in zip(xblocks, dsts):
            r_ap, r_k = rstd_of(xap, xkey, rows)
            yb = ybs.next()
            stt("dve", yb[0:rows, :], xap, r_ap, gbc[0:rows, :], ALU.mult, ALU.mult, [xkey, r_k, gbc.k()], [yb.k()])
            dma_out(dst, yb[0:rows, :], [yb.k()])
        ar.reset(m)

    def scan_group(R, chunks, BcT, CT, tri_m, ugt_m, tot_m):
        nch = len(chunks)
        s32s = [A("s32", [R, 8, 32], F32) for _ in range(nch)]
        rhsAs = [A("rhsA", [R, 16, R], F32) for _ in range(2)]
        CBms = [A("CBm", [R, 4, R], F32) for _ in range(2)]
        Es = Rot([A("E", [R, 4 * R], F32) for _ in range(2)])
        WTs = [A("WT", [R, 16, R], BF16) for _ in range(2)]
        xdt = A("xdt", [R, 16, HD], BF16)
        xw = A("xw", [R, 16, HD], BF16)
        xD = A("xD", [R, 16, HD], BF16)
        ytmp = A("ytmp", [R, 16, HD], F32)
        ybs = [A("ybh", [R, 1024], F32) for _ in range(2)]
        st = {}
        kf = [None] * 8

        def names(tk):
            ci, half = tk
            s32 = s32s[ci]
            kf[0:8] = [s32.k(i) for i in range(8)]
            return (chunks[ci], s32, s32.k(), [s32[:, i, :] for i in range(6)], (2 * ci + half) % 2, 16 * half)

        def only0(f):
            def g(tk):
                if tk[1] == 0:
                    f(tk)
            return g

        def p0(tk):
            ch, s32, k32, (a_, cs, ecs, cd, dte, ssq), si, hs = names(tk)
            tt("dve", a_, ch["dt"], Abc[0:R, :], ALU.mult, [ch["dt_key"], Abc.k()], [kf[0]])
            memset("dve", ssq[:, 0:8], 0.0, [kf[5]])

        def p1(tk):
            ch, s32, k32, (a_, cs, ecs, cd, dte, ssq), si, hs = names(tk)
            pc = ps_next((4, 5, 6, 7))
            mm(PS[pc][0:R, 0:32], tri_m[0:R, 0:R], a_, True, True, [tri_m.k(), kf[0]], [PK[pc]], False)
            mm(PS[pc][0:R, 32:64], tot_m[0:R, 0:R], a_, True, True, [tot_m.k(), kf[0]], [PK[pc]], True)
            cp("dve", s32[:, 6:8, :].rearrange("p a b -> p (a b)"), PS[pc][0:R, 0:64], [PK[pc]], [kf[6], kf[7]])

        def p2(tk):
            ch, s32, k32, (a_, cs, ecs, cd, dte, ssq), si, hs = names(tk)
            cp("dve", cs, s32[:, 6, :], [kf[6]], [kf[1]])

        def p3(tk):
            ch, s32, k32, (a_, cs, ecs, cd, dte, ssq), si, hs = names(tk)
            act(ecs, cs, AF.Exp, [kf[1]], [kf[2]])
            act(cd, s32[:, 7, :], AF.Exp, [kf[7]], [kf[3]])
            tt("dve", dte, s32[:, 7, :], cs, ALU.subtract, [kf[7], kf[1]], [kf[4]])

        def p4(tk):
            ch, s32, k32, (a_, cs, ecs, cd, dte, ssq), si, hs = names(tk)
            act(dte, dte, AF.Exp, [kf[4]], [kf[4]])

        def p5(tk):
            ch, s32, k32, (a_, cs, ecs, cd, dte, ssq), si, hs = names(tk)
            tt("dve", dte, dte, ch["dt"], ALU.mult, [kf[4], ch["dt_key"]], [kf[4]])

        def S0(tk):
            ch, s32, k32, (a_, cs, ecs, cd, dte, ssq), si, hs = names(tk)
            rhsA = rhsAs[si]
            tt("dve", rhsA[:], tri_m[0:R, 0:R].unsqueeze(1).to_broadcast([R, 16, R]),
               a_[:, hs:hs + 16].unsqueeze(2).to_broadcast([R, 16, R]), ALU.mult, [tri_m.k(), kf[0]], [rhsA.k()])

        def CBst(tk):
            ch, s32, k32, _, si, hs = names(tk)
            pcb = ps_next((4, 5, 6, 7))
            for gg in range(4):
                g = 4 * (hs // 16) + gg
                mm(PS[pcb][0:R, gg * R:(gg + 1) * R], BcT[:, g, ch["cb"]], CT[:, g, ch["cb"]], True, True, [BcT.k(g), CT.k(g)], [PK[pcb]], gg == 3)
            tt("dve", CBms[si][:], PS[pcb][0:R, 0:4 * R].rearrange("p (g l) -> p g l", g=4),
               tri_m[0:R, 0:R].unsqueeze(1).to_broadcast([R, 4, R]), ALU.mult, [PK[pcb], tri_m.k()], [CBms[si].k()])

        def mkD(lo):
            def f(tk):
                ch, s32, k32, _, si, hs = names(tk)
                rhsA = rhsAs[si]
                for gg in range(lo, lo + 2):
                    pd = gg
                    for hl in range(4):
                        mm(PS[pd][0:R, hl * R:(hl + 1) * R], ugt_m[0:R, 0:R], rhsA[:, gg * 4 + hl, :], True, True, [ugt_m.k(), rhsA.k()], [PK[pd]], hl == 3)
            return f

        def mkE(lo):
            def f(tk):
                ch, s32, k32, _, si, hs = names(tk)
                for gg in range(lo, lo + 2):
                    pd = gg
                    E = Es.next()
                    act(E[:], PS[pd][0:R, 0:4 * R], AF.Exp, [PK[pd]], [E.k()])
                    tt("pool" if gg % 2 == 0 else "dve", WTs[si][:, gg * 4:gg * 4 + 4, :], E[:].rearrange("p (h l) -> p h l", h=4),
                       CBms[si][:, gg, :].unsqueeze(1).to_broadcast([R, 4, R]), ALU.mult, [E.k(), CBms[si].k()], [WTs[si].k((gg * 4, gg * 4 + 4))])
            return f

        def S3b(tk):
            ch, s32, k32, (a_, cs, ecs, cd, dte, ssq), si, hs = names(tk)
            half = tk[1]
            xsv = ch["xs"][:, hs * HD:(hs + 16) * HD].rearrange("p (h d) -> p h d", h=16)
            bc = lambda v: v[:, hs:hs + 16].unsqueeze(2).to_broadcast([R, 16, HD])
            tt("dve", xw[:], xsv, bc(dte), ALU.mult, [ch["xs_key"], kf[4]], [xw.k()])
            tt("pool", xdt[:], xsv, bc(ch["dt"]), ALU.mult, [ch["xs_key"], ch["dt_key"]], [xdt.k()])
            tt("pool", xD[:], xsv, bc(Dbc[0:R, :]), ALU.mult, [ch["xs_key"], Dbc.k()], [xD.k()])
            pyo = [ps_next((4, 5, 6, 7)), ps_next((4, 5, 6, 7))]
            yo_aps, yo_keys = ch["yoff"](half, pyo)
            for bk in range(2):
                h0 = hs + bk * 8
                tt("dve", ytmp[:, bk * 8:(bk + 1) * 8, :], yo_aps[bk].rearrange("p (h d) -> p h d", h=8),
                   ecs[:, h0:h0 + 8].unsqueeze(2).to_broadcast([R, 8, HD]), ALU.mult, [yo_keys[bk], kf[2]], [ytmp.k((bk * 8, bk * 8 + 8))])
            if ch["state_mm"] is not None:
                pst = ch["state_mm"](half, xw)
                ch["state_upd"](half, pst, cd, kf[3])

        def S4(tk):
            ch, s32, k32, (a_, cs, ecs, cd, dte, ssq), si, hs = names(tk)
            half = tk[1]
            yb = ybs[si]
            pyd = [ps_next((4, 5, 6, 7)), ps_next((4, 5, 6, 7))]
            for bk in range(2):
                mm(PS[pyd[bk]][0:R, 0:512], ident_bf[0:R, 0:R], xD[:, bk * 8:(bk + 1) * 8, :].rearrange("p h d -> p (h d)"), True, False,
                   [ident_bf.k(), xD.k()], [PK[pyd[bk]]], False)
                for hh in range(8):
                    h16 = bk * 8 + hh
                    mm(PS[pyd[bk]][0:R, hh * 64:(hh + 1) * 64], WTs[si][:, h16, :], xdt[:, h16, :], False, hh == 7,
                       [WTs[si].k(h16), xdt.k()], [PK[pyd[bk]]], hh == 7)
            for bk in range(2):
                tt("dve", yb[:, bk * 512:(bk + 1) * 512], ytmp[:, bk * 8:(bk + 1) * 8, :].rearrange("p h d -> p (h d)"), PS[pyd[bk]][0:R, 0:512], ALU.add,
                   [PK[pyd[bk]], ytmp.k()], [yb.k()])

        def S5(tk):
            ch, s32, k32, (a_, cs, ecs, cd, dte, ssq), si, hs = names(tk)
            half = tk[1]
            yb = ybs[si]
            c0 = hs * HD
            tt("pool", yb[:], yb[:], ch["zs"][:, c0:c0 + 1024], ALU.mult, [yb.k(), ch["zs_key"]], [yb.k()])
            for gg in range(4):
                sqb = sqs.next()
                act(sqb[0:R, 0:256], yb[:, gg * 256:(gg + 1) * 256], AF.Square, [yb.k()], [sqb.k(), kf[5]], accum_out=ssq[:, 4 * half + gg:4 * half + gg + 1])
            q0 = 4 * half
            act(ssq[:, 8 + q0:12 + q0], ssq[:, q0:q0 + 4], AF.Sqrt, [kf[5]], [kf[5]], bias=EPS, scale=1.0 / 256)
            recip(ssq[:, 16 + q0:20 + q0], ssq[:, 8 + q0:12 + q0], [kf[5]], [kf[5]])
            tt("dve", yb[:].rearrange("p (g c) -> p g c", g=4), yb[:].rearrange("p (g c) -> p g c", g=4),
               ssq[:, 16 + q0:20 + q0].unsqueeze(2).to_broadcast([R, 4, 256]), ALU.mult, [yb.k(), kf[5]], [yb.k()])

        def S6(tk):
            ch, s32, k32, _, si, hs = names(tk)
            half = tk[1]
            yb = ybs[si]
            for q4 in range(2):
                pb = ps_next((4, 5, 6, 7))
                for q in range(4):
                    kt = q4 * 4 + q
                    tr(PS[pb][:, q * 128:q * 128 + R], yb[:, kt * 128:(kt + 1) * 128], ident[0:R, 0:R], [yb.k()], [PK[pb]], q == 3)
                k0 = half * 8 + q4 * 4
                tt("dve", hT[:, k0:k0 + 4, ch["cb"]], PS[pb][:, 0:512].rearrange("p (a q) -> p a q", a=4)[:, :, 0:R],
                   nwT[:, k0:k0 + 4].unsqueeze(2).to_broadcast([128, 4, R]), ALU.mult, [PK[pb], nwT.k()], [hT.k((k0, k0 + 4))])

        tasks = [(ci, half) for ci in range(nch) for half in range(2)]
        stages = [only0(p0), only0(p1), only0(p2), only0(p3), only0(p4), only0(p5), S0, lambda tk: (CBst(tk), mkD(0)(tk)), lambda tk: (mkE(0)(tk), mkD(2)(tk)),
                  lambda tk: (mkE(2)(tk), S3b(tk)), S4, S5, S6]
        pipeline(tasks, stages)

    def ssd_prompt(gi):
        m = ar.mark()
        last = gi == NG - 1
        zs = A("zs", [128, NB, DIN], BF16)
        xs_tok = A("xs_tok", [128, NB, DIN], BF16)
        BcT = A("BcT", [128, SGR, T], BF16)
        CT = A("CT", [128, SGR, T], BF16)
        Btok = A("Btok", [128, NB, SGR * 128], BF16)
        dts = A("dts", [128, NB, 32], F32)
        m2 = ar.mark()
        U3 = Rot([A("U3", [128, 3 + T], F32) for _ in range(5)])
        tmps = Rot([A("st", [128, T], F32) for _ in range(5)])
        fms = Rot([A("fm", [128, T], BF16) for _ in range(3)])
        uos = Rot([A("uo3", [128, 512], F32) for _ in range(2)])
        xblocks = [(xg[:, b, :], xg.k(b)) for b in range(NB)]
        norm_T_multi([(xap, xkey, b * 128) for b, (xap, xkey) in enumerate(xblocks)], 2, hT)
        hk = hT.k((0, 8))
        for c in range(4):
            wv, wk = ws.get("w_in", w_in, 0, 8, c * 512, 512)
            for b in range(NB):
                pb = ps_next((0, 1, 2, 3))
                for kt in range(8):
                    mm(PS[pb][:, 0:512], hT[:, kt, b * 128:(b + 1) * 128], wv[:, kt, :], kt == 0, kt == 7, [wk, hk], [PK[pb]], kt == 7)
                act(zs[:, b, c * 512:(c + 1) * 512], PS[pb][:, 0:512], AF.Silu, [PK[pb]], [zs.k(b, (c * 512, c * 512 + 512))])
        st = {}

        def x0(j):
            c, q = divmod(j, 4)
            if q == 0:
                st["w"] = ws.get("w_in", w_in, 0, 8, 2048 + c * 512, 512)
            wv, wk = st["w"]
            pb = ps_next((0, 1, 2, 3))
            st[j] = {"pb": pb}
            for kt in range(8):
                mm(PS[pb][:, 0:T], wv[:, kt, q * 128:(q + 1) * 128], hT[:, kt, 0:T], kt == 0, kt == 7, [wk, hk], [PK[pb]], kt == 7)
            if last and q == 3:
                pb2 = ps_next((6, 7))
                for kt in range(8):
                    mm(PS[pb2][0:3, 0:512], hT[:, kt, T - 3:T], wv[:, kt, :], kt == 0, kt == 7, [wk, hk], [PK[pb2]], kt == 7)
                uo = uos.next()
                acopy(uo[0:3, :], PS[pb2][0:3, 0:512], [PK[pb2]], [uo.k()])
                dma_out(o_pconv[:, c * 512:(c + 1) * 512], uo[0:3, :], [uo.k()])

        def x1(j):
            U = U3.next()
            st[j]["U"] = U
            acopy(U[:, 3:3 + T], PS[st[j]["pb"]][:, 0:T], [PK[st[j]["pb"]]], [U.k()])

        def x2(j):
            U = st[j]["U"]
            cp("act", U[:, 0:3], chist[:, j, :], [chist.k(j)], [U.k()])
            cp("act", chist[:, j, :], U[:, T:T + 3], [U.k()], [chist.k(j)])
            t = tmps.next()
            st[j]["t"] = t
            wj = scw[:, j, :]
            ts2("pool", t[:], U[:, 3:3 + T], wj[:, 3:4], wj[:, 4:5], ALU.mult, ALU.add, [U.k(), scw.k()], [t.k()])

        def mk_tap(k):
            def f(j):
                U, t = st[j]["U"], st[j]["t"]
                wj = scw[:, j, :]
                stt("dve", t[:], U[:, k:k + T], wj[:, k:k + 1], t[:], ALU.mult, ALU.add, [U.k(), scw.k(), t.k()], [t.k()])
            return f

        def x6(j):
            t = st[j]["t"]
            if j < 24:
                f = fms.next()
                st[j]["f"] = f
                act(f[:], t[:], AF.Silu, [t.k()], [f.k()])
            else:
                act(CT[:, j - 24, :], t[:], AF.Silu, [t.k()], [CT.k(j - 24)])

        def x7(j):
            if j >= 24:
                return
            f = st[j]["f"]
            pb2 = ps_next((4, 5))
            st[j]["pb2"] = pb2
            for b in range(NB):
                trb(PSB[pb2][:, b * 128:(b + 1) * 128], f[:, b * 128:(b + 1) * 128], ident_bf[:], [f.k()], [PK[pb2]], b == NB - 1)
            if j >= 16:
                cp("pool", BcT[:, j - 16, :], f[:], [f.k()], [BcT.k(j - 16)])

        def x8(j):
            if j >= 24:
                return
            pb2 = st[j]["pb2"]
            pv2 = PSB[pb2][:, 0:T].rearrange("p (b c) -> p b c", b=NB)
            if j < 16:
                cp("dve", xs_tok[:, :, j * 128:(j + 1) * 128], pv2, [PK[pb2]], [xs_tok.k()])
            else:
                g = j - 16
                cp("dve", Btok[:, :, g * 128:(g + 1) * 128], pv2, [PK[pb2]], [Btok.k()])

        pipeline(list(range(32)), [x0, x1, x2, mk_tap(2), mk_tap(1), mk_tap(0), x6, x7, x8])
        wv, wk = ws.get("w_in", w_in, 0, 8, 6144, 32)
        for b in range(NB):
            pb = ps_next((0, 1, 2, 3))
            for kt in range(8):
                mm(PS[pb][:, 0:32], hT[:, kt, b * 128:(b + 1) * 128], wv[:, kt, :], kt == 0, kt == 7, [wk, hk], [PK[pb]], kt == 7)
            tt("dve", dts[:, b, :], PS[pb][:, 0:32], dtb[:], ALU.add, [PK[pb], dtb.k()], [dts.k(b)])
            act(dts[:, b, :], dts[:, b, :], AF.Exp, [dts.k(b)], [dts.k(b)])
            act(dts[:, b, :], dts[:, b, :], AF.Ln, [dts.k(b)], [dts.k(b)], bias=1.0)
        ar.reset(m2)
        def mk_yoff(cb):
            def f(half, pyo):
                for gg in range(4):
                    g = 4 * half + gg
                    bk = gg // 2
                    mm(PS[pyo[bk]][:, (gg % 2) * 256:(gg % 2) * 256 + 256], CT[:, g, cb],
                       hst_bf[:, 4 * g:4 * g + 4, :].rearrange("p h d -> p (h d)"), True, True, [CT.k(g), hst_bf.k((4 * g, 4 * g + 4))], [PK[pyo[bk]]], gg % 2 == 1)
                return [PS[pyo[0]][:, 0:512], PS[pyo[1]][:, 0:512]], [PK[pyo[0]], PK[pyo[1]]]
            return f

        def mk_state_mm(b):
            def f(half, xw):
                pst = [ps_next((4, 5, 6, 7)), ps_next((4, 5, 6, 7))]
                for gg in range(4):
                    g = 4 * half + gg
                    bk = gg // 2
                    mm(PS[pst[bk]][:, (gg % 2) * 256:(gg % 2) * 256 + 256], Btok[:, b, g * 128:(g + 1) * 128],
                       xw[:, gg * 4:gg * 4 + 4, :].rearrange("p h d -> p (h d)"), True, True, [Btok.k(b), xw.k()], [PK[pst[bk]]], gg % 2 == 1)
                return pst
            return f

        def state_upd(half, pst, cd, k32):
            hs = 16 * half
            for bk in range(2):
                h0 = hs + bk * 8
                hv = hst[:, h0:h0 + 8, :]
                tt("pool", hv, hv, cd[:, h0:h0 + 8].unsqueeze(2).to_broadcast([128, 8, HD]), ALU.mult, [hst.k((h0, h0 + 8)), k32], [hst.k((h0, h0 + 8))])
                tt("dve", hv, hv, PS[pst[bk]][:, 0:512].rearrange("p (h d) -> p h d", h=8), ALU.add, [hst.k((h0, h0 + 8)), PK[pst[bk]]],
                   [hst.k((h0, h0 + 8))])
                acopy(hst_bf[:, h0:h0 + 8, :], hv, [hst.k((h0, h0 + 8))], [hst_bf.k((h0, h0 + 8))])

        chunks = []
        for b in range(NB):
            cb = slice(b * 128, (b + 1) * 128)
            chunks.append(dict(cb=cb, xs=xs_tok[:, b, :], xs_key=xs_tok.k(b), zs=zs[:, b, :], zs_key=zs.k(b), dt=dts[:, b, :], dt_key=dts.k(b),
                               yoff=mk_yoff(cb), state_mm=mk_state_mm(b), state_upd=state_upd))
        scan_group(128, chunks, BcT, CT, tri, ugt, ones)
        proj_tm_acc("w_out", w_out, 16, 4, hT, lambda kt: hT.k(kt), xblocks, 128)
        if last:
            ar.reset(m2)
            hso = A("hso", [128, 16, 128], F32)
            hf = hst[:].rearrange("p h d -> p (h d)")
            for q4 in range(4):
                pb = ps_next((4, 5, 6, 7))
                for q in range(4):
                    it = q4 * 4 + q
                    tr(PS[pb][:, q * 128:(q + 1) * 128], hf[:, it * 128:(it + 1) * 128], ident[:], [hst.k()], [PK[pb]], q == 3)
                acopy(hso[:, q4 * 4:q4 * 4 + 4, :], PS[pb][:, 0:512].rearrange("p (a q) -> p a q", a=4), [PK[pb]], [hso.k()])
            dma_out(o_pssm.rearrange("(i r) n -> r i n", r=128), hso[:], [hso.k()])
        ar.reset(m)

    XS = xg[0:TS, 0, :]
    XSK = xg.k(0)

    def attn_sample():
        m = ar.mark()
        BT = [A("BTp", [128, NH, 128], F32), A("BTo", [128, NH, 128], F32)]
        QT = A("QTs", [64, NH, TS], BF16)
        KTs = A("KTs", [64, NKV, TS], BF16)
        Vs = A("Vs", [TS, NKV, 66], BF16)
        Kcf = Rot([A("Kcf", [128, 256], F32) for _ in range(2)])
        KcT = A("KcT", [64, SB_PER_CORE, NKV, 128], BF16)
        Vc = A("Vc", [128, SB_PER_CORE, NKV, 66], BF16)
        BTos = A("BTos", [TS, NH, TS], F32)
        PTc = A("PTc", [128, SB_PER_CORE, 4, TS], BF16)
        PTo = A("PTo", [TS, 4, TS], BF16)
        spb = Rot([A("spbs", [128, 256], F32) for _ in range(4)])
        Otok = A("Otoks", [TS, NH, HD], BF16)
        OT = A("OTs", [128, KT, TS], BF16)
        den = A("dens", [TS, 8, 16], F32)
        kvs = A("kvs", [TS, 512], F32)
        mrow = A("mrow", [1, SB_PER_CORE * 256], BF16)
        mrow_f = A("mrow_f", [1, SB_PER_CORE * 256], F32)
        samem = A("samem", [TS, 16], F32)
        sel = A("sel", [4, TS], F32)
        for w in range(2):
            dma_in(BT[w][:].rearrange("p h q -> p (h q)"), bt_dram[w], [BT[w].k()], rkeys=["bt_dram"])
        dma_in(mrow_f[:], cin["s_mrow"], [mrow_f.k()])
        cp("dve", mrow[:], mrow_f[:], [mrow_f.k()], [mrow.k()])
        dma_in(samem[:], cin["s_samem"], [samem.k()])
        dma_in(sel[:], cin["s_sel"], [sel.k()])
        dma_in(XS, xs_in, [XSK])
        memset("dve", Vs[:], 1.0, [Vs.k()])
        memset("dve", Vc[:], 1.0, [Vc.k()])
        dma_out(o_sk[:, 0:124, :], ck_in[:, 4:128, :], [])
        dma_out(o_sv[:, 0:124, :], cv_in[:, 4:128, :], [])
        for bq in range(SB_PER_CORE):
            dstv = Vc[:, bq, :, 0:64]
            srcv = cv_in[bq].rearrange("k (g d) -> k g d", g=4)
            P.dma("pool", [lambda e, dstv=dstv, srcv=srcv: e.dma_start(out=dstv, in_=srcv)], writes=[Vc.k(bq)])
        for bq in range(SB_PER_CORE):
            kc = Kcf.next()
            dma_in(kc[:], ck_in[bq], [kc.k()])
            pb = ps_next((0, 1, 2, 3))
            for g in range(NKV):
                tr(PS[pb][0:64, g * 128:(g + 1) * 128], kc[:, g * 64:(g + 1) * 64], ident[:], [kc.k()], [PK[pb]], g == 3)
            acopy(KcT[:, bq].rearrange("p g k -> p (g k)"), PS[pb][0:64, 0:512], [PK[pb]], [KcT.k(bq)])
        norm_T(XS, XSK, 0, hT, 0, TS)
        hk = hT.k((0, 8))
        for c in range(2):
            wv, wk = ws.get("wqkv", wqkv, 0, 8, c * 512, 512)
            for hl in range(8):
                h = c * 8 + hl
                pb = ps_next((0, 1))
                for kt in range(8):
                    mm(PS[pb][0:64, 0:TS], wv[:, kt, hl * 64:(hl + 1) * 64], hT[:, kt, 0:TS], kt == 0, kt == 7, [wk, hk], [PK[pb]], kt == 7)
                act(QT[:, h, :], PS[pb][0:64, 0:TS], AF.Identity, [PK[pb]], [QT.k(h)], scale=0.125)
        wv, wk = ws.get("wqkv", wqkv, 0, 8, 1024, 512)
        for g in range(NKV):
            pb = ps_next((0, 1))
            for kt in range(8):
                mm(PS[pb][0:64, 0:TS], wv[:, kt, g * 64:(g + 1) * 64], hT[:, kt, 0:TS], kt == 0, kt == 7, [wk, hk], [PK[pb]], kt == 7)
            acopy(KTs[:, g, :], PS[pb][0:64, 0:TS], [PK[pb]], [KTs.k(g)])
        pb = ps_next((2, 3))
        for kt in range(8):
            mm(PS[pb][0:TS, 0:512], hT[:, kt, 0:TS], wv[:, kt, :], kt == 0, kt == 7, [wk, hk], [PK[pb]], kt == 7)
        acopy(kvs[:], PS[pb][0:TS, 0:512], [PK[pb]], [kvs.k()])
        acopy(Vs[:, :, 0:64], PS[pb][0:TS, 256:512].rearrange("p (g d) -> p g d", g=4), [PK[pb]], [Vs.k()])
        for t in range(4):
            dma_out(o_sk[:, 124 + t, :], kvs[16 * t:16 * t + 16, 0:256], [kvs.k()])
            dma_out(o_sv[:, 124 + t, :], kvs[16 * t:16 * t + 16, 256:512], [kvs.k()])
        pb = ps_next((2, 3))
        mm(PS[pb][0:TS, 0:64].rearrange("p (h t) -> p h t", h=16), sel[:], BT[1][0:4, :, 0:4], True, True, [sel.k(), BT[1].k()], [PK[pb]], True)
        cp("dve", BTos[:].rearrange("p h (t b) -> p h t b", t=4),
           PS[pb][0:TS, 0:64].rearrange("p (h t) -> p h t", h=16).unsqueeze(3).to_broadcast([TS, NH, 4, 16]), [PK[pb]], [BTos.k()])
        for h in range(NH):
            tt("dve", BTos[:, h, :].rearrange("p (t b) -> p t b", t=4), BTos[:, h, :].rearrange("p (t b) -> p t b", t=4),
               samem[:].unsqueeze(1).to_broadcast([TS, 4, 16]), ALU.add, [BTos.k(h), samem.k()], [BTos.k(h)])
        for g in range(NKV):
            stq = {}

            def q0(bq, g=g):
                pb = ps_next((0, 1, 2, 3))
                stq[bq] = [pb, None]
                mm(PS[pb][:, 0:256].rearrange("p (a q) -> p a q", a=4), KcT[:, bq, g, :], QT[:, 4 * g:4 * g + 4, :], True, False,
                   [KcT.k(bq), QT.k((4 * g, 4 * g + 4))], [PK[pb]], False)
                mm(PS[pb][:, 0:256], ones_bf[0:1, :], mrow[0:1, bq * 256:(bq + 1) * 256], False, True, [ones_bf.k(), mrow.k()], [PK[pb]], True)

            def q1(bq, g=g):
                pb = stq[bq][0]
                sp_ = spb.next()
                stq[bq][1] = sp_
                tt("dve", sp_[:].rearrange("p (h t b) -> p h t b", h=4, t=4), PS[pb][:, 0:256].rearrange("p (h t b) -> p h t b", h=4, t=4),
                   BT[0][:, 4 * g:4 * g + 4, 0:4].unsqueeze(3).to_broadcast([128, 4, 4, 16]), ALU.add, [PK[pb], BT[0].k()], [sp_.k()])

            def q2(bq, g=g):
                sp_ = stq[bq][1]
                act(PTc[:, bq].rearrange("p h q -> p (h q)"), sp_[:], AF.Exp, [sp_.k()], [PTc.k(bq)])

            pipeline(list(range(SB_PER_CORE)), [q0, q1, q2])
            pb = ps_next((0, 1, 2, 3))
            mm(PS[pb][0:TS, 0:256].rearrange("p (a q) -> p a q", a=4), KTs[:, g, :], QT[:, 4 * g:4 * g + 4, :], True, True,
               [KTs.k(g), QT.k((4 * g, 4 * g + 4))], [PK[pb]], True)
            sp_ = spb.next()
            tt("dve", sp_[0:TS, :], PS[pb][0:TS, 0:256], BTos[:, 4 * g:4 * g + 4, :].rearrange("p h q -> p (h q)"), ALU.add,
               [PK[pb], BTos.k()], [sp_.k()])
            act(PTo[:].rearrange("p h q -> p (h q)"), sp_[0:TS, :], AF.Exp, [sp_.k()], [PTo.k()])
            po = ps_next((4, 5))
            for hl in range(4):
                for bq in range(SB_PER_CORE):
                    mm(PS[po][0:TS, hl * 65:(hl + 1) * 65], PTc[:, bq, hl, :], Vc[:, bq, g, 0:65], bq == 0, False, [PTc.k(bq), Vc.k(bq)], [PK[po]], False)
                mm(PS[po][0:TS, hl * 65:(hl + 1) * 65], PTo[:, hl, :], Vs[:, g, 0:65], False, True, [PTo.k(), Vs.k()], [PK[po]], hl == 3)
            pov = PS[po][0:TS, 0:260].rearrange("p (h e) -> p h e", h=4)
            dn = den[:, g, 0:4]
            tt("dve", dn, pov[:, :, 64], esink[0:TS, 4 * g:4 * g + 4], ALU.add, [PK[po], esink.k()], [den.k(g)])
            recip(dn, dn, [den.k(g)], [den.k(g)])
            tt("dve", Otok[:, 4 * g:4 * g + 4, :], pov[:, :, 0:64], dn.unsqueeze(2).to_broadcast([TS, 4, 64]), ALU.mult,
               [PK[po], den.k(g)], [Otok.k((4 * g, 4 * g + 4))])
        of = Otok[:].rearrange("p h d -> p (h d)")
        for half in range(2):
            pb = ps_next((6, 7))
            for q in range(4):
                kt = half * 4 + q
                trb(PSB[pb][:, q * 128:q * 128 + TS], of[:, kt * 128:(kt + 1) * 128], ident_bf[0:TS, 0:TS], [Otok.k()], [PK[pb]], q == 3)
            acopy(OT[:, half * 4:half * 4 + 4, :], PSB[pb][:, 0:512].rearrange("p (a q) -> p a q", a=4)[:, :, 0:TS],
                  [PK[pb]], [OT.k((half * 4, half * 4 + 4))])
        proj_tm_acc("wo", wo, 8, 4, OT, lambda kt: OT.k(kt), [(XS, XSK)], TS)
        ar.reset(m)

    def ffn_sample(l):
        m = ar.mark()
        Uh = A("Uh", [128, 44, 32], F32)
        stg = Rot([A("stg", [32, 512], F32) for _ in range(2)])
        for c in range(11):
            st = stg.next()
            dma_in(st[:], sffn_in[l, :, c * 512:(c + 1) * 512], [st.k()])
            pb = ps_next((4, 5))
            for q in range(4):
                tr(PS[pb][:, q * 32:(q + 1) * 32], st[:, q * 128:(q + 1) * 128], ident[0:32, 0:32], [st.k()], [PK[pb]], q == 3)
            acopy(Uh[:, c * 4:(c + 1) * 4, :], PS[pb][:, 0:128].rearrange("p (a q) -> p a q", a=4), [PK[pb]], [Uh.k((c * 4, c * 4 + 4))])
        ffn(l, [(XS, XSK)], TS, TS, 16, lambda j: (Uh[:, j, :], Uh.k(j)), "s", True)
        ar.reset(m)

    def ssd_sample():
        m = ar.mark()
        R = TS
        zs = A("zss", [R, DIN], BF16)
        xs_tok = A("xs_toks", [R, DIN], BF16)
        BcT = A("BcTs", [128, SGR, R], BF16)
        CT = A("CTs", [128, SGR, R], BF16)
        Btok = A("Btoks", [R, SGR * 128], BF16)
        dts = A("dtss", [R, 32], F32)
        Uh3 = A("Uh3", [128, 32, 48], F32)
        s_tri = A("s_tri", [R, R], F32)
        s_ugt = A("s_ugt", [R, R], F32)
        s_same = A("s_same", [R, R], F32)
        seqsel = A("seqsel", [R, 16], F32)
        seqrow = A("seqrow", [128, 16, R], F32)
        rep = A("rep", [32, 16, 128], F32)
        cd_col = A("cd_col", [128, 16, 16], F32)
        cdT = A("cdT", [32, 16], F32)
        xw_all = A("xw_all", [R, DIN], BF16)
        sx = A("sx", [R, 8, 32], F32)
        yoff = A("yoff", [R, DIN], F32)
        for name, t_ in (("s_tri", s_tri), ("s_ugt", s_ugt), ("s_same", s_same), ("s_seqsel", seqsel)):
            dma_in(t_[:], cin[name], [t_.k()])
        dma_in(seqrow[:].rearrange("p b q -> p (b q)"), cin["s_seqrow"], [seqrow.k()])
        dma_in(rep[:].rearrange("p i m -> p (i m)"), cin["s_rep"], [rep.k()])
        m2 = ar.mark()
        stg = Rot([A("stg3", [48, 512], F32) for _ in range(2)])
        U3 = Rot([A("U3s", [128, 48 + R], F32) for _ in range(5)])
        tmps = Rot([A("sts", [128, R], F32) for _ in range(5)])
        fms = Rot([A("fms", [128, R], BF16) for _ in range(3)])
        xbo = Rot([A("xbo", [R, 512], F32) for _ in range(2)])
        for c in range(8):
            st = stg.next()
            dma_in(st[:], sconv_in[:, c * 512:(c + 1) * 512], [st.k()])
            pb = ps_next((4, 5))
            for q in range(4):
                tr(PS[pb][:, q * 48:(q + 1) * 48], st[:, q * 128:(q + 1) * 128], ident[0:48, 0:48], [st.k()], [PK[pb]], q == 3)
            acopy(Uh3[:, c * 4:(c + 1) * 4, :], PS[pb][:, 0:192].rearrange("p (a q) -> p a q", a=4), [PK[pb]], [Uh3.k((c * 4, c * 4 + 4))])
        norm_T(XS, XSK, 2, hT, 0, R)
        hk = hT.k((0, 8))
        for c in range(4):
            wv, wk = ws.get("w_in", w_in, 0, 8, c * 512, 512)
            pb = ps_next((0, 1, 2, 3))
            for kt in range(8):
                mm(PS[pb][0:R, 0:512], hT[:, kt, 0:R], wv[:, kt, :], kt == 0, kt == 7, [wk, hk], [PK[pb]], kt == 7)
            act(zs[:, c * 512:(c + 1) * 512], PS[pb][0:R, 0:512], AF.Silu, [PK[pb]], [zs.k((c * 512, c * 512 + 512))])
        stx = {}

        def x0(j):
            c, q = divmod(j, 4)
            if q == 0:
                stx["w"] = ws.get("w_in", w_in, 0, 8, 2048 + c * 512, 512)
            wv, wk = stx["w"]
            pb = ps_next((0, 1, 2, 3))
            stx[j] = {"pb": pb}
            for kt in range(8):
                mm(PS[pb][:, 0:R], wv[:, kt, q * 128:(q + 1) * 128], hT[:, kt, 0:R], kt == 0, kt == 7, [wk, hk], [PK[pb]], kt == 7)
            if q == 3:
                pb2 = ps_next((6, 7))
                for kt in range(8):
                    mm(PS[pb2][0:R, 0:512], hT[:, kt, 0:R], wv[:, kt, :], kt == 0, kt == 7, [wk, hk], [PK[pb2]], kt == 7)
                xo = xbo.next()
                acopy(xo[:], PS[pb2][0:R, 0:512], [PK[pb2]], [xo.k()])
                dma_out(o_sconv[:, c * 512:(c + 1) * 512], xo[16:64, :], [xo.k()])

        def x1(j):
            U = U3.next()
            stx[j]["U"] = U
            acopy(U[:, 48:48 + R], PS[stx[j]["pb"]][:, 0:R], [PK[stx[j]["pb"]]], [U.k()])
            cp("act", U[:, 0:48], Uh3[:, j, :], [Uh3.k(j)], [U.k()])

        def x2(j):
            U = stx[j]["U"]
            t = tmps.next()
            stx[j]["t"] = t
            wj = scw[:, j, :]
            ts2("pool", t[:], U[:, 48:48 + R], wj[:, 3:4], wj[:, 4:5], ALU.mult, ALU.add, [U.k(), scw.k()], [t.k()])

        def mk_tap(k):
            def f(j):
                U, t = stx[j]["U"], stx[j]["t"]
                wj = scw[:, j, :]
                stt("dve", t[:], U[:, 16 * k:16 * k + R], wj[:, k:k + 1], t[:], ALU.mult, ALU.add, [U.k(), scw.k(), t.k()], [t.k()])
            return f

        def x6(j):
            t = stx[j]["t"]
            if j < 24:
                f = fms.next()
                stx[j]["f"] = f
                act(f[:], t[:], AF.Silu, [t.k()], [f.k()])
            else:
                act(CT[:, j - 24, :], t[:], AF.Silu, [t.k()], [CT.k(j - 24)])

        def x7(j):
            if j >= 24:
                return
            f = stx[j]["f"]
            pb2 = ps_next((4, 5))
            stx[j]["pb2"] = pb2
            trb(PSB[pb2][0:R, 0:128], f[:], ident_bf[:], [f.k()], [PK[pb2]], True)
            if j >= 16:
                cp("pool", BcT[:, j - 16, :], f[:], [f.k()], [BcT.k(j - 16)])

        def x8(j):
            if j >= 24:
                return
            pb2 = stx[j]["pb2"]
            if j < 16:
                cp("dve", xs_tok[:, j * 128:(j + 1) * 128], PSB[pb2][0:R, 0:128], [PK[pb2]], [xs_tok.k((j * 128, j * 128 + 128))])
            else:
                g = j - 16
                cp("dve", Btok[:, g * 128:(g + 1) * 128], PSB[pb2][0:R, 0:128], [PK[pb2]], [Btok.k((g * 128, g * 128 + 128))])

        pipeline(list(range(32)), [x0, x1, x2, mk_tap(2), mk_tap(1), mk_tap(0), x6, x7, x8])
        wv, wk = ws.get("w_in", w_in, 0, 8, 6144, 32)
        pb = ps_next((0, 1, 2, 3))
        for kt in range(8):
            mm(PS[pb][0:R, 0:32], hT[:, kt, 0:R], wv[:, kt, :], kt == 0, kt == 7, [wk, hk], [PK[pb]], kt == 7)
        tt("dve", dts[:], PS[pb][0:R, 0:32], dtb[0:R, :], ALU.add, [PK[pb], dtb.k()], [dts.k()])
        act(dts[:], dts[:], AF.Exp, [dts.k()], [dts.k()])
        act(dts[:], dts[:], AF.Ln, [dts.k()], [dts.k()], bias=1.0)
        ar.reset(m2)
        a_, cs, tot, dte = (sx[:, i, :] for i in range(4))
        kx = sx.k()
        tt("dve", a_, dts[:], Abc[0:R, :], ALU.mult, [dts.k(), Abc.k()], [kx])
        pc = ps_next((0, 1, 2, 3))
        mm(PS[pc][0:R, 0:32], s_tri[:], a_, True, True, [s_tri.k(), kx], [PK[pc]], False)
        mm(PS[pc][0:R, 32:64], s_same[:], a_, True, True, [s_same.k(), kx], [PK[pc]], False)
        mm(PS[pc][0:32, 64:80], a_, seqsel[:], True, True, [kx, seqsel.k()], [PK[pc]], True)
        cp("dve", sx[:, 1:3, :].rearrange("p a b -> p (a b)"), PS[pc][0:R, 0:64], [PK[pc]], [kx])
        cp("dve", cdT[:], PS[pc][0:32, 64:80], [PK[pc]], [cdT.k()])
        act(cdT[:], cdT[:], AF.Exp, [cdT.k()], [cdT.k()])
        tt("dve", dte, tot, cs, ALU.subtract, [kx], [kx])
        act(dte, dte, AF.Exp, [kx], [kx])
        tt("dve", dte, dte, dts[:], ALU.mult, [kx, dts.k()], [kx])
        tt("dve", xw_all[:].rearrange("p (h d) -> p h d", h=SH), xs_tok[:].rearrange("p (h d) -> p h d", h=SH),
           dte.unsqueeze(2).to_broadcast([R, SH, HD]), ALU.mult, [xs_tok.k(), kx], [xw_all.k()])
        pc = ps_next((0, 1, 2, 3))
        for it in range(16):
            mm(PS[pc][:, it * 16:(it + 1) * 16], rep[:, it, :], cdT[:], True, True, [rep.k(), cdT.k()], [PK[pc]], it == 15)
        cp("dve", cd_col[:].rearrange("p i b -> p (i b)"), PS[pc][:, 0:256], [PK[pc]], [cd_col.k()])
        m3 = ar.mark()
        h0n = [A("h0n", [128, 16, 128], F32) for _ in range(3)]
        h0T = [A("h0T", [128, DIN], BF16) for _ in range(2)]
        hnew = [A("hnew", [128, 16, 128], F32) for _ in range(2)]
        CmT = [A("CmT", [128, SGR, R], BF16) for _ in range(3)]
        Bm = [A("Bm", [R, SGR * 128], BF16) for _ in range(3)]

        def b0(bq):
            hn, cm, bm = h0n[bq % 3], CmT[bq % 3], Bm[bq % 3]
            dma_in(hn[:], sssm_in[bq].rearrange("(i r) n -> r i n", r=128), [hn.k()])
            tt("pool", cm[:], CT[:], seqrow[:, bq, :].unsqueeze(1).to_broadcast([128, SGR, R]), ALU.mult, [CT.k(), seqrow.k()], [cm.k()])
            ts1("dve", bm[:], Btok[:], seqsel[:, bq:bq + 1], ALU.mult, [Btok.k(), seqsel.k()], [bm.k()])

        def b1(bq):
            hn, ht = h0n[bq % 3], h0T[bq % 2]
            for q4 in range(4):
                pb = ps_next((0, 1, 2, 3))
                for q in range(4):
                    tr(PS[pb][:, q * 128:(q + 1) * 128], hn[:, q4 * 4 + q, :], ident[:], [hn.k()], [PK[pb]], q == 3)
                acopy(ht[:, q4 * 512:(q4 + 1) * 512], PS[pb][:, 0:512], [PK[pb]], [ht.k((q4 * 512, q4 * 512 + 512))])

        def b2(bq):
            hn, ht, hw, cm, bm = h0n[bq % 3], h0T[bq % 2], hnew[bq % 2], CmT[bq % 3], Bm[bq % 3]
            for g in range(SGR):
                mm(PS[4 + g // 2][0:R, (g % 2) * 256:(g % 2) * 256 + 256], cm[:, g, :], ht[:, g * 256:(g + 1) * 256], bq == 0 and g % 2 == 0,
                   bq == SB_PER_CORE - 1 and g % 2 == 1, [cm.k(g), ht.k((g * 256, g * 256 + 256))], [PK[4 + g // 2]], g % 2 == 1)
            for q4 in range(4):
                pb = ps_next((0, 1, 2, 3))
                for q in range(4):
                    it = q4 * 4 + q
                    mm(PS[pb][:, q * 128:(q + 1) * 128], xw_all[:, it * 128:(it + 1) * 128], bm[:, (it // 2) * 128:(it // 2 + 1) * 128], True, True,
                       [xw_all.k(), bm.k()], [PK[pb]], q == 3)
                for q in range(4):
                    it = q4 * 4 + q
                    stt("dve", hw[:, it, :], hn[:, it, :], cd_col[:, it, bq:bq + 1], PS[pb][:, q * 128:(q + 1) * 128], ALU.mult, ALU.add,
                        [hn.k(it), cd_col.k(), PK[pb]], [hw.k(it)])

        def b3(bq):
            hw = hnew[bq % 2]
            dma_out(o_sssm[bq].rearrange("(i r) n -> r i n", r=128), hw[:], [hw.k()])

        pipeline(list(range(SB_PER_CORE)), [b0, b1, b2, b3])
        for q4 in range(4):
            acopy(yoff[:, q4 * 512:(q4 + 1) * 512], PS[4 + q4][0:R, 0:512], [PK[4 + q4]], [yoff.k((q4 * 512, q4 * 512 + 512))])
        ar.reset(m3)
        def yoff_sample(half, pyo):
            c0 = half * 1024
            return [yoff[:, c0:c0 + 512], yoff[:, c0 + 512:c0 + 1024]], [yoff.k(), yoff.k()]

        scan_group(R, [dict(cb=slice(0, R), xs=xs_tok[:], xs_key=xs_tok.k(), zs=zs[:], zs_key=zs.k(), dt=dts[:], dt_key=dts.k(),
                            yoff=yoff_sample, state_mm=None, state_upd=None)], BcT, CT, s_tri, s_ugt, s_same)
        proj_tm_acc("w_out", w_out, 16, 4, hT, lambda kt: hT.k(kt), [(XS, XSK)], R)
        ar.reset(m)

    def dbg(name, ap, key):
        if name in dbg_out:
            dma_out(dbg_out[name], ap, [key])

    for gi in range(NG):
        if stop == "init":
            break
        xblocks = [(xg[:, b, :], xg.k(b)) for b in range(NB)]
        try:
            attn_prompt(gi)
        except _Stop:
            break
        if stop == "attn":
            break
        if gi == 0:
            dbg("x_attn", xg[:].rearrange("p b c -> p (b c)"), xg.k())
        ffn(0, xblocks, T, 128, 1, lambda j: (fhist[:, 0, j, :], fhist.k(0, j)), "p", gi == NG - 1)
        if gi == 0:
            dbg("x_ffn0", xg[:].rearrange("p b c -> p (b c)"), xg.k())
        if stop == "ffn0":
            break
        ssd_prompt(gi)
        if stop == "ssd":
            break
        if gi == 0:
            dbg("x_ssd", xg[:].rearrange("p b c -> p (b c)"), xg.k())
        ffn(1, xblocks, T, 128, 1, lambda j: (fhist[:, 1, j, :], fhist.k(1, j)), "p", gi == NG - 1)
        final_out(xblocks, [o_y[(gi * NB + b) * 128:(gi * NB + b + 1) * 128, :] for b in range(NB)], 128)

    if do_sample and stop is None:
        attn_sample()
        dbg("xs_attn", XS, XSK)
        ffn_sample(0)
        dbg("xs_ffn0", XS, XSK)
        ssd_sample()
        dbg("xs_ssd", XS, XSK)
        ffn_sample(1)
        final_out([(XS, XSK)], [o_ys], TS)

    P.finish("sp")
    with contextlib.ExitStack() as ctx:
        P.emit(ctx)
    return nc, ws.req, ar.hi


_CACHE = {}


def _get_program(**kw):
    key = tuple(sorted((k, str(v)) for k, v in kw.items()))
    if key not in _CACHE:
        _, req, _ = build_program(order=None, **kw)
        nc, req2, hi = build_program(order=req, **kw)
        assert [r[1:] for r in req] == [r[1:] for r in req2]
        _CACHE[key] = nc
    return _CACHE[key]


def _core_inputs(inp, c, consts):
    f = np.ascontiguousarray
    s = c % 4
    b0 = c * SB_PER_CORE
    bs = slice(b0, b0 + SB_PER_CORE)
    m = {}
    m["xp"] = f(inp["x_prompt"][s])
    m["xs"] = f(np.transpose(inp["x_sample"][bs], (1, 0, 2)).reshape(TS, D))
    m["ck"] = f(inp["cache_k_win"][0, bs].reshape(SB_PER_CORE, 128, 256))
    m["cv"] = f(inp["cache_v_win"][0, bs].reshape(SB_PER_CORE, 128, 256))
    m["sconv"] = f(np.transpose(inp["state_ssm_conv"][0, bs], (1, 0, 2)).reshape(3 * SB_PER_CORE, CONVD))
    m["sssm"] = f(inp["state_ssm"][0, bs].reshape(SB_PER_CORE, 2048, 128))
    m["sffn"] = f(np.transpose(inp["state_ffn_conv"][:, bs], (0, 2, 1, 3)).reshape(2, 2 * SB_PER_CORE, NUP))
    m["table"] = f(inp["rel_bias_table"])
    m["wqkv"] = f(inp["attn_wqkv"][0])
    m["wo"] = f(inp["attn_wo"][0])
    m["w_in"] = f(inp["ssm_w_in"][0])
    m["w_out"] = f(inp["ssm_w_out"][0])
    for l in range(2):
        m[f"w_up{l}"] = f(inp["ffn_w_up"][l])
        m[f"w_dn{l}"] = f(inp["ffn_w_down"][l])
    g = np.stack([inp["norm_mix"][0], inp["norm_ffn"][0], inp["norm_mix"][1], inp["norm_ffn"][1]])
    m["gT"] = f(np.transpose(g.reshape(4, KT, 128), (2, 0, 1)).reshape(128, 4 * KT))
    m["gfin"] = f(inp["norm_final"].reshape(1, D))
    m["sinks"] = f(inp["attn_sinks"].reshape(1, 16))
    fw = np.concatenate([inp["ffn_conv_w"], inp["ffn_conv_b"][:, None, :]], axis=1)
    m["fcw"] = f(np.transpose(fw.reshape(2, 4, 44, 128), (3, 0, 2, 1)).reshape(128, 2 * 44 * 4))
    sw = np.concatenate([inp["ssm_conv_w"][0], inp["ssm_conv_b"][0][None, :]], axis=0)
    m["scw"] = f(np.transpose(sw.reshape(5, 32, 128), (2, 1, 0)).reshape(128, 32 * 5))
    m["ssmv"] = f(np.concatenate([inp["ssm_dt_bias"][0], inp["ssm_A_log"][0], inp["ssm_D"][0]]).reshape(1, 96))
    m["nwT"] = f(inp["ssm_norm"][0].reshape(16, 128).T)
    for k, v in consts.items():
        m["c_" + k] = v
    return {k: np.asarray(v, dtype=np.float32) for k, v in m.items()}


def kernel(**inputs):
    inp = {k: np.asarray(v) for k, v in inputs.items()}
    nc = _get_program(NG=8, do_sample=True)
    consts = host_consts()
    in_maps = [_core_inputs(inp, c, consts) for c in range(NCORES)]
    res = run_bass_kernel_spmd(nc, in_maps, core_ids=list(range(NCORES))).results
    B = 4
    y_prompt = np.stack([res[s]["o_y"] for s in range(B)])
    p_k = np.stack([res[s]["o_pk"].reshape(128, 4, 64) for s in range(B)])[None]
    p_v = np.stack([res[s]["o_pv"].reshape(128, 4, 64) for s in range(B)])[None]
    p_conv = np.stack([res[s]["o_pconv"] for s in range(B)])[None]
    p_ssm = np.stack([res[s]["o_pssm"].reshape(32, 64, 128) for s in range(B)])[None]
    p_ffn = np.stack([res[s]["o_pffn"] for s in range(B)], axis=1)
    cat = lambda fn: np.concatenate([fn(res[c]) for c in range(NCORES)], axis=0)
    y_sample = cat(lambda r: np.transpose(r["o_ys"].reshape(4, SB_PER_CORE, D), (1, 0, 2)))
    s_k = cat(lambda r: r["o_sk"].reshape(SB_PER_CORE, 128, 4, 64))[None]
    s_v = cat(lambda r: r["o_sv"].reshape(SB_PER_CORE, 128, 4, 64))[None]
    s_conv = cat(lambda r: np.transpose(r["o_sconv"].reshape(3, SB_PER_CORE, CONVD), (1, 0, 2)))[None]
    s_ssm = cat(lambda r: r["o_sssm"].reshape(SB_PER_CORE, 32, 64, 128))[None]
    s_ffn = np.concatenate([np.transpose(res[c]["o_sffn"].reshape(2, 2, SB_PER_CORE, NUP), (0, 2, 1, 3)) for c in range(NCORES)], axis=1)
    outs = (y_prompt, y_sample, p_k, p_v, p_conv, p_ssm, p_ffn, s_k, s_v, s_conv, s_ssm, s_ffn)
    return tuple(np.ascontiguousarray(o, dtype=np.float32) for o in outs)
```
